# Optimizing a Trainium2 kernel written in Bass

```python
import math
import jax, jax.numpy as jnp
from jax import lax
import numpy as np

D_MODEL = 4096
BATCH = 2
SEQ = 4096
DEPTH = 1

MIX_WIDTH = D_MODEL
ML_HEADS = 8
ML_DV = MIX_WIDTH // 2 // ML_HEADS
ML_DQK = ML_DV // 2
ML_WIDTH = ML_HEADS * ML_DV
ML_QK_WIDTH = ML_HEADS * ML_DQK
ML_CHUNK = 64
ML_CONV = 4
NSA_HEADS = 16
NSA_HD = (MIX_WIDTH - ML_WIDTH) // NSA_HEADS
NSA_GROUPS = 4
NSA_HPG = NSA_HEADS // NSA_GROUPS
NSA_WIDTH = NSA_HEADS * NSA_HD
NSA_KV_WIDTH = NSA_GROUPS * NSA_HD
CMP_STRIDE = 16
CMP_LEN = 2 * CMP_STRIDE
CMP_HIDDEN = 2 * NSA_HD
SEL_LEN = 64
SEL_TOPK = 16
SEL_QBLOCK = 64
WIN = 512
WIN_QBLOCK = 128
N_BRANCH = 3
REL_BUCKETS = 32
REL_MAX_DIST = 128
PLE_DIM = 256
EPS = 1e-6
NEG_INF = -1e30
FORCE_SCORE = 1e4
IN_SPLITS = (ML_QK_WIDTH, ML_QK_WIDTH, ML_WIDTH, ML_WIDTH, ML_WIDTH, ML_HEADS, ML_HEADS,
             NSA_WIDTH, NSA_KV_WIDTH, NSA_KV_WIDTH, NSA_KV_WIDTH, NSA_KV_WIDTH,
             NSA_KV_WIDTH, NSA_KV_WIDTH, NSA_HEADS * N_BRANCH, NSA_WIDTH)
IN_WIDTH = sum(IN_SPLITS)

kernel_name = 'hymba_mlstm_nsa_block'


def _split(u, widths):
    offs = np.cumsum(widths)[:-1].tolist()
    return jnp.split(u, offs, axis=-1)


def _rmsnorm(x, w):
    xf = x.astype(jnp.float32)
    y = xf * lax.rsqrt(jnp.mean(xf * xf, axis=-1, keepdims=True) + EPS)
    return (y * w.astype(jnp.float32)).astype(x.dtype)


def _causal_conv(x, w):
    K, S = w.shape[0], x.shape[1]
    xp = jnp.pad(x, ((0, 0), (K - 1, 0), (0, 0)))
    return sum(xp[:, j:j + S] * w[j] for j in range(K))


def _rel_bucket(dist):
    n = jnp.maximum(dist, 0)
    max_exact = REL_BUCKETS // 2
    nf = jnp.maximum(n, 1).astype(jnp.float32)
    large = max_exact + (jnp.log(nf / max_exact) / math.log(REL_MAX_DIST / max_exact)
                         * (REL_BUCKETS - max_exact)).astype(jnp.int32)
    large = jnp.minimum(large, REL_BUCKETS - 1)
    return jnp.where(n < max_exact, n, large)


def _masked_softmax(scores, mask):
    s = jnp.where(mask, scores, NEG_INF)
    return jnp.where(mask, jax.nn.softmax(s, axis=-1), 0.0)


def _mlstm(q, k, v, log_i, log_f):
    B, S, NH, _ = q.shape
    nc = S // ML_CHUNK

    def to_chunks(a):
        a = a.astype(jnp.float32).reshape((B, nc, ML_CHUNK) + a.shape[2:])
        return jnp.swapaxes(jnp.moveaxis(a, 1, 0), 2, 3)

    tril = jnp.tril(jnp.ones((ML_CHUNK, ML_CHUNK), dtype=bool))

    def step(carry, xs):
        C, n, m = carry
        qc, kc, vc, li, lf = xs
        b = jnp.cumsum(lf, axis=-1)
        D = jnp.where(tril, b[..., :, None] - b[..., None, :] + li[..., None, :], -jnp.inf)
        a = b + m[..., None]
        m_t = jnp.maximum(a, jnp.max(D, axis=-1))
        Dw = jnp.exp(D - m_t[..., None])
        inter = jnp.exp(a - m_t)
        sc = jnp.einsum('bhtd,bhsd->bhts', qc, kc) * Dw
        num = inter[..., None] * jnp.einsum('bhtd,bhdv->bhtv', qc, C) + jnp.einsum('bhts,bhsv->bhtv', sc, vc)
        den = inter * jnp.einsum('bhtd,bhd->bht', qc, n) + jnp.sum(sc, axis=-1)
        h = num / jnp.maximum(jnp.abs(den), jnp.exp(-m_t))[..., None]
        bL = b[..., -1]
        w = bL[..., None] - b + li
        m_new = jnp.maximum(bL + m, jnp.max(w, axis=-1))
        wk = jnp.exp(w - m_new[..., None])
        decay = jnp.exp(bL + m - m_new)
        C_new = decay[..., None, None] * C + jnp.einsum('bhs,bhsd,bhsv->bhdv', wk, kc, vc)
        n_new = decay[..., None] * n + jnp.einsum('bhs,bhsd->bhd', wk, kc)
        return (C_new, n_new, m_new), h

    init = (jnp.zeros((B, NH, ML_DQK, ML_DV), jnp.float32),
            jnp.zeros((B, NH, ML_DQK), jnp.float32),
            jnp.zeros((B, NH), jnp.float32))
    _, h = lax.scan(step, init, (to_chunks(q), to_chunks(k), to_chunks(v),
                                 to_chunks(log_i), to_chunks(log_f)))
    h = jnp.moveaxis(jnp.swapaxes(h, 2, 3), 0, 1)
    return h.reshape(B, S, NH, ML_DV)


def _compress(x, pe, w1, w2):
    B, S, G, hd = x.shape
    seg = x.reshape(B, S // CMP_STRIDE, CMP_STRIDE, G, hd)
    blk = jnp.concatenate([seg[:, :-1], seg[:, 1:]], axis=2) + pe[:, None, :]
    blk = jnp.moveaxis(blk, 2, 3).reshape(B, -1, G, CMP_LEN * hd)
    return jax.nn.silu(blk @ w1) @ w2


def _nsa(q, kc, vc, ks, vs, kwin, vwin, g, q_norm_w, k_norm_w,
         pe_k, pe_v, k_w1, k_w2, v_w1, v_w2, rel_bias):
    B, S = q.shape[:2]
    G, Hg, hd = NSA_GROUPS, NSA_HPG, NSA_HD
    scale = hd ** -0.5
    t = jnp.arange(S)
    qg = _rmsnorm(q.reshape(B, S, G, Hg, hd), q_norm_w)

    kcmp = _rmsnorm(_compress(kc.reshape(B, S, G, hd), pe_k, k_w1, k_w2), k_norm_w[0])
    vcmp = _compress(vc.reshape(B, S, G, hd), pe_v, v_w1, v_w2)
    n_cmp = S // CMP_STRIDE - 1
    cmp_start = jnp.arange(n_cmp) * CMP_STRIDE
    dist_c = t[:, None] - (cmp_start + CMP_LEN - 1)[None, :]
    bias_c = rel_bias[_rel_bucket(dist_c)].reshape(S, n_cmp, G, Hg).transpose(2, 3, 0, 1)
    sc = jnp.einsum('bsghd,bcgd->bghsc', qg, kcmp).astype(jnp.float32) * scale + bias_c.astype(jnp.float32)
    p_c = _masked_softmax(sc, dist_c >= 0)
    o_c = jnp.einsum('bghsc,bcgd->bsghd', p_c.astype(vcmp.dtype), vcmp)

    n_sel = S // SEL_LEN
    sel_start = jnp.arange(n_sel) * SEL_LEN
    overlap = ((cmp_start[:, None] < sel_start[None, :] + SEL_LEN)
               & (cmp_start[:, None] + CMP_LEN > sel_start[None, :])).astype(jnp.float32)
    imp = jnp.einsum('bghsc,cn->bgsn', p_c, overlap)
    cur = (t // SEL_LEN)[:, None]
    blk_i = jnp.arange(n_sel)[None, :]
    forced = (blk_i == 0) | (blk_i == cur) | (blk_i == cur - 1)
    imp = jnp.where(forced, FORCE_SCORE, imp)
    imp = jnp.where(blk_i > cur, NEG_INF, imp)
    k_top = min(SEL_TOPK, n_sel)
    _, idx = lax.top_k(imp, k_top)

    ks_b = _rmsnorm(ks.reshape(B, n_sel, SEL_LEN, G, hd), k_norm_w[1]).transpose(0, 3, 1, 2, 4)
    vs_b = vs.reshape(B, n_sel, SEL_LEN, G, hd).transpose(0, 3, 1, 2, 4)
    nqb = S // SEL_QBLOCK
    q_blocks = jnp.moveaxis(qg.reshape(B, nqb, SEL_QBLOCK, G, Hg, hd), 1, 0)
    idx_blocks = idx.reshape(B, G, nqb, SEL_QBLOCK, k_top).transpose(2, 0, 1, 3, 4)
    t_blocks = t.reshape(nqb, SEL_QBLOCK)
    tbl = rel_bias.reshape(REL_BUCKETS, G, Hg).transpose(1, 0, 2)
    bi = jnp.arange(B)[:, None, None, None]
    gi = jnp.arange(G)[None, :, None, None]
    n_keys = k_top * SEL_LEN

    def sel_block(args):
        qb, ib, tb = args
        kg = ks_b[bi, gi, ib].reshape(B, G, SEL_QBLOCK, n_keys, hd)
        vg = vs_b[bi, gi, ib].reshape(B, G, SEL_QBLOCK, n_keys, hd)
        pos = (ib[..., None] * SEL_LEN + jnp.arange(SEL_LEN)).reshape(B, G, SEL_QBLOCK, n_keys)
        dist = tb[:, None] - pos
        bias = jnp.moveaxis(tbl[gi, _rel_bucket(dist)], -1, 2)
        s = jnp.einsum('bqghd,bgqkd->bghqk', qb, kg).astype(jnp.float32) * scale + bias.astype(jnp.float32)
        pr = _masked_softmax(s, (dist >= 0)[:, :, None])
        return jnp.einsum('bghqk,bgqkd->bqghd', pr.astype(vg.dtype), vg)

    o_s = lax.map(sel_block, (q_blocks, idx_blocks, t_blocks))
    o_s = jnp.moveaxis(o_s, 0, 1).reshape(B, S, G, Hg, hd)

    nwb = S // WIN_QBLOCK
    n_back = WIN // WIN_QBLOCK

    def band(a):
        ap = jnp.pad(a, ((0, 0), (WIN, 0), (0, 0), (0, 0))).reshape(B, nwb + n_back, WIN_QBLOCK, G, hd)
        return jnp.concatenate([ap[:, j:j + nwb] for j in range(n_back + 1)], axis=2)

    kwb = band(_rmsnorm(kwin.reshape(B, S, G, hd), k_norm_w[2]))
    vwb = band(vwin.reshape(B, S, G, hd))
    qw = qg.reshape(B, nwb, WIN_QBLOCK, G, Hg, hd)
    kw_len = (n_back + 1) * WIN_QBLOCK
    qi = jnp.arange(WIN_QBLOCK)
    ki = jnp.arange(kw_len)
    dist_w = qi[:, None] + WIN - ki[None, :]
    k_abs = jnp.arange(nwb)[:, None, None] * WIN_QBLOCK - WIN + ki[None, None, :]
    mask_w = (dist_w >= 0) & (dist_w < WIN) & (k_abs >= 0)
    bias_w = rel_bias[_rel_bucket(dist_w)].reshape(WIN_QBLOCK, kw_len, G, Hg).transpose(2, 3, 0, 1)
    s = jnp.einsum('bnqghd,bnkgd->bnghqk', qw, kwb).astype(jnp.float32) * scale + bias_w.astype(jnp.float32)
    pr = _masked_softmax(s, mask_w[None, :, None, None])
    o_w = jnp.einsum('bnghqk,bnkgd->bnqghd', pr.astype(vwb.dtype), vwb).reshape(B, S, G, Hg, hd)

    gs = jax.nn.sigmoid(g.reshape(B, S, G, Hg, N_BRANCH))
    o = gs[..., 0:1] * o_c + gs[..., 1:2] * o_s + gs[..., 2:3] * o_w
    return o.reshape(B, S, NSA_WIDTH)


def setup_inputs(seed: int = 0) -> dict:
    key = jax.random.key(seed)
    ks = jax.random.split(key, 20)
    L = DEPTH

    def nrm(k, shape, s):
        return s * jax.random.normal(k, shape, jnp.float32)

    return {
        'x': nrm(ks[0], (BATCH, SEQ, D_MODEL), 1.0),
        'p': nrm(ks[1], (DEPTH, BATCH, SEQ, PLE_DIM), 1.0),
        'norm_w': 1.0 + nrm(ks[2], (L, D_MODEL), 0.05),
        'w_in': nrm(ks[3], (L, D_MODEL, IN_WIDTH), D_MODEL ** -0.5),
        'ml_conv_w': nrm(ks[4], (L, ML_CONV, 2 * ML_QK_WIDTH), ML_CONV ** -0.5),
        'ml_i_bias': nrm(ks[5], (L, ML_HEADS), 0.1),
        'ml_f_bias': 3.0 + nrm(ks[6], (L, ML_HEADS), 0.5),
        'ml_head_norm_w': 1.0 + nrm(ks[7], (L, ML_HEADS, ML_DV), 0.05),
        'nsa_q_norm_w': 1.0 + nrm(ks[8], (L, NSA_HD), 0.05),
        'nsa_k_norm_w': 1.0 + nrm(ks[9], (L, N_BRANCH, NSA_HD), 0.05),
        'cmp_pe_k': nrm(ks[10], (L, CMP_LEN, NSA_HD), 0.1),
        'cmp_pe_v': nrm(ks[11], (L, CMP_LEN, NSA_HD), 0.1),
        'cmp_k_w1': nrm(ks[12], (L, CMP_LEN * NSA_HD, CMP_HIDDEN), (CMP_LEN * NSA_HD) ** -0.5),
        'cmp_k_w2': nrm(ks[13], (L, CMP_HIDDEN, NSA_HD), CMP_HIDDEN ** -0.5),
        'cmp_v_w1': nrm(ks[14], (L, CMP_LEN * NSA_HD, CMP_HIDDEN), (CMP_LEN * NSA_HD) ** -0.5),
        'cmp_v_w2': nrm(ks[15], (L, CMP_HIDDEN, NSA_HD), CMP_HIDDEN ** -0.5),
        'rel_bias': nrm(ks[16], (REL_BUCKETS, NSA_HEADS), 0.5),
        'w_out': nrm(ks[17], (L, MIX_WIDTH, D_MODEL), MIX_WIDTH ** -0.5),
        'ple_proj': nrm(ks[18], (L, PLE_DIM, D_MODEL), PLE_DIM ** -0.5),
        'ple_gate': nrm(ks[19], (L, D_MODEL, D_MODEL), D_MODEL ** -0.5),
    }


def reference(x, p, norm_w, w_in, ml_conv_w, ml_i_bias, ml_f_bias, ml_head_norm_w,
              nsa_q_norm_w, nsa_k_norm_w, cmp_pe_k, cmp_pe_v, cmp_k_w1, cmp_k_w2,
              cmp_v_w1, cmp_v_w2, rel_bias, w_out, ple_proj, ple_gate):
    B, S, _ = x.shape
    for layer in range(DEPTH):
        h = _rmsnorm(x, norm_w[layer])
        u = h @ w_in[layer]
        (ml_q, ml_k, ml_v, ml_o, ml_z, ml_i, ml_f,
         ns_q, ns_kc, ns_vc, ns_ks, ns_vs, ns_kw, ns_vw, ns_g, ns_z) = _split(u, IN_SPLITS)

        qk = jax.nn.silu(_causal_conv(jnp.concatenate([ml_q, ml_k], axis=-1), ml_conv_w[layer]))
        mq, mk = jnp.split(qk, 2, axis=-1)
        mq = mq.reshape(B, S, ML_HEADS, ML_DQK)
        mk = mk.reshape(B, S, ML_HEADS, ML_DQK) * (ML_DQK ** -0.5)
        mv = ml_v.reshape(B, S, ML_HEADS, ML_DV)
        log_i = (ml_i + ml_i_bias[layer]).astype(jnp.float32)
        log_f = jax.nn.log_sigmoid((ml_f + ml_f_bias[layer]).astype(jnp.float32))
        hm = _rmsnorm(_mlstm(mq, mk, mv, log_i, log_f).astype(x.dtype), ml_head_norm_w[layer])
        y_ml = hm.reshape(B, S, ML_WIDTH) * jax.nn.sigmoid(ml_o) * jax.nn.silu(ml_z)

        y_ns = _nsa(ns_q, ns_kc, ns_vc, ns_ks, ns_vs, ns_kw, ns_vw, ns_g,
                    nsa_q_norm_w[layer], nsa_k_norm_w[layer], cmp_pe_k[layer], cmp_pe_v[layer],
                    cmp_k_w1[layer], cmp_k_w2[layer], cmp_v_w1[layer], cmp_v_w2[layer],
                    rel_bias) * jax.nn.silu(ns_z)

        x = x + jnp.concatenate([y_ml, y_ns], axis=-1) @ w_out[layer]
        x = x + jax.nn.sigmoid(x @ ple_gate[layer]) * (p[layer] @ ple_proj[layer])
    return x
```

```python
import numpy as np
from contextlib import ExitStack
import concourse.bass as bass
import concourse.mybir as mybir
from concourse.bass_utils import run_bass_kernel_spmd

F32 = mybir.dt.float32
BF16 = mybir.dt.bfloat16
AF = mybir.ActivationFunctionType
ALU = mybir.AluOpType
AX = mybir.AxisListType

D = 4096
S = 4096
NT = S // 128
WCOLS = 3856
EPS = 1e-6
NEG = -30000.0


class TL:
    def __init__(self, sem, inc):
        self.sem = sem
        self.inc = inc
        self.count = 0


class Res:
    def __init__(self, name):
        self.name = name
        self.w = None
        self.r = {}


class Buf(Res):
    def __init__(self, kb, name, t):
        super().__init__(name)
        self.kb = kb
        self.t = t
        self._chan = None
        self.subs = {}

    def __getitem__(self, k):
        return self.t[k]

    def chan(self):
        if self._chan is None:
            self._chan = TL(self.kb.new_sem("c_" + self.name), 16)
            self.kb.chans.append(self._chan)
        return self._chan

    def sub(self, key):
        if key not in self.subs:
            b = Buf(self.kb, "%s_%s" % (self.name, key), self.t)
            self.subs[key] = b
        return self.subs[key]


class Eng:
    def __init__(self, name, tl):
        self.name = name
        self.tl = tl
        self.seen = {}
        self.ops = []


class KB:
    def __init__(self, nc):
        self.nc = nc
        self.es = ExitStack()
        self.nsem = 0
        self.engs = {}
        for n in ("pe", "dve", "act", "pool", "sp"):
            self.engs[n] = Eng(n, TL(self.new_sem("e_" + n), 1))
        self.nbuf = 0
        self.mute = False
        self.chans = []

    def new_sem(self, name):
        self.nsem += 1
        return self.es.enter_context(self.nc.semaphore(name))

    def sb(self, name, shape, dt, es=None):
        t = (es or self.es).enter_context(self.nc.sbuf_tensor(name, list(shape), dt))
        return Buf(self, name, t)

    def ps(self, name, shape, dt=F32):
        t = self.es.enter_context(self.nc.psum_tensor(name, list(shape), dt))
        return Buf(self, name, t)

    def dram(self, name, shape, dt, kind="Internal"):
        t = self.nc.dram_tensor(name, list(shape), dt, kind=kind)
        return Buf(self, name, t.ap())

    def _deps(self, E, reads, writes, skip_self):
        deps = {}

        def need(tlv):
            if tlv is None:
                return
            tl, v = tlv
            if skip_self and tl is E.tl:
                return
            if E.seen.get(tl, 0) >= v:
                return
            if deps.get(tl, 0) < v:
                deps[tl] = v

        for r in reads:
            need(r.w)
        for w in writes:
            need(w.w)
            for tl, v in w.r.items():
                need((tl, v))
        return deps, need

    def op(self, eng, fn, reads=(), writes=()):
        if self.mute:
            return
        E = self.engs[eng]
        deps, _ = self._deps(E, reads, writes, skip_self=(eng == "pe"))
        for tl, v in deps.items():
            E.seen[tl] = v
        E.tl.count += 1
        val = E.tl.count
        E.ops.append(([(tl.sem, v) for tl, v in deps.items()], fn, (E.tl.sem, 1)))
        for r in reads:
            r.r[E.tl] = val
        for w in writes:
            w.w = (E.tl, val)
            w.r = {}

    def dma(self, q, out_ap, in_ap, reads, writes, owner, group=False):
        if self.mute:
            return
        E = self.engs[q]
        ch = owner.chan()
        deps, need = self._deps(E, reads, writes, skip_self=False)
        if (not group) and ch.count > 0:
            need((ch, ch.count))
        for tl, v in deps.items():
            E.seen[tl] = v
        ch.count += 16
        val = ch.count
        E.ops.append(([(tl.sem, v) for tl, v in deps.items()],
                      lambda h: h.dma_start(out=out_ap, in_=in_ap), (ch.sem, 16)))
        for r in reads:
            r.r[ch] = val
        for w in writes:
            w.w = (ch, val)
            w.r = {}

    def custom(self, eng, fn, reads, writes, tl):
        E = self.engs[eng]
        deps, need = self._deps(E, reads, writes, skip_self=False)
        for t2, v in deps.items():
            E.seen[t2] = v
        tl.count += tl.inc
        val = tl.count
        E.ops.append(([(t2.sem, v) for t2, v in deps.items()], fn, (tl.sem, tl.inc)))
        for r in reads:
            r.r[tl] = val
        for w in writes:
            w.w = (tl, val)
            w.r = {}

    def barrier(self):
        tls = [e.tl for e in self.engs.values() if e.tl.count] + \
              [t for t in self.chans if t.count]
        for E in self.engs.values():
            waits = []
            for tl in tls:
                if tl is E.tl:
                    continue
                if E.seen.get(tl, 0) < tl.count:
                    E.seen[tl] = tl.count
                    waits.append((tl.sem, tl.count))
            if waits:
                E.ops.append((waits, None, None))

    def finish(self, final_res):
        E = self.engs["sp"]
        waits = {}
        for r in final_res:
            if r.w is not None:
                tl, v = r.w
                waits[tl] = max(waits.get(tl, 0), v)
        for e in self.engs.values():
            if e.tl.count:
                waits[e.tl] = e.tl.count
        E.ops.append(([(tl.sem, v) for tl, v in waits.items()], None, None))
        with self.nc.Block() as block:
            def runner(name):
                def f(h):
                    for waits_, fn, inc in self.engs[name].ops:
                        for sem, v in waits_:
                            h.wait_ge(sem, v)
                        if fn is not None:
                            ins = fn(h)
                            ins.then_inc(inc[0], inc[1])
                return f
            block.tensor(runner("pe"))
            block.vector(runner("dve"))
            block.scalar(runner("act"))
            block.gpsimd(runner("pool"))
            block.sync(runner("sp"))
        self.es.close()


class Prog:
    def __init__(self, debug=False, stop_after=99, skip1=False, p2stop=99, nchunks=NT, cstop=99, have_nsa=False, nsa_tiles=8):
        self.have_nsa = have_nsa
        self.nsa_tiles = nsa_tiles
        self.nchunks = nchunks
        self.cstop = cstop
        self.skip1 = skip1
        self.p2stop = p2stop
        self.debug = debug
        self.stop_after = stop_after
        nc = bass.Bass("TRN2", target_bir_lowering=False)
        self.nc = nc
        kb = KB(nc)
        self.kb = kb
        self.inp = {}
        self.outs = []
        self.rr = 0

    def din(self, name, shape, dt=F32):
        b = self.kb.dram(name, shape, dt, kind="ExternalInput")
        self.inp[name] = b
        return b

    def dout(self, name, shape, dt=F32):
        b = self.kb.dram(name, shape, dt, kind="ExternalOutput")
        self.outs.append(b)
        return b

    def evac(self, out_ap, in_ap, reads, writes, scale=None):
        kb = self.kb
        self.rr += 1
        if self.rr % 2 == 0:
            kb.op("act", lambda h: h.activation(out=out_ap, in_=in_ap, func=AF.Copy),
                  reads, writes)
        else:
            kb.op("dve", lambda h: h.tensor_copy(out=out_ap, in_=in_ap), reads, writes)

    def cast(self, out_ap, in_ap, reads, writes):
        kb = self.kb
        self.rr += 1
        m = self.rr % 3
        if m == 0:
            kb.op("act", lambda h: h.activation(out=out_ap, in_=in_ap, func=AF.Copy),
                  reads, writes)
        elif m == 1:
            kb.op("dve", lambda h: h.tensor_copy(out=out_ap, in_=in_ap), reads, writes)
        else:
            kb.op("pool", lambda h: h.tensor_copy(out=out_ap, in_=in_ap), reads, writes)

    def build(self):
        kb = self.kb
        nc = self.nc
        dbg = self.debug
        x = self.din("x", [S, D])
        wcore = self.din("wcore", [D, WCOLS])
        normw = self.din("normw", [128, D])
        ident_in = self.din("ident", [128, 128])
        UT = kb.dram("UT", [12 * 128, S], F32)
        VML = kb.dram("VML", [S, 512], F32)
        OML = kb.dram("OML", [S, 512], F32)
        ZML = kb.dram("ZML", [S, 512], F32)
        MISC = kb.dram("MISC", [S, 272], F32)
        ZNS = kb.dram("ZNS", [S, 512], F32)
        tm_dst = [VML, OML, ZML, MISC, ZNS]
        tm_cols = [(1536, 512), (2048, 512), (2560, 512), (3072, 272), (3344, 512)]

        ident_f = kb.sb("ident_f", [128, 128], F32)
        ident_b = kb.sb("ident_b", [128, 128], BF16)
        kb.dma("sp", ident_f[:], ident_in[:], [ident_in], [ident_f], ident_f)
        kb.op("dve", lambda h: h.tensor_copy(out=ident_b[:], in_=ident_f[:]), [ident_f], [ident_b])

        psf = [kb.ps("psf%d" % i, [128, 512], F32) for i in range(6)]
        psb = [kb.ps("psb%d" % i, [128, 1024], BF16) for i in range(2)]

        with ExitStack() as es1:
            nw = kb.sb("nw", [128, D], F32, es1)
            kb.dma("sp", nw[:], normw[:], [normw], [nw], nw)
            hT = kb.sb("hT", [128, 32, 1024], BF16, es1)
            xt = [kb.sb("xt%d" % i, [128, D], F32, es1) for i in range(2)]
            xs = [kb.sb("xs%d" % i, [128, D], BF16, es1) for i in range(1)]
            ss = [kb.sb("ss%d" % i, [128, 1], F32, es1) for i in range(2)]
            rstd = [kb.sb("rstd%d" % i, [128, 1], F32, es1) for i in range(2)]
            wst = [kb.sb("wst%d" % i, [128, 2, 512], F32, es1) for i in range(3)]
            wbf = [kb.sb("wbf%d" % i, [128, 32, 512], BF16, es1) for i in range(2)]
            ost = [kb.sb("ost%d" % i, [128, 512], F32, es1) for i in range(3)]
            wcv = wcore.t.rearrange("(k p) c -> p k c", p=128)
            xtile = 0
            wld = 0
            ostc = 0
            psc = 0
            for tt in range(0 if self.skip1 else 4):
                for i8 in range(8):
                    ti = tt * 8 + i8
                    xb = xt[xtile % 2]
                    xsb = xs[0]
                    ssb = ss[xtile % 2]
                    rsb = rstd[xtile % 2]
                    xtile += 1
                    kb.dma("sp", xb[:], x[ti * 128:(ti + 1) * 128, :], [x], [xb], xb)
                    kb.op("act", lambda h, xb=xb, ssb=ssb, xsb=xsb: h.activation(
                        out=xsb[:], in_=xb[:], func=AF.Square, accum_out=ssb[:]),
                        [xb], [xsb, ssb])
                    kb.op("dve", lambda h, ssb=ssb, rsb=rsb: h.tensor_scalar(
                        out=rsb[:], in0=ssb[:], scalar1=1.0 / D, scalar2=EPS,
                        op0=ALU.mult, op1=ALU.add), [ssb], [rsb])
                    kb.op("act", lambda h, rsb=rsb: h.activation(
                        out=rsb[:], in_=rsb[:], func=AF.Sqrt), [rsb], [rsb])
                    kb.op("dve", lambda h, rsb=rsb: h.reciprocal(out=rsb[:], in_=rsb[:]),
                          [rsb], [rsb])
                    kb.op("dve", lambda h, xb=xb, rsb=rsb, xsb=xsb: h.scalar_tensor_tensor(
                        out=xsb[:], in0=xb[:], scalar=rsb[:], in1=nw[:],
                        op0=ALU.mult, op1=ALU.mult), [xb, rsb, nw], [xsb])
                    for j4 in range(4):
                        pb = psb[j4 % 2]
                        for k8 in range(8):
                            kk = j4 * 8 + k8
                            kb.op("pe", lambda h, pb=pb, k8=k8, kk=kk, xsb=xsb: h.transpose(
                                out=pb[:, k8 * 128:(k8 + 1) * 128],
                                in_=xsb[:, kk * 128:(kk + 1) * 128], identity=ident_b[:]),
                                [xsb, ident_b], [pb])
                        self.evac(hT[:, j4 * 8:(j4 + 1) * 8, i8 * 128:(i8 + 1) * 128],
                                  pb[:].rearrange("p (k t) -> p k t", k=8),
                                  [pb], [hT.sub(i8)])
                hT_all = [hT.sub(i) for i in range(8)]

                def load_w(c0, ncol):
                    nonlocal wld
                    wb = wbf[wld % 2]
                    wld += 1
                    for kq in range(16):
                        st = wst[kq % 3]
                        kb.dma("sp", st[:, :, :ncol], wcv[:, kq * 2:(kq + 1) * 2, c0:c0 + ncol],
                               [wcore], [st], st)
                        self.cast(wb[:, kq * 2:(kq + 1) * 2, :ncol], st[:, :, :ncol], [st], [wb])
                    return wb

                for cg in range(3):
                    wb = load_w(cg * 512, 512)
                    for c4 in range(4):
                        ct = cg * 4 + c4
                        for th in range(2):
                            pt = psf[psc % 6]
                            psc += 1
                            for k in range(32):
                                kb.op("pe", lambda h, pt=pt, wb=wb, c4=c4, k=k, th=th: h.matmul(
                                    pt[:], lhsT=wb[:, k, c4 * 128:(c4 + 1) * 128],
                                    rhs=hT[:, k, th * 512:(th + 1) * 512],
                                    start=(k == 0), stop=(k == 31)),
                                    [wb] + hT_all, [pt])
                            ob = ost[ostc % 3]
                            ostc += 1
                            self.evac(ob[:], pt[:], [pt], [ob])
                            t0 = tt * 1024 + th * 512
                            kb.dma("pool", UT[ct * 128:(ct + 1) * 128, t0:t0 + 512], ob[:],
                                   [ob], [UT.sub((ct, tt))], ob)
                for gi, (c0, ncol) in enumerate(tm_cols):
                    wb = load_w(c0, ncol)
                    for i8 in range(8):
                        pt = psf[psc % 6]
                        psc += 1
                        for k in range(32):
                            kb.op("pe", lambda h, pt=pt, wb=wb, k=k, i8=i8, ncol=ncol: h.matmul(
                                pt[:, :ncol], lhsT=hT[:, k, i8 * 128:(i8 + 1) * 128],
                                rhs=wb[:, k, :ncol], start=(k == 0), stop=(k == 31)),
                                [wb] + hT_all, [pt])
                        ob = ost[ostc % 3]
                        ostc += 1
                        self.evac(ob[:, :ncol], pt[:, :ncol], [pt], [ob])
                        t0 = tt * 1024 + i8 * 128
                        kb.dma("pool", tm_dst[gi][t0:t0 + 128, :], ob[:, :ncol],
                               [ob], [tm_dst[gi].sub(tt)], ob)

        kb.barrier()
        YT = kb.dram("YT", [8, 1024, 512], BF16)
        self.YT = YT
        if self.stop_after >= 2:
            self.phase2(UT, VML, OML, ZML, MISC, YT, psf, psb, ident_f, ident_b)

        if dbg and self.stop_after < 2:
            with ExitStack() as esd:
                db = kb.sb("dbgbuf", [128, 4096], F32, esd)
                o_ut = self.dout("o_ut", [12 * 128, S])
                ut_all = [UT.sub((ct, tt)) for ct in range(12) for tt in range(4)]
                for ct in range(12):
                    kb.dma("sp", db[:], UT[ct * 128:(ct + 1) * 128, :], ut_all, [db], db)
                    kb.dma("sp", o_ut[ct * 128:(ct + 1) * 128, :], db[:], [db], [o_ut], db)
                for nm, src, ncol in (("o_vml", VML, 512), ("o_misc", MISC, 272)):
                    o = self.dout(nm, [S, ncol])
                    srcs = [src.sub(tt) for tt in range(4)]
                    for i in range(NT):
                        kb.dma("sp", db[:, :ncol], src[i * 128:(i + 1) * 128, :], srcs, [db], db)
                        kb.dma("sp", o[i * 128:(i + 1) * 128, :], db[:, :ncol], [db], [o], db)

        kb.mute = False
        kb.barrier()
        if self.have_nsa and self.stop_after >= 3:
            self.phase3(UT, MISC, ZNS, YT, psf, psb, ident_b)
            kb.barrier()
        if self.stop_after >= 4:
            self.phase4(YT, psf, psb, ident_b)
            kb.barrier()
        if dbg and self.stop_after >= 2:
            with ExitStack() as esd:
                db2 = kb.sb("dbg2", [128, 8, 2048], BF16, esd)
                o_yt = self.dout("o_yt", [1024, S], BF16)
                yt_all = [YT.sub((k_, t_)) for k_ in ("ml", "ns") for t_ in range(8)]
                for tg in range(8):
                    kb.dma("sp", db2[:, :, 0:512], YT[tg].rearrange("(j p) t -> p j t", p=128),
                           yt_all, [db2], db2)
                    kb.dma("sp", o_yt[:, tg * 512:(tg + 1) * 512].rearrange("(j p) t -> p j t", p=128),
                           db2[:, :, 0:512], [db2], [o_yt], db2)
        kb.finish(self.outs)
        return nc


    def phase2(self, UT, VML, OML, ZML, MISC, YT, psf, psb, ident_f, ident_b):
        kb = self.kb
        LNC = float(np.log(128.0 ** -0.5))
        convw_in = self.din("convw", [128, 16])
        gb_in = self.din("gbias", [128, 4])
        hnw_in = self.din("hnw", [128, 512])
        tri_in = self.din("tri", [128, 128])
        ut_all = [UT.sub((ct, tt)) for ct in range(12) for tt in range(4)]
        misc_all = [MISC.sub(tt) for tt in range(4)]
        vml_all = [VML.sub(tt) for tt in range(4)]
        oml_all = [OML.sub(tt) for tt in range(4)]
        zml_all = [ZML.sub(tt) for tt in range(4)]
        with ExitStack() as es:
            convw = kb.sb("convw_s", [128, 16], F32, es)
            gb = kb.sb("gb_s", [128, 4], F32, es)
            hnw = kb.sb("hnw_s", [128, 512], F32, es)
            tri = kb.sb("tri_s", [128, 128], F32, es)
            ones = kb.sb("ones_s", [128, 128], F32, es)
            kb.dma("sp", convw[:], convw_in[:], [convw_in], [convw], convw)
            kb.dma("sp", gb[:], gb_in[:], [gb_in], [gb], gb)
            kb.dma("sp", hnw[:], hnw_in[:], [hnw_in], [hnw], hnw)
            kb.dma("sp", tri[:], tri_in[:], [tri_in], [tri], tri)
            kb.op("pool", lambda h: h.memset(ones[:], 1.0), [], [ones])
            G = kb.sb("G_s", [128, NT, 4], F32, es)
            for n in range(NT):
                kb.dma("sp", G[:, n, :], MISC[n * 128:(n + 1) * 128, 256:260],
                       misc_all, [G], G, group=True)
            LI = kb.sb("LI_s", [128, NT, 2], F32, es)
            LF = kb.sb("LF_s", [128, NT, 2], F32, es)
            for j in range(2):
                kb.op("dve", lambda h, j=j: h.tensor_scalar(
                    out=LI[:, :, j], in0=G[:, :, j], scalar1=gb[:, j:j + 1], scalar2=LNC,
                    op0=ALU.add, op1=ALU.add), [G, gb], [LI])
                kb.op("dve", lambda h, j=j: h.tensor_scalar(
                    out=LF[:, :, j], in0=G[:, :, 2 + j], scalar1=gb[:, 2 + j:3 + j], scalar2=None,
                    op0=ALU.add), [G, gb], [LF])
            kb.op("act", lambda h: h.activation(out=LF[:], in_=LF[:], func=AF.Exp, scale=-1.0),
                  [LF], [LF])
            kb.op("dve", lambda h: h.tensor_scalar_add(out=LF[:], in0=LF[:], scalar1=1.0), [LF], [LF])
            kb.op("act", lambda h: h.activation(out=LF[:], in_=LF[:], func=AF.Ln), [LF], [LF])
            kb.op("dve", lambda h: h.tensor_scalar_mul(out=LF[:], in0=LF[:], scalar1=-1.0), [LF], [LF])

            if self.p2stop < 1:
                return
            LFHL = kb.sb("LFHL_s", [128, NT, 2, 2], BF16, es)
            LFH32 = kb.sb("LFH32_s", [128, NT, 2], F32, es)
            LFL32 = kb.sb("LFL32_s", [128, NT, 2], F32, es)
            tri_b = kb.sb("tri_b", [128, 128], BF16, es)
            ones_b = kb.sb("ones_b", [128, 128], BF16, es)
            kb.op("dve", lambda h: h.tensor_copy(out=tri_b[:], in_=tri[:]), [tri], [tri_b])
            kb.op("dve", lambda h: h.tensor_copy(out=ones_b[:], in_=ones[:]), [ones], [ones_b])
            kb.op("dve", lambda h: h.tensor_copy(out=LFHL[:, :, :, 0], in_=LF[:]), [LF], [LFHL])
            kb.op("dve", lambda h: h.tensor_copy(out=LFH32[:], in_=LFHL[:, :, :, 0]), [LFHL], [LFH32])
            kb.op("dve", lambda h: h.tensor_tensor(out=LFL32[:], in0=LF[:], in1=LFH32[:],
                                                   op=ALU.subtract), [LF, LFH32], [LFL32])
            kb.op("dve", lambda h: h.tensor_copy(out=LFHL[:, :, :, 1], in_=LFL32[:]), [LFL32], [LFHL])
            P = kb.sb("P_s", [128, S + 4], F32, es)
            acc = kb.sb("acc_s", [128, S], F32, es)
            qc = [kb.sb("qc%d" % h_, [128, S], F32, es) for h_ in range(2)]
            kT = [kb.sb("kT%d" % h_, [128, S], BF16, es) for h_ in range(2)]
            kb.op("pool", lambda h: h.memset(P[:, 0:4], 0.0), [], [P.sub("pad")])
            for ci in range(4):
                kb.dma("sp", P[:, 4:4 + S], UT[ci * 128:(ci + 1) * 128, :], ut_all, [P], P)
                for j in range(4):
                    if j == 0:
                        kb.op("dve", lambda h, ci=ci: h.tensor_scalar(
                            out=acc[:], in0=P[:, 1:1 + S], scalar1=convw[:, ci * 4:ci * 4 + 1],
                            scalar2=None, op0=ALU.mult), [P, P.sub("pad"), convw], [acc])
                    else:
                        kb.op("dve", lambda h, ci=ci, j=j: h.scalar_tensor_tensor(
                            out=acc[:], in0=P[:, 1 + j:1 + j + S],
                            scalar=convw[:, ci * 4 + j:ci * 4 + j + 1], in1=acc[:],
                            op0=ALU.mult, op1=ALU.add), [P, P.sub("pad"), convw, acc], [acc])
                dst = qc[ci] if ci < 2 else kT[ci - 2]
                kb.op("act", lambda h: h.activation(out=P[:, 4:4 + S], in_=acc[:], func=AF.Exp,
                                                    scale=-1.0), [acc], [P])
                kb.op("dve", lambda h: h.tensor_scalar_add(out=P[:, 4:4 + S], in0=P[:, 4:4 + S],
                                                           scalar1=1.0), [P], [P])
                kb.op("dve", lambda h: h.reciprocal(out=P[:, 4:4 + S], in_=P[:, 4:4 + S]), [P], [P])
                kb.op("dve", lambda h, dst=dst: h.tensor_tensor(
                    out=dst[:], in0=acc[:], in1=P[:, 4:4 + S], op=ALU.mult), [acc, P], [dst])

            if self.p2stop < 2:
                return
            vst = kb.sb("vst_s", [128, 512], F32, es)
            vp = [kb.sb("vp%d" % h_, [128, NT, 257], BF16, es) for h_ in range(2)]
            for h_ in range(2):
                kb.op("pool", lambda h, h_=h_: h.memset(vp[h_][:, :, 256:257], 1.0), [], [vp[h_].sub("one")])
            for n in range(NT):
                kb.dma("sp", vst[:], VML[n * 128:(n + 1) * 128, :], vml_all, [vst], vst)
                for h_ in range(2):
                    self.cast(vp[h_][:, n, 0:256], vst[:, h_ * 256:(h_ + 1) * 256], [vst], [vp[h_]])

            if self.p2stop < 3:
                return
            C = [kb.sb("C%d" % h_, [128, 257], F32, es) for h_ in range(2)]
            Cb = [kb.sb("Cb%d" % h_, [128, 257], BF16, es) for h_ in range(2)]
            for h_ in range(2):
                kb.op("pool", lambda h, h_=h_: h.memset(C[h_][:], 0.0), [], [C[h_]])
                kb.op("pool", lambda h, h_=h_: h.memset(Cb[h_][:], 0.0), [], [Cb[h_]])
            LFB = [kb.sb("LFB%d" % i, [128, 128], BF16, es) for i in range(2)]
            LFBl = [kb.sb("LFBl%d" % i, [128, 128], BF16, es) for i in range(2)]
            eBr = [kb.sb("eBr%d" % i, [128, 128], F32, es) for i in range(2)]
            eBL = [kb.sb("eBL%d" % i, [128, 1], F32, es) for i in range(2)]
            E2 = [kb.sb("E2_%d" % i, [128, 128], F32, es) for i in range(2)]
            hraw = [kb.sb("hraw%d" % i, [128, 257], F32, es) for i in range(2)]
            wk = [kb.sb("wk%d" % i, [128, 1], F32, es) for i in range(2)]
            qs = [kb.sb("qs%d" % i, [128, 128], BF16, es) for i in range(2)]
            scT = [kb.sb("scT%d" % i, [128, 128], BF16, es) for i in range(2)]
            kh = [kb.sb("kh%d" % i, [128, 128], BF16, es) for i in range(2)]
            dm = [kb.sb("dm%d" % i, [128, 1], F32, es) for i in range(2)]
            hn = [kb.sb("hn%d" % i, [128, 256], F32, es) for i in range(2)]
            sq = [kb.sb("sq%d" % i, [128, 256], F32, es) for i in range(2)]
            ssq = [kb.sb("ssq%d" % i, [128, 1], F32, es) for i in range(2)]
            ot = [kb.sb("ot%d" % i, [128, 512], F32, es) for i in range(2)]
            zt = [kb.sb("zt%d" % i, [128, 512], F32, es) for i in range(2)]
            eo = [kb.sb("eo%d" % i, [128, 256], F32, es) for i in range(2)]
            ez = [kb.sb("ez%d" % i, [128, 256], F32, es) for i in range(2)]
            yb = [kb.sb("yb%d" % i, [128, 256], BF16, es) for i in range(2)]
            yst = [kb.sb("yst%d" % i, [128, 4, 512], BF16, es) for i in range(2)]
            it = 0
            for c in range(self.nchunks if self.p2stop >= 4 else 0):
                o_t = ot[c % 2]
                z_t = zt[c % 2]
                kb.dma("sp", o_t[:], OML[c * 128:(c + 1) * 128, :], oml_all, [o_t], o_t)
                kb.dma("sp", z_t[:], ZML[c * 128:(c + 1) * 128, :], zml_all, [z_t], z_t)
                ys = yst[(c // 4) % 2]
                for h_ in range(2):
                    i2 = it % 2
                    it += 1
                    cs = slice(c * 128, (c + 1) * 128)
                    pA = psf[0 + i2]
                    pO = psf[2 + i2]
                    pC = psf[4 + i2]
                    pT = psb[i2]
                    lfc = LF[:, c, h_:h_ + 1]
                    lic = LI[:, c, h_:h_ + 1]
                    kb.op("dve", lambda h, i2=i2, c=c, h_=h_: h.tensor_scalar(
                        out=LFB[i2][:], in0=ones_b[:], scalar1=LFH32[:, c, h_:h_ + 1], scalar2=None,
                        op0=ALU.mult), [ones_b, LFH32], [LFB[i2]])
                    kb.op("dve", lambda h, i2=i2, c=c, h_=h_: h.tensor_scalar(
                        out=LFBl[i2][:], in0=ones_b[:], scalar1=LFL32[:, c, h_:h_ + 1], scalar2=None,
                        op0=ALU.mult), [ones_b, LFL32], [LFBl[i2]])
                    kb.op("pe", lambda h, i2=i2, pA=pA: h.matmul(
                        pA[:, 0:128], lhsT=LFB[i2][:], rhs=tri_b[:], start=True, stop=False),
                        [LFB[i2], tri_b], [pA])
                    kb.op("pe", lambda h, i2=i2, pA=pA: h.matmul(
                        pA[:, 0:128], lhsT=LFBl[i2][:], rhs=tri_b[:], start=False, stop=True),
                        [LFBl[i2], tri_b], [pA])
                    kb.op("pe", lambda h, i2=i2, pA=pA: h.matmul(
                        pA[:, 128:256], lhsT=tri_b[:], rhs=LFB[i2][:], start=True, stop=False),
                        [LFB[i2], tri_b], [pA])
                    kb.op("pe", lambda h, i2=i2, pA=pA: h.matmul(
                        pA[:, 128:256], lhsT=tri_b[:], rhs=LFBl[i2][:], start=False, stop=True),
                        [LFBl[i2], tri_b], [pA])
                    kb.op("act", lambda h, i2=i2, pA=pA: h.activation(
                        out=eBr[i2][:], in_=pA[:, 0:128], func=AF.Exp), [pA], [eBr[i2]])
                    kb.op("act", lambda h, i2=i2, pA=pA, lic=lic: h.activation(
                        out=E2[i2][:], in_=pA[:, 128:256], func=AF.Exp, scale=-1.0, bias=lic),
                        [pA, LI], [E2[i2]])
                    kb.op("dve", lambda h, i2=i2: h.tensor_tensor(
                        out=wk[i2][:], in0=E2[i2][:, 0:1], in1=eBr[i2][:, 127:128], op=ALU.mult),
                        [E2[i2], eBr[i2]], [wk[i2]])
                    if self.cstop <= 1:
                        kb.mute = True
                    kb.op("dve", lambda h, i2=i2, h_=h_, cs=cs: h.tensor_tensor(
                        out=qs[i2][:], in0=qc[h_][:, cs], in1=eBr[i2][:], op=ALU.mult),
                        [qc[h_], eBr[i2]], [qs[i2]])
                    kb.op("pe", lambda h, i2=i2, h_=h_, cs=cs, pA=pA: h.matmul(
                        pA[:, 256:384], lhsT=kT[h_][:, cs], rhs=qs[i2][:], start=True, stop=True),
                        [kT[h_], qs[i2]], [pA])
                    kb.op("dve", lambda h, i2=i2, pA=pA: h.scalar_tensor_tensor(
                        out=scT[i2][:], in0=pA[:, 256:384], scalar=E2[i2][:, 0:1], in1=tri[:],
                        op0=ALU.mult, op1=ALU.mult), [pA, E2[i2], tri], [scT[i2]])
                    if self.cstop <= 2:
                        kb.mute = True
                    kb.op("pe", lambda h, h_=h_, cs=cs, pT=pT: h.transpose(
                        out=pT[:, 0:128], in_=kT[h_][:, cs], identity=ident_b[:]),
                        [kT[h_], ident_b], [pT])
                    kb.op("act", lambda h, i2=i2, pT=pT: h.activation(
                        out=kh[i2][:], in_=pT[:, 0:128], func=AF.Copy, scale=wk[i2][:]),
                        [pT, wk[i2]], [kh[i2]])
                    if self.cstop <= 3:
                        kb.mute = True
                    kb.op("pe", lambda h, i2=i2, h_=h_, pO=pO: h.matmul(
                        pO[:, 0:257], lhsT=qs[i2][:], rhs=Cb[h_][:], start=True, stop=False),
                        [qs[i2], Cb[h_]], [pO])
                    kb.op("pe", lambda h, i2=i2, h_=h_, c=c, pO=pO: h.matmul(
                        pO[:, 0:257], lhsT=scT[i2][:], rhs=vp[h_][:, c, :], start=False, stop=True),
                        [scT[i2], vp[h_], vp[h_].sub("one")], [pO])
                    kb.op("pe", lambda h, i2=i2, h_=h_, c=c, pC=pC: h.matmul(
                        pC[:, 0:257], lhsT=kh[i2][:], rhs=vp[h_][:, c, :], start=True, stop=True),
                        [kh[i2], vp[h_], vp[h_].sub("one")], [pC])
                    kb.op("dve", lambda h, i2=i2, h_=h_, pC=pC: h.scalar_tensor_tensor(
                        out=C[h_][:], in0=C[h_][:], scalar=eBr[i2][:, 127:128], in1=pC[:, 0:257],
                        op0=ALU.mult, op1=ALU.add), [C[h_], eBr[i2], pC], [C[h_]])
                    kb.op("pool", lambda h, h_=h_: h.tensor_copy(out=Cb[h_][:], in_=C[h_][:]),
                          [C[h_]], [Cb[h_]])
                    if self.cstop <= 4:
                        kb.mute = True
                    kb.op("act", lambda h, i2=i2, pO=pO: h.activation(
                        out=hraw[i2][:], in_=pO[:, 0:257], func=AF.Copy), [pO], [hraw[i2]])
                    kb.op("dve", lambda h, i2=i2: h.tensor_scalar(
                        out=wk[i2][:], in0=hraw[i2][:, 256:257], scalar1=-1.0, scalar2=1.0,
                        op0=ALU.mult, op1=ALU.max), [hraw[i2]], [wk[i2]])
                    kb.op("dve", lambda h, i2=i2: h.scalar_tensor_tensor(
                        out=dm[i2][:], in0=hraw[i2][:, 256:257], scalar=1.0, in1=wk[i2][:],
                        op0=ALU.max, op1=ALU.max), [hraw[i2], wk[i2]], [dm[i2]])
                    kb.op("dve", lambda h, i2=i2: h.reciprocal(out=dm[i2][:], in_=dm[i2][:]),
                          [dm[i2]], [dm[i2]])
                    kb.op("act", lambda h, i2=i2: h.activation(
                        out=sq[i2][:], in_=hraw[i2][:, 0:256], func=AF.Square, scale=dm[i2][:],
                        accum_out=ssq[i2][:]), [hraw[i2], dm[i2]], [sq[i2], ssq[i2]])
                    kb.op("dve", lambda h, i2=i2: h.tensor_scalar(
                        out=ssq[i2][:], in0=ssq[i2][:], scalar1=1.0 / 256, scalar2=EPS,
                        op0=ALU.mult, op1=ALU.add), [ssq[i2]], [ssq[i2]])
                    kb.op("act", lambda h, i2=i2: h.activation(
                        out=ssq[i2][:], in_=ssq[i2][:], func=AF.Ln), [ssq[i2]], [ssq[i2]])
                    kb.op("act", lambda h, i2=i2: h.activation(
                        out=ssq[i2][:], in_=ssq[i2][:], func=AF.Exp, scale=-0.5), [ssq[i2]], [ssq[i2]])
                    kb.op("dve", lambda h, i2=i2: h.tensor_tensor(
                        out=ssq[i2][:], in0=ssq[i2][:], in1=dm[i2][:], op=ALU.mult),
                        [ssq[i2], dm[i2]], [ssq[i2]])
                    if self.cstop <= 5:
                        kb.mute = True
                    hs = slice(h_ * 256, (h_ + 1) * 256)
                    kb.op("act", lambda h, i2=i2, o_t=o_t, hs=hs: h.activation(
                        out=eo[i2][:], in_=o_t[:, hs], func=AF.Exp, scale=-1.0), [o_t], [eo[i2]])
                    kb.op("act", lambda h, i2=i2, z_t=z_t, hs=hs: h.activation(
                        out=ez[i2][:], in_=z_t[:, hs], func=AF.Exp, scale=-1.0), [z_t], [ez[i2]])
                    kb.op("pool", lambda h, i2=i2: h.tensor_scalar_add(
                        out=ez[i2][:], in0=ez[i2][:], scalar1=1.0), [ez[i2]], [ez[i2]])
                    kb.op("pool", lambda h, i2=i2: h.tensor_scalar_add(
                        out=eo[i2][:], in0=eo[i2][:], scalar1=1.0), [eo[i2]], [eo[i2]])
                    kb.op("pool", lambda h, i2=i2: h.tensor_tensor(
                        out=eo[i2][:], in0=eo[i2][:], in1=ez[i2][:], op=ALU.mult),
                        [eo[i2], ez[i2]], [eo[i2]])
                    kb.op("dve", lambda h, i2=i2: h.reciprocal(out=eo[i2][:], in_=eo[i2][:]),
                          [eo[i2]], [eo[i2]])
                    kb.op("pool", lambda h, i2=i2, z_t=z_t, hs=hs: h.tensor_tensor(
                        out=eo[i2][:], in0=eo[i2][:], in1=z_t[:, hs], op=ALU.mult),
                        [eo[i2], z_t], [eo[i2]])
                    kb.op("dve", lambda h, i2=i2, hs=hs: h.scalar_tensor_tensor(
                        out=hn[i2][:], in0=hraw[i2][:, 0:256], scalar=ssq[i2][:], in1=hnw[:, hs],
                        op0=ALU.mult, op1=ALU.mult), [hraw[i2], ssq[i2], hnw], [hn[i2]])
                    kb.op("dve", lambda h, i2=i2: h.tensor_tensor(
                        out=yb[i2][:], in0=hn[i2][:], in1=eo[i2][:], op=ALU.mult),
                        [hn[i2], eo[i2]], [yb[i2]])
                    if self.cstop <= 6:
                        kb.mute = True
                    for j in range(2):
                        kb.op("pe", lambda h, i2=i2, j=j, pT=pT: h.transpose(
                            out=pT[:, 256 + j * 128:256 + (j + 1) * 128],
                            in_=yb[i2][:, j * 128:(j + 1) * 128], identity=ident_b[:]),
                            [yb[i2], ident_b], [pT])
                    self.evac(ys[:, h_ * 2:h_ * 2 + 2, (c % 4) * 128:(c % 4 + 1) * 128],
                              pT[:, 256:512].rearrange("p (j t) -> p j t", j=2), [pT], [ys])
                if c % 4 == 3:
                    kb.dma("pool", YT[c // 4, 0:512, :].rearrange("(j p) t -> p j t", p=128),
                           ys[:], [ys], [YT.sub(("ml", c // 4))], ys)


    def phase4(self, YT, psf, psb, ident_b):
        kb = self.kb
        RG = [[0, 1, 2, 3], [4, 5, 6, 7]]
        wq_in = self.din("wq", [D, 1024])
        gq_in = self.din("gq", [D, 1024])
        pq_in = self.din("pq", [256, 1024])
        pT_in = self.din("pT", [256, S])
        xq_in = self.din("xq", [S, 1024])
        out = self.dout("out", [S, 1024])
        YG = kb.dram("YG", [8, 4 * 1024, 512], BF16)
        XT = kb.dram("XT", [8, 1024, 512], BF16)
        XG = kb.dram("XG", [8, 4 * 1024, 512], BF16)
        XN = kb.dram("XN", [S, 1024], F32)
        cc = TL(kb.new_sem("cc"), 1)
        with ExitStack() as es:
            if not self.have_nsa:
                zt_ = kb.sb("zeros_b", [128, 4, 512], BF16, es)
                kb.op("pool", lambda h: h.memset(zt_[:], 0.0), [], [zt_])
                for tg in range(8):
                    kb.dma("sp", YT[tg, 512:1024, :].rearrange("(j p) t -> p j t", p=128),
                           zt_[:], [zt_], [YT.sub(("ns", tg))], zt_)
            for tg in range(8):
                kb.custom("pool", lambda h, tg=tg: h.collective_compute(
                    "AllGather", ALU.bypass, replica_groups=RG, ins=[YT[tg]], outs=[YG[tg]]),
                    [YT.sub(("ml", tg)), YT.sub(("ns", tg))], [YG.sub(tg)], cc)
            wbf = kb.sb("w4bf", [128, 32, 1024], BF16, es)
            wst = [kb.sb("w4st%d" % i, [128, 1, 1024], F32, es) for i in range(3)]
            act_t = [kb.sb("a4t%d" % i, [128, 32, 512], BF16, es) for i in range(2)]
            xt_ = [kb.sb("x4t%d" % i, [128, 512], F32, es) for i in range(2)]
            xn = [kb.sb("xn4_%d" % i, [128, 512], F32, es) for i in range(2)]
            xnb = [kb.sb("xnb4_%d" % i, [128, 512], BF16, es) for i in range(2)]
            xTs = [kb.sb("xTs%d" % i, [128, 8, 512], BF16, es) for i in range(2)]
            sg = [kb.sb("sg4_%d" % i, [128, 512], F32, es) for i in range(2)]
            pqb = kb.sb("pqb", [128, 2, 1024], BF16, es)
            pTb = kb.sb("pTb", [128, 2, S], BF16, es)

            def load_w(src):
                v = src.t.rearrange("(k p) c -> p k c", p=128)
                for kq in range(32):
                    st = wst[kq % 3]
                    kb.dma("sp", st[:], v[:, kq:kq + 1, :], [src], [st], st)
                    self.cast(wbf[:, kq:kq + 1, :], st[:], [st], [wbf])

            load_w(wq_in)
            cnt = 0
            for tg in range(8):
                at = act_t[tg % 2]
                for q4 in range(4):
                    kb.dma("sp", at[:, q4 * 8:(q4 + 1) * 8, :],
                           YG[tg].rearrange("(k p) t -> p k t", p=128)[:, q4 * 8:(q4 + 1) * 8, :],
                           [YG.sub(tg)], [at], at, group=(q4 > 0))
                xs_ = xTs[tg % 2]
                for i4 in range(4):
                    t0 = tg * 512 + i4 * 128
                    for nt in range(2):
                        pt = psf[cnt % 4]
                        xb = xt_[cnt % 2]
                        xo = xn[cnt % 2]
                        xob = xnb[cnt % 2]
                        cnt += 1
                        kb.dma("sp", xb[:], xq_in[t0:t0 + 128, nt * 512:(nt + 1) * 512],
                               [xq_in], [xb], xb)
                        for k in range(32):
                            kb.op("pe", lambda h, pt=pt, at=at, k=k, i4=i4, nt=nt: h.matmul(
                                pt[:], lhsT=at[:, k, i4 * 128:(i4 + 1) * 128],
                                rhs=wbf[:, k, nt * 512:(nt + 1) * 512],
                                start=(k == 0), stop=(k == 31)), [at, wbf], [pt])
                        kb.op("dve", lambda h, pt=pt, xb=xb, xo=xo: h.tensor_tensor(
                            out=xo[:], in0=pt[:], in1=xb[:], op=ALU.add), [pt, xb], [xo])
                        kb.dma("pool", XN[t0:t0 + 128, nt * 512:(nt + 1) * 512], xo[:],
                               [xo], [XN], xo)
                        kb.op("act", lambda h, xo=xo, xob=xob: h.activation(
                            out=xob[:], in_=xo[:], func=AF.Copy), [xo], [xob])
                        pb = psb[cnt % 2]
                        for j in range(4):
                            kb.op("pe", lambda h, pb=pb, xob=xob, j=j: h.transpose(
                                out=pb[:, j * 128:(j + 1) * 128],
                                in_=xob[:, j * 128:(j + 1) * 128], identity=ident_b[:]),
                                [xob, ident_b], [pb])
                        self.evac(xs_[:, nt * 4:(nt + 1) * 4, i4 * 128:(i4 + 1) * 128],
                                  pb[:, 0:512].rearrange("p (j t) -> p j t", j=4), [pb], [xs_])
                kb.dma("pool", XT[tg].rearrange("(j p) t -> p j t", p=128),
                       xs_[:], [xs_], [XT.sub(tg)], xs_)
                kb.custom("pool", lambda h, tg=tg: h.collective_compute(
                    "AllGather", ALU.bypass, replica_groups=RG, ins=[XT[tg]], outs=[XG[tg]]),
                    [XT.sub(tg)], [XG.sub(tg)], cc)
            load_w(gq_in)
            for j in range(2):
                st = wst[j % 3]
                kb.dma("sp", st[:, 0, :], pq_in[j * 128:(j + 1) * 128, :], [pq_in], [st], st)
                self.cast(pqb[:, j, :], st[:, 0, :], [st], [pqb])
            for j in range(2):
                for q in range(4):
                    st = wst[(j * 4 + q) % 3]
                    kb.dma("sp", st[:, 0, :], pT_in[j * 128:(j + 1) * 128, q * 1024:(q + 1) * 1024],
                           [pT_in], [st], st)
                    self.cast(pTb[:, j, q * 1024:(q + 1) * 1024], st[:, 0, :], [st], [pTb])
            for tg in range(8):
                at = act_t[tg % 2]
                for q4 in range(4):
                    kb.dma("sp", at[:, q4 * 8:(q4 + 1) * 8, :],
                           XG[tg].rearrange("(k p) t -> p k t", p=128)[:, q4 * 8:(q4 + 1) * 8, :],
                           [XG.sub(tg)], [at], at, group=(q4 > 0))
                for i4 in range(4):
                    t0 = tg * 512 + i4 * 128
                    for nt in range(2):
                        pt = psf[cnt % 4]
                        pp = psf[4 + cnt % 2]
                        xb = xt_[cnt % 2]
                        xo = xn[cnt % 2]
                        sgt = sg[cnt % 2]
                        cnt += 1
                        kb.dma("sp", xb[:], XN[t0:t0 + 128, nt * 512:(nt + 1) * 512], [XN], [xb], xb)
                        for k in range(32):
                            kb.op("pe", lambda h, pt=pt, at=at, k=k, i4=i4, nt=nt: h.matmul(
                                pt[:], lhsT=at[:, k, i4 * 128:(i4 + 1) * 128],
                                rhs=wbf[:, k, nt * 512:(nt + 1) * 512],
                                start=(k == 0), stop=(k == 31)), [at, wbf], [pt])
                        for k in range(2):
                            kb.op("pe", lambda h, pp=pp, k=k, t0=t0, nt=nt: h.matmul(
                                pp[:], lhsT=pTb[:, k, t0:t0 + 128],
                                rhs=pqb[:, k, nt * 512:(nt + 1) * 512],
                                start=(k == 0), stop=(k == 1)), [pTb, pqb], [pp])
                        kb.op("act", lambda h, pt=pt, sgt=sgt: h.activation(
                            out=sgt[:], in_=pt[:], func=AF.Sigmoid), [pt], [sgt])
                        kb.op("dve", lambda h, pp=pp, sgt=sgt: h.tensor_tensor(
                            out=sgt[:], in0=sgt[:], in1=pp[:], op=ALU.mult), [sgt, pp], [sgt])
                        kb.op("pool", lambda h, sgt=sgt, xb=xb, xo=xo: h.tensor_tensor(
                            out=xo[:], in0=sgt[:], in1=xb[:], op=ALU.add), [sgt, xb], [xo])
                        kb.dma("pool", out[t0:t0 + 128, nt * 512:(nt + 1) * 512], xo[:],
                               [xo], [out], xo)

    def phase3(self, UT, MISC, ZNS, YT, psf, psb, ident_b):
        kb = self.kb
        ut_all = [UT.sub((ct, tt)) for ct in range(12) for tt in range(4)]
        misc_all = [MISC.sub(tt) for tt in range(4)]
        zns_all = [ZNS.sub(tt) for tt in range(4)]
        qnw_in = self.din("qnw", [128, 1])
        knw_in = self.din("knw", [128, 3])
        peK_in = self.din("peK", [128, 32])
        peV_in = self.din("peV", [128, 32])
        w1k_in = self.din("w1k", [D, 256])
        w2k_in = self.din("w2k", [256, 128])
        w1v_in = self.din("w1v", [D, 256])
        w2v_in = self.din("w2v", [256, 128])
        biasC_in = self.din("biasC", [8, 128, S])
        stripS_in = self.din("stripS", [4, 128, 1024])
        stripW_in = self.din("stripW", [4, 128, 1408])
        b31_in = self.din("b31", [128, 4])
        ov_in = self.din("ov", [256, 64])
        E_in = self.din("Esel", [64, S])
        keep_in = self.din("keep", [128, NT * 64])
        add_in = self.din("addc", [128, NT * 64])
        with ExitStack() as es:
            def small(name, src, shape):
                t = kb.sb(name, shape, F32, es)
                kb.dma("sp", t[:], src[:], [src], [t], t)
                return t
            qnw = small("qnw_s", qnw_in, [128, 1])
            knw = small("knw_s", knw_in, [128, 3])
            peK = small("peK_s", peK_in, [128, 32])
            peV = small("peV_s", peV_in, [128, 32])
            b31 = small("b31_s", b31_in, [128, 4])
            keep = small("keep_s", keep_in, [128, NT * 64])
            addc = small("add_s", add_in, [128, NT * 64])
            ones_b = kb.sb("ones3b", [128, 128], BF16, es)
            kb.op("pool", lambda h: h.memset(ones_b[:], 1.0), [], [ones_b])
            qn = [kb.sb("qn%d" % h_, [128, S], BF16, es) for h_ in range(4)]
            ksn = kb.sb("ksn", [128, S], BF16, es)
            kwn = kb.sb("kwn", [128, S], BF16, es)
            kcn = kb.sb("kcn", [128, 256], BF16, es)
            vcp = kb.sb("vcp", [128, 2, 193], BF16, es)
            vsp = kb.sb("vsp", [128, NT, 129], BF16, es)
            vwp = kb.sb("vwp", [128, NT, 129], BF16, es)
            stS = kb.sb("stS", [128, 4, 1024], BF16, es)
            stW = kb.sb("stW", [128, 4, 1408], BF16, es)
            Eb = kb.sb("Eb", [64, S], BF16, es)
            SG = kb.sb("SG", [128, NT, 12], F32, es)
            with ExitStack() as es2:
                X = kb.sb("X3", [128, S], F32, es2)
                sqb = [kb.sb("sqb%d" % i, [128, 512], BF16, es2) for i in range(2)]
                rr_ = [kb.sb("rr%d" % i, [128, 512], F32, es2) for i in range(2)]
                cnt = [0]

                def norm_block(src_ap, n, wcol, mul, add, srcbuf):
                    i2 = cnt[0] % 2
                    cnt[0] += 1
                    ps = psf[i2]
                    kb.op("act", lambda h: h.activation(out=sqb[i2][:, :n], in_=src_ap, func=AF.Square),
                          [srcbuf], [sqb[i2]])
                    kb.op("pe", lambda h: h.matmul(ps[:, :n], lhsT=ones_b[:], rhs=sqb[i2][:, :n],
                                                   start=True, stop=True), [ones_b, sqb[i2]], [ps])
                    kb.op("dve", lambda h: h.tensor_scalar(out=rr_[i2][:, :n], in0=ps[:, :n], scalar1=mul,
                                                           scalar2=add, op0=ALU.mult, op1=ALU.add),
                          [ps], [rr_[i2]])
                    kb.op("act", lambda h: h.activation(out=rr_[i2][:, :n], in_=rr_[i2][:, :n], func=AF.Ln),
                          [rr_[i2]], [rr_[i2]])
                    kb.op("act", lambda h: h.activation(out=rr_[i2][:, :n], in_=rr_[i2][:, :n], func=AF.Exp,
                                                        scale=-0.5), [rr_[i2]], [rr_[i2]])
                    return i2

                def fm_norm(ct, wcol, mul, add, dst, wbuf):
                    kb.dma("sp", X[:], UT[ct * 128:(ct + 1) * 128, :], ut_all, [X], X)
                    for cb in range(8):
                        cs = slice(cb * 512, (cb + 1) * 512)
                        i2 = norm_block(X[:, cs], 512, wcol, mul, add, X)
                        kb.op("dve", lambda h, i2=i2, cs=cs: h.scalar_tensor_tensor(
                            out=dst[:, cs], in0=X[:, cs], scalar=wcol, in1=rr_[i2][:],
                            op0=ALU.mult, op1=ALU.mult), [X, rr_[i2], wbuf], [dst])

                for h_ in range(4):
                    fm_norm(4 + h_, qnw[:, 0:1], 1.0, 128.0 * EPS, qn[h_], qnw)
                fm_norm(10, knw[:, 1:2], 1.0 / 128, EPS, ksn, knw)
                fm_norm(11, knw[:, 2:3], 1.0 / 128, EPS, kwn, knw)

                Rlo = kb.sb("Rlo", [128, 16, 256], BF16, es2)
                Rhi = kb.sb("Rhi", [128, 16, 256], BF16, es2)
                w1b = kb.sb("w1b", [128, 32, 256], BF16, es2)
                w1st = [kb.sb("w1st%d" % i, [128, 4, 256], F32, es2) for i in range(2)]
                w2st = kb.sb("w2st", [128, 2, 128], F32, es2)
                w2b = kb.sb("w2b", [128, 2, 128], BF16, es2)
                AT = [kb.sb("AT%d" % i, [128, 256], BF16, es2) for i in range(2)]
                tmpa = kb.sb("tmpa", [128, 256], F32, es2)
                kcf = kb.sb("kcf", [128, 256], F32, es2)
                ovst = kb.sb("ovst", [128, 2, 64], F32, es2)
                for which in range(2):
                    ct = 8 + which
                    pe_ = peK if which == 0 else peV
                    w1_in = w1k_in if which == 0 else w1v_in
                    w2_in = w2k_in if which == 0 else w2v_in
                    kb.dma("sp", X[:], UT[ct * 128:(ct + 1) * 128, :], ut_all, [X], X)
                    Xv = X.t[:].rearrange("d (c l) -> d l c", l=16)
                    for l in range(16):
                        kb.op("dve", lambda h, l=l, pe_=pe_, Xv=Xv: h.tensor_scalar(
                            out=Rlo[:, l, :], in0=Xv[:, l, :], scalar1=pe_[:, l:l + 1], scalar2=None,
                            op0=ALU.add), [X, pe_], [Rlo])
                        kb.op("pool", lambda h, l=l, pe_=pe_, Xv=Xv: h.tensor_scalar(
                            out=Rhi[:, l, :], in0=Xv[:, l, :], scalar1=pe_[:, 16 + l:17 + l], scalar2=None,
                            op0=ALU.add), [X, pe_], [Rhi])
                    w1v_ = w1_in.t.rearrange("(l d) j -> d l j", d=128)
                    for q in range(8):
                        st = w1st[q % 2]
                        kb.dma("sp", st[:], w1v_[:, q * 4:(q + 1) * 4, :], [w1_in], [st], st)
                        self.cast(w1b[:, q * 4:(q + 1) * 4, :], st[:], [st], [w1b])
                    kb.dma("sp", w2st[:], w2_in.t.rearrange("(j p) d -> p j d", p=128), [w2_in], [w2st], w2st)
                    kb.op("dve", lambda h: h.tensor_copy(out=w2b[:], in_=w2st[:]), [w2st], [w2b])
                    for jt in range(2):
                        ps = psf[2 + jt]
                        for l in range(32):
                            rhs = Rlo[:, l, 0:255] if l < 16 else Rhi[:, l - 16, 1:256]
                            kb.op("pe", lambda h, ps=ps, l=l, jt=jt, rhs=rhs: h.matmul(
                                ps[:, 0:255], lhsT=w1b[:, l, jt * 128:(jt + 1) * 128], rhs=rhs,
                                start=(l == 0), stop=(l == 31)), [w1b, Rlo, Rhi], [ps])
                        kb.op("act", lambda h, ps=ps: h.activation(out=tmpa[:, 0:255], in_=ps[:, 0:255],
                                                                   func=AF.Exp, scale=-1.0), [ps], [tmpa])
                        kb.op("dve", lambda h: h.tensor_scalar_add(out=tmpa[:, 0:255], in0=tmpa[:, 0:255],
                                                                   scalar1=1.0), [tmpa], [tmpa])
                        kb.op("dve", lambda h: h.reciprocal(out=tmpa[:, 0:255], in_=tmpa[:, 0:255]),
                              [tmpa], [tmpa])
                        kb.op("pool", lambda h, jt=jt: h.memset(AT[jt][:, 255:256], 0.0), [], [AT[jt].sub("z")])
                        kb.op("dve", lambda h, ps=ps, jt=jt: h.tensor_tensor(
                            out=AT[jt][:, 0:255], in0=tmpa[:, 0:255], in1=ps[:, 0:255], op=ALU.mult),
                            [tmpa, ps], [AT[jt]])
                    if which == 0:
                        ps = psf[4]
                        for jt in range(2):
                            kb.op("pe", lambda h, ps=ps, jt=jt: h.matmul(
                                ps[:, 0:256], lhsT=w2b[:, jt, :], rhs=AT[jt][:], start=(jt == 0),
                                stop=(jt == 1)), [w2b, AT[jt], AT[jt].sub("z")], [ps])
                        kb.op("act", lambda h, ps=ps: h.activation(out=kcf[:], in_=ps[:, 0:256], func=AF.Copy),
                              [ps], [kcf])
                        i2 = norm_block(kcf[:], 256, None, 1.0 / 128, EPS, kcf)
                        kb.op("dve", lambda h, i2=i2: h.scalar_tensor_tensor(
                            out=kcn[:], in0=kcf[:], scalar=knw[:, 0:1], in1=rr_[i2][:, 0:256],
                            op0=ALU.mult, op1=ALU.mult), [kcf, rr_[i2], knw], [kcn])
                    else:
                        for c2 in range(2):
                            ps = psf[4 + c2]
                            for jt in range(2):
                                kb.op("pe", lambda h, ps=ps, jt=jt, c2=c2: h.matmul(
                                    ps[:, 0:128], lhsT=AT[jt][:, c2 * 128:(c2 + 1) * 128], rhs=w2b[:, jt, :],
                                    start=(jt == 0), stop=(jt == 1)), [w2b, AT[jt], AT[jt].sub("z")], [ps])
                            kb.op("act", lambda h, ps=ps, c2=c2: h.activation(
                                out=vcp[:, c2, 0:128], in_=ps[:, 0:128], func=AF.Copy), [ps], [vcp])
                kb.op("pool", lambda h: h.memset(vcp[:, :, 128:129], 1.0), [], [vcp.sub("one")])
                kb.dma("sp", ovst[:], ov_in.t.rearrange("(c p) n -> p c n", p=128), [ov_in], [ovst], ovst)
                kb.op("dve", lambda h: h.tensor_copy(out=vcp[:, :, 129:193], in_=ovst[:]), [ovst], [vcp.sub("ov")])
                vcp_all = [vcp, vcp.sub("one"), vcp.sub("ov")]
                for h_ in range(4):
                    kb.dma("sp", X[:, 0:1024], stripS_in[h_], [stripS_in], [X], X)
                    self.cast(stS[:, h_, :], X[:, 0:1024], [X], [stS])
                    kb.dma("sp", X[:, 0:1408], stripW_in[h_], [stripW_in], [X], X)
                    self.cast(stW[:, h_, :], X[:, 0:1408], [X], [stW])
                kb.dma("sp", X[0:64, :], E_in[:], [E_in], [X], X)
                kb.op("dve", lambda h: h.tensor_copy(out=Eb[:], in_=X[0:64, :]), [X], [Eb])
                kb.op("pool", lambda h: h.memset(vsp[:, :, 128:129], 1.0), [], [vsp.sub("one")])
                kb.op("pool", lambda h: h.memset(vwp[:, :, 128:129], 1.0), [], [vwp.sub("one")])
                vst = [kb.sb("vst3_%d" % i, [128, 272], F32, es2) for i in range(2)]
                for n in range(NT):
                    st = vst[n % 2]
                    kb.dma("sp", st[:], MISC[n * 128:(n + 1) * 128, :], misc_all, [st], st)
                    self.cast(vsp[:, n, 0:128], st[:, 0:128], [st], [vsp])
                    self.cast(vwp[:, n, 0:128], st[:, 128:256], [st], [vwp])
                    kb.op("pool", lambda h, st=st, n=n: h.tensor_copy(out=SG[:, n, :], in_=st[:, 260:272]),
                          [st], [SG])
                kb.op("act", lambda h: h.activation(out=SG[:], in_=SG[:], func=AF.Exp, scale=-1.0), [SG], [SG])
                kb.op("dve", lambda h: h.tensor_scalar_add(out=SG[:], in0=SG[:], scalar1=1.0), [SG], [SG])
                kb.op("dve", lambda h: h.reciprocal(out=SG[:], in_=SG[:]), [SG], [SG])
            kb.barrier()
            vsp_all = [vsp, vsp.sub("one")]
            vwp_all = [vwp, vwp.sub("one")]
            bst = [kb.sb("bst%d" % i, [128, 512], F32, es) for i in range(2)]
            bcb = [kb.sb("bcb%d" % i, [128, 512], BF16, es) for i in range(2)]
            PT = [kb.sb("PT%d" % i, [128, 512], BF16, es) for i in range(4)]
            OC = kb.sb("OC", [128, 4, 4, 193], F32, es)
            OS = kb.sb("OS", [128, 4, 4, 129], F32, es)
            OW = kb.sb("OW", [128, 4, 4, 129], F32, es)
            rc4 = kb.sb("rc4", [128, 4], F32, es)
            imp = kb.sb("imp", [128, 64], F32, es)
            rep = kb.sb("rep", [128, 64], F32, es)
            m8a = kb.sb("m8a", [128, 8], F32, es)
            m8b = kb.sb("m8b", [128, 8], F32, es)
            selm = kb.sb("selm", [128, 64], F32, es)
            selb = kb.sb("selb", [128, 64], BF16, es)
            selT = kb.sb("selT", [64, 512], BF16, es)
            D12 = kb.sb("D12", [128, 4, 3], F32, es)
            CF = kb.sb("CF", [128, 12], F32, es)
            zt = [kb.sb("z3_%d" % i, [128, 512], F32, es) for i in range(2)]
            gz = kb.sb("gz", [128, 512], F32, es)
            yacc = kb.sb("yacc", [128, 128], F32, es)
            ybf = [kb.sb("ybf%d" % i, [128, 128], BF16, es) for i in range(2)]
            yst = [kb.sb("yst3_%d" % i, [128, 4, 512], BF16, es) for i in range(2)]
            sc = 0
            pc = 0
            for i in range(self.nsa_tiles):
                tcs = slice(i * 512, (i + 1) * 512)
                for h_ in range(4):
                    pts = []
                    for ct in range(2):
                        b1 = bst[sc % 2]
                        b2 = bcb[sc % 2]
                        pS = psf[sc % 2]
                        sc += 1
                        ptile = PT[pc % 4]
                        pc += 1
                        pts.append(ptile)
                        kb.dma("sp", b1[:], biasC_in[h_ * 2 + ct, :, tcs], [biasC_in], [b1], b1)
                        kb.op("pool", lambda h, b1=b1, b2=b2: h.tensor_copy(out=b2[:], in_=b1[:]), [b1], [b2])
                        kb.op("pe", lambda h, pS=pS, ct=ct, h_=h_, tcs=tcs: h.matmul(
                            pS[:], lhsT=kcn[:, ct * 128:(ct + 1) * 128], rhs=qn[h_][:, tcs],
                            start=True, stop=False), [kcn, qn[h_]], [pS])
                        kb.op("pe", lambda h, pS=pS, b2=b2: h.matmul(
                            pS[:], lhsT=ident_b[:], rhs=b2[:], start=False, stop=True), [ident_b, b2], [pS])
                        kb.op("act", lambda h, pS=pS, ptile=ptile: h.activation(
                            out=ptile[:], in_=pS[:], func=AF.Exp), [pS], [ptile])
                    for tt in range(4):
                        pO = psf[2 + tt]
                        for ct in range(2):
                            kb.op("pe", lambda h, pO=pO, ct=ct, tt=tt, p_=pts[ct]: h.matmul(
                                pO[:, 0:193], lhsT=p_[:, tt * 128:(tt + 1) * 128], rhs=vcp[:, ct, :],
                                start=(ct == 0), stop=(ct == 1)), [pts[ct]] + vcp_all, [pO])
                        self.evac(OC[:, h_, tt, :], pO[:, 0:193], [pO], [OC.sub((h_, tt))])
                pT = psb[0]
                for tt in range(4):
                    oc_r = [OC.sub((h_, tt)) for h_ in range(4)]
                    kt = slice((4 * i + tt) * 64, (4 * i + tt + 1) * 64)
                    kb.op("dve", lambda h, tt=tt: h.tensor_scalar_max(out=rc4[:], in0=OC[:, :, tt, 128],
                                                                      scalar1=1.0e-30), oc_r, [rc4])
                    kb.op("dve", lambda h: h.reciprocal(out=rc4[:], in_=rc4[:]), [rc4], [rc4])
                    for h_ in range(4):
                        if h_ == 0:
                            kb.op("dve", lambda h, tt=tt: h.tensor_scalar(
                                out=imp[:], in0=OC[:, 0, tt, 129:193], scalar1=rc4[:, 0:1], scalar2=None,
                                op0=ALU.mult), oc_r + [rc4], [imp])
                        else:
                            kb.op("dve", lambda h, tt=tt, h_=h_: h.scalar_tensor_tensor(
                                out=imp[:], in0=OC[:, h_, tt, 129:193], scalar=rc4[:, h_:h_ + 1], in1=imp[:],
                                op0=ALU.mult, op1=ALU.add), oc_r + [rc4, imp], [imp])
                    kb.op("dve", lambda h, kt=kt: h.tensor_tensor(out=imp[:], in0=imp[:], in1=keep[:, kt],
                                                                  op=ALU.mult), [imp, keep], [imp])
                    kb.op("dve", lambda h, kt=kt: h.tensor_tensor(out=imp[:], in0=imp[:], in1=addc[:, kt],
                                                                  op=ALU.add), [imp, addc], [imp])
                    kb.op("dve", lambda h: h.max(out=m8a[:], in_=imp[:]), [imp], [m8a])
                    kb.op("dve", lambda h: h.match_replace(out=rep[:], in_to_replace=m8a[:], in_values=imp[:],
                                                           imm_value=-3.0e38), [imp, m8a], [rep])
                    kb.op("dve", lambda h: h.max(out=m8b[:], in_=rep[:]), [rep], [m8b])
                    kb.op("dve", lambda h: h.tensor_scalar(out=selm[:], in0=imp[:], scalar1=m8b[:, 7:8],
                                                           scalar2=None, op0=ALU.is_ge), [imp, m8b], [selm])
                    kb.op("dve", lambda h: h.tensor_scalar(out=selb[:], in0=selm[:], scalar1=-NEG, scalar2=NEG,
                                                           op0=ALU.mult, op1=ALU.add), [selm], [selb])
                    kb.op("pe", lambda h, tt=tt, pT=pT: h.transpose(
                        out=pT[0:64, tt * 128:(tt + 1) * 128], in_=selb[:], identity=ident_b[:]),
                        [selb, ident_b], [pT])
                self.evac(selT[:], pT[0:64, 0:512], [pT], [selT])
                items = []
                for br in range(2):
                    j0 = 0 if br == 0 else max(0, 4 * i - 4)
                    for h_ in range(4):
                        for j in range(j0, 4 * i + 4):
                            items.append(dict(br=br, h=h_, j=j, j0=j0))

                def emit_qk(it, i=i, tcs=tcs):
                    nonlocal sc, pc
                    br, h_, j = it["br"], it["h"], it["j"]
                    kn = ksn if br == 0 else kwn
                    st_ = stS if br == 0 else stW
                    pS = psf[sc % 2]
                    sc += 1
                    ptile = PT[pc % 4]
                    pc += 1
                    it["pS"], it["pt"] = pS, ptile
                    delta = 512 * i - 128 * j
                    near = (br == 1) or (delta <= 128)
                    it["near"] = near
                    kb.op("pe", lambda h: h.matmul(
                        pS[:], lhsT=kn[:, j * 128:(j + 1) * 128], rhs=qn[h_][:, tcs],
                        start=True, stop=False), [kn, qn[h_]], [pS])
                    if br == 0:
                        kb.op("pe", lambda h: h.matmul(
                            pS[:], lhsT=Eb[0:64, j * 128:(j + 1) * 128], rhs=selT[0:64, :],
                            start=False, stop=(not near)), [Eb, selT], [pS])
                    if near:
                        off = delta + 384
                        kb.op("pe", lambda h: h.matmul(
                            pS[:], lhsT=ident_b[:], rhs=st_[:, h_, off:off + 512],
                            start=False, stop=True), [ident_b, st_], [pS])

                def emit_exp(it):
                    pS, ptile, h_ = it["pS"], it["pt"], it["h"]
                    if it["near"]:
                        kb.op("act", lambda h: h.activation(out=ptile[:], in_=pS[:], func=AF.Exp),
                              [pS], [ptile])
                    else:
                        kb.op("act", lambda h: h.activation(out=ptile[:], in_=pS[:], func=AF.Exp,
                                                            bias=b31[:, h_:h_ + 1]), [pS, b31], [ptile])

                def emit_pv(it, i=i):
                    br, h_, j, j0 = it["br"], it["h"], it["j"], it["j0"]
                    vp_ = vsp if br == 0 else vwp
                    vp_all = vsp_all if br == 0 else vwp_all
                    OB = OS if br == 0 else OW
                    ptile = it["pt"]
                    for tt in range(4):
                        if j > 4 * i + tt:
                            continue
                        pO = psf[2 + tt]
                        kb.op("pe", lambda h, pO=pO, tt=tt: h.matmul(
                            pO[:, 0:129], lhsT=ptile[:, tt * 128:(tt + 1) * 128], rhs=vp_[:, j, :],
                            start=(j == j0), stop=(j == 4 * i + tt)), [ptile] + vp_all, [pO])
                    if j == 4 * i + 3:
                        for tt in range(4):
                            self.evac(OB[:, h_, tt, :], psf[2 + tt][:, 0:129], [psf[2 + tt]],
                                      [OB.sub((h_, tt))])

                emit_qk(items[0])
                for k_, it in enumerate(items):
                    if k_ + 1 < len(items):
                        emit_qk(items[k_ + 1])
                    emit_exp(it)
                    emit_pv(it)
                ys = yst[i % 2]
                pY = psb[1]
                for tt in range(4):
                    n = 4 * i + tt
                    z_t = zt[n % 2]
                    kb.dma("sp", z_t[:], ZNS[n * 128:(n + 1) * 128, :], zns_all, [z_t], z_t)
                    srcs = [OC.sub((h_, tt)) for h_ in range(4)] + [OS.sub((h_, tt)) for h_ in range(4)] + \
                           [OW.sub((h_, tt)) for h_ in range(4)]
                    for bi, OB in enumerate((OC, OS, OW)):
                        kb.op("dve", lambda h, bi=bi, OB=OB, tt=tt: h.tensor_copy(
                            out=D12[:, :, bi], in_=OB[:, :, tt, 128]), srcs, [D12])
                    kb.op("dve", lambda h: h.tensor_scalar_max(out=D12[:], in0=D12[:], scalar1=1.0e-30),
                          [D12], [D12])
                    kb.op("dve", lambda h: h.reciprocal(out=D12[:], in_=D12[:]), [D12], [D12])
                    kb.op("dve", lambda h, n=n: h.tensor_tensor(
                        out=CF[:], in0=D12[:].rearrange("p a b -> p (a b)"), in1=SG[:, n, :], op=ALU.mult),
                        [D12, SG], [CF])
                    kb.op("act", lambda h, z_t=z_t: h.activation(out=gz[:], in_=z_t[:], func=AF.Exp, scale=-1.0),
                          [z_t], [gz])
                    kb.op("pool", lambda h: h.tensor_scalar_add(out=gz[:], in0=gz[:], scalar1=1.0), [gz], [gz])
                    kb.op("dve", lambda h: h.reciprocal(out=gz[:], in_=gz[:]), [gz], [gz])
                    kb.op("pool", lambda h, z_t=z_t: h.tensor_tensor(out=gz[:], in0=gz[:], in1=z_t[:], op=ALU.mult),
                          [gz, z_t], [gz])
                    for h_ in range(4):
                        yb_ = ybf[h_ % 2]
                        kb.op("dve", lambda h, h_=h_, tt=tt: h.tensor_scalar(
                            out=yacc[:], in0=OC[:, h_, tt, 0:128], scalar1=CF[:, 3 * h_:3 * h_ + 1], scalar2=None,
                            op0=ALU.mult), srcs + [CF], [yacc])
                        kb.op("dve", lambda h, h_=h_, tt=tt: h.scalar_tensor_tensor(
                            out=yacc[:], in0=OS[:, h_, tt, 0:128], scalar=CF[:, 3 * h_ + 1:3 * h_ + 2], in1=yacc[:],
                            op0=ALU.mult, op1=ALU.add), srcs + [CF, yacc], [yacc])
                        kb.op("dve", lambda h, h_=h_, tt=tt: h.scalar_tensor_tensor(
                            out=yacc[:], in0=OW[:, h_, tt, 0:128], scalar=CF[:, 3 * h_ + 2:3 * h_ + 3], in1=yacc[:],
                            op0=ALU.mult, op1=ALU.add), srcs + [CF, yacc], [yacc])
                        kb.op("dve", lambda h, h_=h_, yb_=yb_: h.tensor_tensor(
                            out=yb_[:], in0=yacc[:], in1=gz[:, h_ * 128:(h_ + 1) * 128], op=ALU.mult),
                            [yacc, gz], [yb_])
                        kb.op("pe", lambda h, h_=h_, yb_=yb_, pY=pY: h.transpose(
                            out=pY[:, h_ * 128:(h_ + 1) * 128], in_=yb_[:], identity=ident_b[:]),
                            [yb_, ident_b], [pY])
                    self.evac(ys[:, :, tt * 128:(tt + 1) * 128],
                              pY[:, 0:512].rearrange("p (j t) -> p j t", j=4), [pY], [ys])
                kb.dma("pool", YT[i, 512:1024, :].rearrange("(j p) t -> p j t", p=128), ys[:],
                       [ys], [YT.sub(("ns", i))], ys)


def make_core_inputs(inputs, c):
    g = c % 4
    b = c // 4
    w_in = inputs["w_in"][0]
    cols = []
    cols += list(range(256 * g, 256 * g + 256))
    cols += list(range(1024 + 256 * g, 1024 + 256 * g + 256))
    cols += list(range(8208 + 512 * g, 8208 + 512 * g + 512))
    cols += list(range(10256 + 128 * g, 10256 + 128 * g + 128))
    cols += list(range(10768 + 128 * g, 10768 + 128 * g + 128))
    cols += list(range(11280 + 128 * g, 11280 + 128 * g + 128))
    cols += list(range(12304 + 128 * g, 12304 + 128 * g + 128))
    cols += list(range(2048 + 512 * g, 2048 + 512 * g + 512))
    cols += list(range(4096 + 512 * g, 4096 + 512 * g + 512))
    cols += list(range(6144 + 512 * g, 6144 + 512 * g + 512))
    cols += list(range(11792 + 128 * g, 11792 + 128 * g + 128))
    cols += list(range(12816 + 128 * g, 12816 + 128 * g + 128))
    cols += [8192 + 2 * g, 8192 + 2 * g + 1, 8200 + 2 * g, 8200 + 2 * g + 1]
    cols += list(range(13328 + 12 * g, 13328 + 12 * g + 12))
    cols += list(range(13376 + 512 * g, 13376 + 512 * g + 512))
    assert len(cols) == WCOLS
    m = {
        "x": np.ascontiguousarray(inputs["x"][b]),
        "wcore": np.ascontiguousarray(w_in[:, cols]),
        "normw": np.ascontiguousarray(np.broadcast_to(inputs["norm_w"][0][None, :], (128, D))),
        "ident": np.eye(128, dtype=np.float32),
    }
    cw = inputs["ml_conv_w"][0]
    convw = np.zeros((128, 16), np.float32)
    for ci in range(4):
        base = (0 if ci < 2 else 1024) + 256 * g + 128 * (ci % 2)
        convw[:, ci * 4:(ci + 1) * 4] = cw[:, base:base + 128].T
    m["convw"] = convw
    gbv = np.array([inputs["ml_i_bias"][0][2 * g], inputs["ml_i_bias"][0][2 * g + 1],
                    inputs["ml_f_bias"][0][2 * g], inputs["ml_f_bias"][0][2 * g + 1]], np.float32)
    m["gbias"] = np.ascontiguousarray(np.broadcast_to(gbv[None, :], (128, 4)))
    hn = inputs["ml_head_norm_w"][0][2 * g:2 * g + 2].reshape(1, 512)
    m["hnw"] = np.ascontiguousarray(np.broadcast_to(hn, (128, 512)))
    m["tri"] = np.triu(np.ones((128, 128), np.float32))
    r = g
    qs = slice(1024 * r, 1024 * (r + 1))
    rows = []
    for gg in range(4):
        rows += list(range(512 * gg, 512 * gg + 512))
        rows += list(range(2048 + 512 * gg, 2048 + 512 * gg + 512))
    m["wq"] = np.ascontiguousarray(inputs["w_out"][0][rows][:, qs])
    m["gq"] = np.ascontiguousarray(inputs["ple_gate"][0][:, qs])
    m["pq"] = np.ascontiguousarray(inputs["ple_proj"][0][:, qs])
    m["pT"] = np.ascontiguousarray(inputs["p"][0, b].T)
    m["xq"] = np.ascontiguousarray(inputs["x"][b][:, qs])
    rb = inputs["rel_bias"][:, 4 * g:4 * g + 4]
    m["qnw"] = np.ascontiguousarray(inputs["nsa_q_norm_w"][0].reshape(128, 1))
    m["knw"] = np.ascontiguousarray(inputs["nsa_k_norm_w"][0].T)
    m["peK"] = np.ascontiguousarray(inputs["cmp_pe_k"][0].T)
    m["peV"] = np.ascontiguousarray(inputs["cmp_pe_v"][0].T)
    m["w1k"] = inputs["cmp_k_w1"][0]
    m["w2k"] = inputs["cmp_k_w2"][0]
    m["w1v"] = inputs["cmp_v_w1"][0]
    m["w2v"] = inputs["cmp_v_w2"][0]
    tab = _nsa_tables()
    negf = np.float32(NEG)
    bc = np.where(tab["c_valid"][None], rb.T[:, tab["c_bucket"]], negf).astype(np.float32)
    m["biasC"] = np.ascontiguousarray(bc.reshape(8, 128, S))
    m["stripS"] = np.ascontiguousarray(
        np.where(tab["s_valid"][None], rb.T[:, tab["s_bucket"]], negf).astype(np.float32))
    m["stripW"] = np.ascontiguousarray(
        np.where(tab["w_valid"][None], rb.T[:, tab["w_bucket"]], negf).astype(np.float32))
    m["b31"] = np.ascontiguousarray(np.broadcast_to(rb[31][None, :], (128, 4)))
    m["ov"] = tab["ov"]
    m["Esel"] = tab["E"]
    m["keep"] = tab["keep"]
    m["addc"] = tab["add"]
    return m


_TAB = {}


def _bucket(dist):
    import math
    n = np.maximum(dist, 0)
    nf = np.maximum(n, 1).astype(np.float32)
    large = 16 + (np.log(nf / np.float32(16)) / np.float32(math.log(128 / 16))
                  * np.float32(16)).astype(np.int32)
    large = np.minimum(large, 31)
    return np.where(n < 16, n, large)


def _nsa_tables():
    if _TAB:
        return _TAB
    t = np.arange(S)
    c = np.arange(256)
    dist = t[None, :] - (16 * c[:, None] + 31)
    _TAB["c_valid"] = (dist >= 0) & (c[:, None] <= 254)
    _TAB["c_bucket"] = _bucket(dist)
    sl = np.arange(128)[:, None]
    u = np.arange(1024)[None, :]
    d = u - 384 - sl
    _TAB["s_valid"] = d >= 0
    _TAB["s_bucket"] = _bucket(d)
    u = np.arange(1408)[None, :]
    d = u - 384 - sl
    _TAB["w_valid"] = (d >= 0) & (d < 512)
    _TAB["w_bucket"] = _bucket(d)
    n = np.arange(64)
    ov = ((16 * c[:, None] < 64 * n[None, :] + 64) & (16 * c[:, None] + 32 > 64 * n[None, :])
          & (c[:, None] <= 254)).astype(np.float32)
    _TAB["ov"] = np.ascontiguousarray(ov)
    _TAB["E"] = np.ascontiguousarray((t[None, :] // 64 == n[:, None]).astype(np.float32))
    tt = np.arange(NT)[None, :, None]
    p = np.arange(128)[:, None, None]
    cur = (128 * tt + p) // 64
    nn = n[None, None, :]
    forced = (nn == 0) | (nn == cur) | (nn == cur - 1)
    future = nn > cur
    keep = (~(forced | future)).astype(np.float32)
    add = np.where(forced, 1.0e4 + nn, np.where(future, -1.0e30, 0.0)).astype(np.float32)
    _TAB["keep"] = np.ascontiguousarray(keep.reshape(128, NT * 64))
    _TAB["add"] = np.ascontiguousarray(add.reshape(128, NT * 64))
    return _TAB


_CACHE = {}


def kernel(**inputs):
    inputs = {k: np.asarray(v) for k, v in inputs.items()}
    if "nc" not in _CACHE:
        _CACHE["nc"] = Prog(debug=False, have_nsa=True).build()
    nc = _CACHE["nc"]
    in_maps = [make_core_inputs(inputs, c) for c in range(8)]
    res = run_bass_kernel_spmd(nc, in_maps, core_ids=list(range(8)))
    out = np.zeros((2, S, D), np.float32)
    for c in range(8):
        out[c // 4, :, 1024 * (c % 4):1024 * (c % 4 + 1)] = res.results[c]["out"]
    return out
```

```python
import numpy as np
from contextlib import ExitStack
import concourse.bass as bass
import concourse.mybir as mybir
from concourse.bass_utils import run_bass_kernel_spmd

F32 = mybir.dt.float32
BF16 = mybir.dt.bfloat16
AF = mybir.ActivationFunctionType
ALU = mybir.AluOpType
AX = mybir.AxisListType

D = 4096
S = 4096
NT = S // 128
WCOLS = 3856
EPS = 1e-6
NEG = -30000.0


class TL:
    def __init__(self, sem, inc):
        self.sem = sem
        self.inc = inc
        self.count = 0


class Res:
    def __init__(self, name):
        self.name = name
        self.w = None
        self.r = {}


class Buf(Res):
    def __init__(self, kb, name, t):
        super().__init__(name)
        self.kb = kb
        self.t = t
        self._chan = None
        self.subs = {}

    def __getitem__(self, k):
        return self.t[k]

    def chan(self):
        if self._chan is None:
            self._chan = TL(self.kb.new_sem("c_" + self.name), 16)
            self.kb.chans.append(self._chan)
        return self._chan

    def sub(self, key):
        if key not in self.subs:
            b = Buf(self.kb, "%s_%s" % (self.name, key), self.t)
            self.subs[key] = b
        return self.subs[key]


class Eng:
    def __init__(self, name, tl):
        self.name = name
        self.tl = tl
        self.seen = {}
        self.ops = []


class KB:
    def __init__(self, nc):
        self.nc = nc
        self.es = ExitStack()
        self.nsem = 0
        self.engs = {}
        for n in ("pe", "dve", "act", "pool", "sp"):
            self.engs[n] = Eng(n, TL(self.new_sem("e_" + n), 1))
        self.nbuf = 0
        self.mute = False
        self.chans = []

    def new_sem(self, name):
        self.nsem += 1
        return self.es.enter_context(self.nc.semaphore(name))

    def sb(self, name, shape, dt, es=None):
        t = (es or self.es).enter_context(self.nc.sbuf_tensor(name, list(shape), dt))
        return Buf(self, name, t)

    def ps(self, name, shape, dt=F32):
        t = self.es.enter_context(self.nc.psum_tensor(name, list(shape), dt))
        return Buf(self, name, t)

    def dram(self, name, shape, dt, kind="Internal"):
        t = self.nc.dram_tensor(name, list(shape), dt, kind=kind)
        return Buf(self, name, t.ap())

    def _deps(self, E, reads, writes, skip_self):
        deps = {}

        def need(tlv):
            if tlv is None:
                return
            tl, v = tlv
            if skip_self and tl is E.tl:
                return
            if E.seen.get(tl, 0) >= v:
                return
            if deps.get(tl, 0) < v:
                deps[tl] = v

        for r in reads:
            need(r.w)
        for w in writes:
            need(w.w)
            for tl, v in w.r.items():
                need((tl, v))
        return deps, need

    def op(self, eng, fn, reads=(), writes=()):
        if self.mute:
            return
        E = self.engs[eng]
        deps, _ = self._deps(E, reads, writes, skip_self=(eng == "pe"))
        for tl, v in deps.items():
            E.seen[tl] = v
        E.tl.count += 1
        val = E.tl.count
        E.ops.append(([(tl.sem, v) for tl, v in deps.items()], fn, (E.tl.sem, 1)))
        for r in reads:
            r.r[E.tl] = val
        for w in writes:
            w.w = (E.tl, val)
            w.r = {}

    def dma(self, q, out_ap, in_ap, reads, writes, owner, group=False):
        if self.mute:
            return
        E = self.engs[q]
        ch = owner.chan()
        deps, need = self._deps(E, reads, writes, skip_self=False)
        if (not group) and ch.count > 0:
            need((ch, ch.count))
        for tl, v in deps.items():
            E.seen[tl] = v
        ch.count += 16
        val = ch.count
        E.ops.append(([(tl.sem, v) for tl, v in deps.items()],
                      lambda h: h.dma_start(out=out_ap, in_=in_ap), (ch.sem, 16)))
        for r in reads:
            r.r[ch] = val
        for w in writes:
            w.w = (ch, val)
            w.r = {}

    def custom(self, eng, fn, reads, writes, tl):
        E = self.engs[eng]
        deps, need = self._deps(E, reads, writes, skip_self=False)
        for t2, v in deps.items():
            E.seen[t2] = v
        tl.count += tl.inc
        val = tl.count
        E.ops.append(([(t2.sem, v) for t2, v in deps.items()], fn, (tl.sem, tl.inc)))
        for r in reads:
            r.r[tl] = val
        for w in writes:
            w.w = (tl, val)
            w.r = {}

    def barrier(self):
        tls = [e.tl for e in self.engs.values() if e.tl.count] + \
              [t for t in self.chans if t.count]
        for E in self.engs.values():
            waits = []
            for tl in tls:
                if tl is E.tl:
                    continue
                if E.seen.get(tl, 0) < tl.count:
                    E.seen[tl] = tl.count
                    waits.append((tl.sem, tl.count))
            if waits:
                E.ops.append((waits, None, None))

    def finish(self, final_res):
        E = self.engs["sp"]
        waits = {}
        for r in final_res:
            if r.w is not None:
                tl, v = r.w
                waits[tl] = max(waits.get(tl, 0), v)
        for e in self.engs.values():
            if e.tl.count:
                waits[e.tl] = e.tl.count
        E.ops.append(([(tl.sem, v) for tl, v in waits.items()], None, None))
        with self.nc.Block() as block:
            def runner(name):
                def f(h):
                    for waits_, fn, inc in self.engs[name].ops:
                        for sem, v in waits_:
                            h.wait_ge(sem, v)
                        if fn is not None:
                            ins = fn(h)
                            ins.then_inc(inc[0], inc[1])
                return f
            block.tensor(runner("pe"))
            block.vector(runner("dve"))
            block.scalar(runner("act"))
            block.gpsimd(runner("pool"))
            block.sync(runner("sp"))
        self.es.close()


class Prog:
    def __init__(self, debug=False, stop_after=99, skip1=False, p2stop=99, nchunks=NT, cstop=99, have_nsa=False, nsa_tiles=8):
        self.have_nsa = have_nsa
        self.nsa_tiles = nsa_tiles
        self.nchunks = nchunks
        self.cstop = cstop
        self.skip1 = skip1
        self.p2stop = p2stop
        self.debug = debug
        self.stop_after = stop_after
        nc = bass.Bass("TRN2", target_bir_lowering=False)
        self.nc = nc
        kb = KB(nc)
        self.kb = kb
        self.inp = {}
        self.outs = []
        self.rr = 0

    def din(self, name, shape, dt=F32):
        b = self.kb.dram(name, shape, dt, kind="ExternalInput")
        self.inp[name] = b
        return b

    def dout(self, name, shape, dt=F32):
        b = self.kb.dram(name, shape, dt, kind="ExternalOutput")
        self.outs.append(b)
        return b

    def evac(self, out_ap, in_ap, reads, writes, scale=None):
        kb = self.kb
        self.rr += 1
        if self.rr % 2 == 0:
            kb.op("act", lambda h: h.activation(out=out_ap, in_=in_ap, func=AF.Copy),
                  reads, writes)
        else:
            kb.op("dve", lambda h: h.tensor_copy(out=out_ap, in_=in_ap), reads, writes)

    def cast(self, out_ap, in_ap, reads, writes):
        kb = self.kb
        self.rr += 1
        m = self.rr % 3
        if m == 0:
            kb.op("act", lambda h: h.activation(out=out_ap, in_=in_ap, func=AF.Copy),
                  reads, writes)
        elif m == 1:
            kb.op("dve", lambda h: h.tensor_copy(out=out_ap, in_=in_ap), reads, writes)
        else:
            kb.op("pool", lambda h: h.tensor_copy(out=out_ap, in_=in_ap), reads, writes)

    def build(self):
        kb = self.kb
        nc = self.nc
        dbg = self.debug
        x = self.din("x", [S, D])
        wcore = self.din("wcore", [D, WCOLS])
        normw = self.din("normw", [128, D])
        ident_in = self.din("ident", [128, 128])
        UT = kb.dram("UT", [12 * 128, S], F32)
        VML = kb.dram("VML", [S, 512], F32)
        OML = kb.dram("OML", [S, 512], F32)
        ZML = kb.dram("ZML", [S, 512], F32)
        MISC = kb.dram("MISC", [S, 272], F32)
        ZNS = kb.dram("ZNS", [S, 512], F32)
        WBF = kb.dram("WBF", [128, 32, WCOLS], BF16)
        tm_dst = [VML, OML, ZML, MISC, ZNS]
        tm_cols = [(1536, 512), (2048, 512), (2560, 512), (3072, 272), (3344, 512)]

        ident_f = kb.sb("ident_f", [128, 128], F32)
        ident_b = kb.sb("ident_b", [128, 128], BF16)
        kb.dma("sp", ident_f[:], ident_in[:], [ident_in], [ident_f], ident_f)
        kb.op("dve", lambda h: h.tensor_copy(out=ident_b[:], in_=ident_f[:]), [ident_f], [ident_b])

        psf = [kb.ps("psf%d" % i, [128, 512], F32) for i in range(6)]
        psb = [kb.ps("psb%d" % i, [128, 1024], BF16) for i in range(2)]

        with ExitStack() as es1:
            nw = kb.sb("nw", [128, D], F32, es1)
            kb.dma("sp", nw[:], normw[:], [normw], [nw], nw)
            hT = kb.sb("hT", [128, 32, 1024], BF16, es1)
            xt = [kb.sb("xt%d" % i, [128, D], F32, es1) for i in range(2)]
            xs = [kb.sb("xs%d" % i, [128, D], BF16, es1) for i in range(1)]
            ss = [kb.sb("ss%d" % i, [128, 1], F32, es1) for i in range(2)]
            rstd = [kb.sb("rstd%d" % i, [128, 1], F32, es1) for i in range(2)]
            wst = [kb.sb("wst%d" % i, [128, 2, 512], F32, es1) for i in range(3)]
            wbf = [kb.sb("wbf%d" % i, [128, 32, 512], BF16, es1) for i in range(2)]
            ost = [kb.sb("ost%d" % i, [128, 512], F32, es1) for i in range(3)]
            wcv = wcore.t.rearrange("(k p) c -> p k c", p=128)
            xtile = 0
            wld = 0
            ostc = 0
            psc = 0
            for tt in range(0 if self.skip1 else 4):
                for i8 in range(8):
                    ti = tt * 8 + i8
                    xb = xt[xtile % 2]
                    xsb = xs[0]
                    ssb = ss[xtile % 2]
                    rsb = rstd[xtile % 2]
                    xtile += 1
                    kb.dma("sp", xb[:], x[ti * 128:(ti + 1) * 128, :], [x], [xb], xb)
                    kb.op("act", lambda h, xb=xb, ssb=ssb, xsb=xsb: h.activation(
                        out=xsb[:], in_=xb[:], func=AF.Square, accum_out=ssb[:]),
                        [xb], [xsb, ssb])
                    kb.op("dve", lambda h, ssb=ssb, rsb=rsb: h.tensor_scalar(
                        out=rsb[:], in0=ssb[:], scalar1=1.0 / D, scalar2=EPS,
                        op0=ALU.mult, op1=ALU.add), [ssb], [rsb])
                    kb.op("act", lambda h, rsb=rsb: h.activation(
                        out=rsb[:], in_=rsb[:], func=AF.Sqrt), [rsb], [rsb])
                    kb.op("dve", lambda h, rsb=rsb: h.reciprocal(out=rsb[:], in_=rsb[:]),
                          [rsb], [rsb])
                    kb.op("dve", lambda h, xb=xb, rsb=rsb, xsb=xsb: h.scalar_tensor_tensor(
                        out=xsb[:], in0=xb[:], scalar=rsb[:], in1=nw[:],
                        op0=ALU.mult, op1=ALU.mult), [xb, rsb, nw], [xsb])
                    for j4 in range(4):
                        pb = psb[j4 % 2]
                        for k8 in range(8):
                            kk = j4 * 8 + k8
                            kb.op("pe", lambda h, pb=pb, k8=k8, kk=kk, xsb=xsb: h.transpose(
                                out=pb[:, k8 * 128:(k8 + 1) * 128],
                                in_=xsb[:, kk * 128:(kk + 1) * 128], identity=ident_b[:]),
                                [xsb, ident_b], [pb])
                        self.evac(hT[:, j4 * 8:(j4 + 1) * 8, i8 * 128:(i8 + 1) * 128],
                                  pb[:].rearrange("p (k t) -> p k t", k=8),
                                  [pb], [hT.sub(i8)])
                hT_all = [hT.sub(i) for i in range(8)]

                def load_w(c0, ncol, tt=tt):
                    nonlocal wld
                    wb = wbf[wld % 2]
                    wld += 1
                    if tt > 0:
                        for q in range(4):
                            kb.dma("sp", wb[:, q * 8:(q + 1) * 8, :ncol],
                                   WBF[:, q * 8:(q + 1) * 8, c0:c0 + ncol],
                                   [WBF.sub(c0)], [wb], wb, group=(q > 0))
                        return wb
                    for kq in range(16):
                        st = wst[kq % 3]
                        kb.dma("sp", st[:, :, :ncol], wcv[:, kq * 2:(kq + 1) * 2, c0:c0 + ncol],
                               [wcore], [st], st)
                        self.cast(wb[:, kq * 2:(kq + 1) * 2, :ncol], st[:, :, :ncol], [st], [wb])
                    for q in range(4):
                        kb.dma("sp", WBF[:, q * 8:(q + 1) * 8, c0:c0 + ncol], wb[:, q * 8:(q + 1) * 8, :ncol],
                               [wb], [WBF.sub(c0)], wb, group=(q > 0))
                    return wb

                for cg in range(3):
                    wb = load_w(cg * 512, 512)
                    for c4 in range(4):
                        ct = cg * 4 + c4
                        for th in range(2):
                            pt = psf[psc % 6]
                            psc += 1
                            for k in range(32):
                                kb.op("pe", lambda h, pt=pt, wb=wb, c4=c4, k=k, th=th: h.matmul(
                                    pt[:], lhsT=wb[:, k, c4 * 128:(c4 + 1) * 128],
                                    rhs=hT[:, k, th * 512:(th + 1) * 512],
                                    start=(k == 0), stop=(k == 31)),
                                    [wb] + hT_all, [pt])
                            ob = ost[ostc % 3]
                            ostc += 1
                            self.evac(ob[:], pt[:], [pt], [ob])
                            t0 = tt * 1024 + th * 512
                            kb.dma("pool", UT[ct * 128:(ct + 1) * 128, t0:t0 + 512], ob[:],
                                   [ob], [UT.sub((ct, tt))], ob)
                for gi, (c0, ncol) in enumerate(tm_cols):
                    wb = load_w(c0, ncol)
                    for i8 in range(8):
                        pt = psf[psc % 6]
                        psc += 1
                        for k in range(32):
                            kb.op("pe", lambda h, pt=pt, wb=wb, k=k, i8=i8, ncol=ncol: h.matmul(
                                pt[:, :ncol], lhsT=hT[:, k, i8 * 128:(i8 + 1) * 128],
                                rhs=wb[:, k, :ncol], start=(k == 0), stop=(k == 31)),
                                [wb] + hT_all, [pt])
                        ob = ost[ostc % 3]
                        ostc += 1
                        self.evac(ob[:, :ncol], pt[:, :ncol], [pt], [ob])
                        t0 = tt * 1024 + i8 * 128
                        kb.dma("pool", tm_dst[gi][t0:t0 + 128, :], ob[:, :ncol],
                               [ob], [tm_dst[gi].sub(tt)], ob)

        kb.barrier()
        YT = kb.dram("YT", [8, 1024, 512], BF16)
        self.YT = YT
        if self.stop_after >= 2:
            self.phase2(UT, VML, OML, ZML, MISC, YT, psf, psb, ident_f, ident_b)

        if dbg and self.stop_after < 2:
            with ExitStack() as esd:
                db = kb.sb("dbgbuf", [128, 4096], F32, esd)
                o_ut = self.dout("o_ut", [12 * 128, S])
                ut_all = [UT.sub((ct, tt)) for ct in range(12) for tt in range(4)]
                for ct in range(12):
                    kb.dma("sp", db[:], UT[ct * 128:(ct + 1) * 128, :], ut_all, [db], db)
                    kb.dma("sp", o_ut[ct * 128:(ct + 1) * 128, :], db[:], [db], [o_ut], db)
                for nm, src, ncol in (("o_vml", VML, 512), ("o_misc", MISC, 272)):
                    o = self.dout(nm, [S, ncol])
                    srcs = [src.sub(tt) for tt in range(4)]
                    for i in range(NT):
                        kb.dma("sp", db[:, :ncol], src[i * 128:(i + 1) * 128, :], srcs, [db], db)
                        kb.dma("sp", o[i * 128:(i + 1) * 128, :], db[:, :ncol], [db], [o], db)

        kb.mute = False
        kb.barrier()
        if self.have_nsa and self.stop_after >= 3:
            self.phase3(UT, MISC, ZNS, YT, psf, psb, ident_b)
            kb.barrier()
        if self.stop_after >= 4:
            self.phase4(YT, psf, psb, ident_b)
            kb.barrier()
        if dbg and self.stop_after >= 2:
            with ExitStack() as esd:
                db2 = kb.sb("dbg2", [128, 8, 2048], BF16, esd)
                o_yt = self.dout("o_yt", [1024, S], BF16)
                yt_all = [YT.sub((k_, t_)) for k_ in ("ml", "ns") for t_ in range(8)]
                for tg in range(8):
                    kb.dma("sp", db2[:, :, 0:512], YT[tg].rearrange("(j p) t -> p j t", p=128),
                           yt_all, [db2], db2)
                    kb.dma("sp", o_yt[:, tg * 512:(tg + 1) * 512].rearrange("(j p) t -> p j t", p=128),
                           db2[:, :, 0:512], [db2], [o_yt], db2)
        kb.finish(self.outs)
        return nc


    def phase2(self, UT, VML, OML, ZML, MISC, YT, psf, psb, ident_f, ident_b):
        kb = self.kb
        LNC = float(np.log(128.0 ** -0.5))
        convw_in = self.din("convw", [128, 16])
        gb_in = self.din("gbias", [128, 4])
        hnw_in = self.din("hnw", [128, 512])
        tri_in = self.din("tri", [128, 128])
        ut_all = [UT.sub((ct, tt)) for ct in range(12) for tt in range(4)]
        misc_all = [MISC.sub(tt) for tt in range(4)]
        vml_all = [VML.sub(tt) for tt in range(4)]
        oml_all = [OML.sub(tt) for tt in range(4)]
        zml_all = [ZML.sub(tt) for tt in range(4)]
        with ExitStack() as es:
            convw = kb.sb("convw_s", [128, 16], F32, es)
            gb = kb.sb("gb_s", [128, 4], F32, es)
            hnw = kb.sb("hnw_s", [128, 512], F32, es)
            tri = kb.sb("tri_s", [128, 128], F32, es)
            ones = kb.sb("ones_s", [128, 128], F32, es)
            kb.dma("sp", convw[:], convw_in[:], [convw_in], [convw], convw)
            kb.dma("sp", gb[:], gb_in[:], [gb_in], [gb], gb)
            kb.dma("sp", hnw[:], hnw_in[:], [hnw_in], [hnw], hnw)
            kb.dma("sp", tri[:], tri_in[:], [tri_in], [tri], tri)
            kb.op("pool", lambda h: h.memset(ones[:], 1.0), [], [ones])
            G = kb.sb("G_s", [128, NT, 4], F32, es)
            for n in range(NT):
                kb.dma("sp", G[:, n, :], MISC[n * 128:(n + 1) * 128, 256:260],
                       misc_all, [G], G, group=True)
            LI = kb.sb("LI_s", [128, NT, 2], F32, es)
            LF = kb.sb("LF_s", [128, NT, 2], F32, es)
            for j in range(2):
                kb.op("dve", lambda h, j=j: h.tensor_scalar(
                    out=LI[:, :, j], in0=G[:, :, j], scalar1=gb[:, j:j + 1], scalar2=LNC,
                    op0=ALU.add, op1=ALU.add), [G, gb], [LI])
                kb.op("dve", lambda h, j=j: h.tensor_scalar(
                    out=LF[:, :, j], in0=G[:, :, 2 + j], scalar1=gb[:, 2 + j:3 + j], scalar2=None,
                    op0=ALU.add), [G, gb], [LF])
            kb.op("act", lambda h: h.activation(out=LF[:], in_=LF[:], func=AF.Exp, scale=-1.0),
                  [LF], [LF])
            kb.op("dve", lambda h: h.tensor_scalar_add(out=LF[:], in0=LF[:], scalar1=1.0), [LF], [LF])
            kb.op("act", lambda h: h.activation(out=LF[:], in_=LF[:], func=AF.Ln), [LF], [LF])
            kb.op("dve", lambda h: h.tensor_scalar_mul(out=LF[:], in0=LF[:], scalar1=-1.0), [LF], [LF])

            if self.p2stop < 1:
                return
            LFHL = kb.sb("LFHL_s", [128, NT, 2, 2], BF16, es)
            LFH32 = kb.sb("LFH32_s", [128, NT, 2], F32, es)
            LFL32 = kb.sb("LFL32_s", [128, NT, 2], F32, es)
            tri_b = kb.sb("tri_b", [128, 128], BF16, es)
            ones_b = kb.sb("ones_b", [128, 128], BF16, es)
            kb.op("dve", lambda h: h.tensor_copy(out=tri_b[:], in_=tri[:]), [tri], [tri_b])
            kb.op("dve", lambda h: h.tensor_copy(out=ones_b[:], in_=ones[:]), [ones], [ones_b])
            kb.op("dve", lambda h: h.tensor_copy(out=LFHL[:, :, :, 0], in_=LF[:]), [LF], [LFHL])
            kb.op("dve", lambda h: h.tensor_copy(out=LFH32[:], in_=LFHL[:, :, :, 0]), [LFHL], [LFH32])
            kb.op("dve", lambda h: h.tensor_tensor(out=LFL32[:], in0=LF[:], in1=LFH32[:],
                                                   op=ALU.subtract), [LF, LFH32], [LFL32])
            kb.op("dve", lambda h: h.tensor_copy(out=LFHL[:, :, :, 1], in_=LFL32[:]), [LFL32], [LFHL])
            P = kb.sb("P_s", [128, S + 4], F32, es)
            acc = kb.sb("acc_s", [128, S], F32, es)
            qc = [kb.sb("qc%d" % h_, [128, S], F32, es) for h_ in range(2)]
            kT = [kb.sb("kT%d" % h_, [128, S], BF16, es) for h_ in range(2)]
            kb.op("pool", lambda h: h.memset(P[:, 0:4], 0.0), [], [P.sub("pad")])
            for ci in range(4):
                kb.dma("sp", P[:, 4:4 + S], UT[ci * 128:(ci + 1) * 128, :], ut_all, [P], P)
                for j in range(4):
                    if j == 0:
                        kb.op("dve", lambda h, ci=ci: h.tensor_scalar(
                            out=acc[:], in0=P[:, 1:1 + S], scalar1=convw[:, ci * 4:ci * 4 + 1],
                            scalar2=None, op0=ALU.mult), [P, P.sub("pad"), convw], [acc])
                    else:
                        kb.op("dve", lambda h, ci=ci, j=j: h.scalar_tensor_tensor(
                            out=acc[:], in0=P[:, 1 + j:1 + j + S],
                            scalar=convw[:, ci * 4 + j:ci * 4 + j + 1], in1=acc[:],
                            op0=ALU.mult, op1=ALU.add), [P, P.sub("pad"), convw, acc], [acc])
                dst = qc[ci] if ci < 2 else kT[ci - 2]
                kb.op("act", lambda h: h.activation(out=P[:, 4:4 + S], in_=acc[:], func=AF.Exp,
                                                    scale=-1.0), [acc], [P])
                kb.op("dve", lambda h: h.tensor_scalar_add(out=P[:, 4:4 + S], in0=P[:, 4:4 + S],
                                                           scalar1=1.0), [P], [P])
                kb.op("dve", lambda h: h.reciprocal(out=P[:, 4:4 + S], in_=P[:, 4:4 + S]), [P], [P])
                kb.op("dve", lambda h, dst=dst: h.tensor_tensor(
                    out=dst[:], in0=acc[:], in1=P[:, 4:4 + S], op=ALU.mult), [acc, P], [dst])

            if self.p2stop < 2:
                return
            vst = kb.sb("vst_s", [128, 512], F32, es)
            vp = [kb.sb("vp%d" % h_, [128, NT, 257], BF16, es) for h_ in range(2)]
            for h_ in range(2):
                kb.op("pool", lambda h, h_=h_: h.memset(vp[h_][:, :, 256:257], 1.0), [], [vp[h_].sub("one")])
            for n in range(NT):
                kb.dma("sp", vst[:], VML[n * 128:(n + 1) * 128, :], vml_all, [vst], vst)
                for h_ in range(2):
                    self.cast(vp[h_][:, n, 0:256], vst[:, h_ * 256:(h_ + 1) * 256], [vst], [vp[h_]])

            if self.p2stop < 3:
                return
            C = [kb.sb("C%d" % h_, [128, 257], F32, es) for h_ in range(2)]
            Cb = [kb.sb("Cb%d" % h_, [128, 257], BF16, es) for h_ in range(2)]
            for h_ in range(2):
                kb.op("pool", lambda h, h_=h_: h.memset(C[h_][:], 0.0), [], [C[h_]])
                kb.op("pool", lambda h, h_=h_: h.memset(Cb[h_][:], 0.0), [], [Cb[h_]])
            LFB = [kb.sb("LFB%d" % i, [128, 128], BF16, es) for i in range(2)]
            LFBl = [kb.sb("LFBl%d" % i, [128, 128], BF16, es) for i in range(2)]
            eBr = [kb.sb("eBr%d" % i, [128, 128], F32, es) for i in range(2)]
            eBL = [kb.sb("eBL%d" % i, [128, 1], F32, es) for i in range(2)]
            E2 = [kb.sb("E2_%d" % i, [128, 128], F32, es) for i in range(2)]
            hraw = [kb.sb("hraw%d" % i, [128, 257], F32, es) for i in range(2)]
            wk = [kb.sb("wk%d" % i, [128, 1], F32, es) for i in range(2)]
            qs = [kb.sb("qs%d" % i, [128, 128], BF16, es) for i in range(2)]
            scT = [kb.sb("scT%d" % i, [128, 128], BF16, es) for i in range(2)]
            kh = [kb.sb("kh%d" % i, [128, 128], BF16, es) for i in range(2)]
            dm = [kb.sb("dm%d" % i, [128, 1], F32, es) for i in range(2)]
            hn = [kb.sb("hn%d" % i, [128, 256], F32, es) for i in range(2)]
            sq = [kb.sb("sq%d" % i, [128, 256], F32, es) for i in range(2)]
            ssq = [kb.sb("ssq%d" % i, [128, 1], F32, es) for i in range(2)]
            ot = [kb.sb("ot%d" % i, [128, 512], F32, es) for i in range(2)]
            zt = [kb.sb("zt%d" % i, [128, 512], F32, es) for i in range(2)]
            eo = [kb.sb("eo%d" % i, [128, 256], F32, es) for i in range(2)]
            ez = [kb.sb("ez%d" % i, [128, 256], F32, es) for i in range(2)]
            yb = [kb.sb("yb%d" % i, [128, 256], BF16, es) for i in range(2)]
            yst = [kb.sb("yst%d" % i, [128, 4, 512], BF16, es) for i in range(2)]
            it = 0
            for c in range(self.nchunks if self.p2stop >= 4 else 0):
                o_t = ot[c % 2]
                z_t = zt[c % 2]
                kb.dma("sp", o_t[:], OML[c * 128:(c + 1) * 128, :], oml_all, [o_t], o_t)
                kb.dma("sp", z_t[:], ZML[c * 128:(c + 1) * 128, :], zml_all, [z_t], z_t)
                ys = yst[(c // 4) % 2]

                def head_steps(c, h_, o_t, z_t, ys):
                    i2 = h_
                    cs = slice(c * 128, (c + 1) * 128)
                    pA = psf[0 + i2]
                    pO = psf[2 + i2]
                    pC = psf[4 + i2]
                    pT = psb[i2]
                    lfc = LF[:, c, h_:h_ + 1]
                    lic = LI[:, c, h_:h_ + 1]
                    kb.op("dve", lambda h, i2=i2, c=c, h_=h_: h.tensor_scalar(
                        out=LFB[i2][:], in0=ones_b[:], scalar1=LFH32[:, c, h_:h_ + 1], scalar2=None,
                        op0=ALU.mult), [ones_b, LFH32], [LFB[i2]])
                    kb.op("dve", lambda h, i2=i2, c=c, h_=h_: h.tensor_scalar(
                        out=LFBl[i2][:], in0=ones_b[:], scalar1=LFL32[:, c, h_:h_ + 1], scalar2=None,
                        op0=ALU.mult), [ones_b, LFL32], [LFBl[i2]])
                    kb.op("pe", lambda h, i2=i2, pA=pA: h.matmul(
                        pA[:, 0:128], lhsT=LFB[i2][:], rhs=tri_b[:], start=True, stop=False),
                        [LFB[i2], tri_b], [pA])
                    kb.op("pe", lambda h, i2=i2, pA=pA: h.matmul(
                        pA[:, 0:128], lhsT=LFBl[i2][:], rhs=tri_b[:], start=False, stop=True),
                        [LFBl[i2], tri_b], [pA])
                    kb.op("pe", lambda h, i2=i2, pA=pA: h.matmul(
                        pA[:, 128:256], lhsT=tri_b[:], rhs=LFB[i2][:], start=True, stop=False),
                        [LFB[i2], tri_b], [pA])
                    kb.op("pe", lambda h, i2=i2, pA=pA: h.matmul(
                        pA[:, 128:256], lhsT=tri_b[:], rhs=LFBl[i2][:], start=False, stop=True),
                        [LFBl[i2], tri_b], [pA])
                    kb.op("act", lambda h, i2=i2, pA=pA: h.activation(
                        out=eBr[i2][:], in_=pA[:, 0:128], func=AF.Exp), [pA], [eBr[i2]])
                    kb.op("act", lambda h, i2=i2, pA=pA, lic=lic: h.activation(
                        out=E2[i2][:], in_=pA[:, 128:256], func=AF.Exp, scale=-1.0, bias=lic),
                        [pA, LI], [E2[i2]])
                    kb.op("dve", lambda h, i2=i2: h.tensor_tensor(
                        out=wk[i2][:], in0=E2[i2][:, 0:1], in1=eBr[i2][:, 127:128], op=ALU.mult),
                        [E2[i2], eBr[i2]], [wk[i2]])
                    if self.cstop <= 1:
                        kb.mute = True
                    yield
                    kb.op("dve", lambda h, i2=i2, h_=h_, cs=cs: h.tensor_tensor(
                        out=qs[i2][:], in0=qc[h_][:, cs], in1=eBr[i2][:], op=ALU.mult),
                        [qc[h_], eBr[i2]], [qs[i2]])
                    kb.op("pe", lambda h, i2=i2, h_=h_, cs=cs, pA=pA: h.matmul(
                        pA[:, 256:384], lhsT=kT[h_][:, cs], rhs=qs[i2][:], start=True, stop=True),
                        [kT[h_], qs[i2]], [pA])
                    kb.op("dve", lambda h, i2=i2, pA=pA: h.scalar_tensor_tensor(
                        out=scT[i2][:], in0=pA[:, 256:384], scalar=E2[i2][:, 0:1], in1=tri[:],
                        op0=ALU.mult, op1=ALU.mult), [pA, E2[i2], tri], [scT[i2]])
                    if self.cstop <= 2:
                        kb.mute = True
                    yield
                    kb.op("pe", lambda h, h_=h_, cs=cs, pT=pT: h.transpose(
                        out=pT[:, 0:128], in_=kT[h_][:, cs], identity=ident_b[:]),
                        [kT[h_], ident_b], [pT])
                    kb.op("act", lambda h, i2=i2, pT=pT: h.activation(
                        out=kh[i2][:], in_=pT[:, 0:128], func=AF.Copy, scale=wk[i2][:]),
                        [pT, wk[i2]], [kh[i2]])
                    if self.cstop <= 3:
                        kb.mute = True
                    yield
                    kb.op("pe", lambda h, i2=i2, h_=h_, pO=pO: h.matmul(
                        pO[:, 0:257], lhsT=qs[i2][:], rhs=Cb[h_][:], start=True, stop=False),
                        [qs[i2], Cb[h_]], [pO])
                    kb.op("pe", lambda h, i2=i2, h_=h_, c=c, pO=pO: h.matmul(
                        pO[:, 0:257], lhsT=scT[i2][:], rhs=vp[h_][:, c, :], start=False, stop=True),
                        [scT[i2], vp[h_], vp[h_].sub("one")], [pO])
                    yield
                    kb.op("pe", lambda h, i2=i2, h_=h_, c=c, pC=pC: h.matmul(
                        pC[:, 0:257], lhsT=kh[i2][:], rhs=vp[h_][:, c, :], start=True, stop=True),
                        [kh[i2], vp[h_], vp[h_].sub("one")], [pC])
                    kb.op("dve", lambda h, i2=i2, h_=h_, pC=pC: h.scalar_tensor_tensor(
                        out=C[h_][:], in0=C[h_][:], scalar=eBr[i2][:, 127:128], in1=pC[:, 0:257],
                        op0=ALU.mult, op1=ALU.add), [C[h_], eBr[i2], pC], [C[h_]])
                    kb.op("pool", lambda h, h_=h_: h.tensor_copy(out=Cb[h_][:], in_=C[h_][:]),
                          [C[h_]], [Cb[h_]])
                    if self.cstop <= 4:
                        kb.mute = True
                    yield
                    kb.op("act", lambda h, i2=i2, pO=pO: h.activation(
                        out=hraw[i2][:], in_=pO[:, 0:257], func=AF.Copy), [pO], [hraw[i2]])
                    kb.op("dve", lambda h, i2=i2: h.tensor_scalar(
                        out=wk[i2][:], in0=hraw[i2][:, 256:257], scalar1=-1.0, scalar2=1.0,
                        op0=ALU.mult, op1=ALU.max), [hraw[i2]], [wk[i2]])
                    kb.op("dve", lambda h, i2=i2: h.scalar_tensor_tensor(
                        out=dm[i2][:], in0=hraw[i2][:, 256:257], scalar=1.0, in1=wk[i2][:],
                        op0=ALU.max, op1=ALU.max), [hraw[i2], wk[i2]], [dm[i2]])
                    kb.op("dve", lambda h, i2=i2: h.reciprocal(out=dm[i2][:], in_=dm[i2][:]),
                          [dm[i2]], [dm[i2]])
                    kb.op("act", lambda h, i2=i2: h.activation(
                        out=sq[i2][:], in_=hraw[i2][:, 0:256], func=AF.Square, scale=dm[i2][:],
                        accum_out=ssq[i2][:]), [hraw[i2], dm[i2]], [sq[i2], ssq[i2]])
                    kb.op("dve", lambda h, i2=i2: h.tensor_scalar(
                        out=ssq[i2][:], in0=ssq[i2][:], scalar1=1.0 / 256, scalar2=EPS,
                        op0=ALU.mult, op1=ALU.add), [ssq[i2]], [ssq[i2]])
                    kb.op("act", lambda h, i2=i2: h.activation(
                        out=ssq[i2][:], in_=ssq[i2][:], func=AF.Ln), [ssq[i2]], [ssq[i2]])
                    kb.op("act", lambda h, i2=i2: h.activation(
                        out=ssq[i2][:], in_=ssq[i2][:], func=AF.Exp, scale=-0.5), [ssq[i2]], [ssq[i2]])
                    kb.op("dve", lambda h, i2=i2: h.tensor_tensor(
                        out=ssq[i2][:], in0=ssq[i2][:], in1=dm[i2][:], op=ALU.mult),
                        [ssq[i2], dm[i2]], [ssq[i2]])
                    if self.cstop <= 5:
                        kb.mute = True
                    yield
                    hs = slice(h_ * 256, (h_ + 1) * 256)
                    kb.op("act", lambda h, i2=i2, o_t=o_t, hs=hs: h.activation(
                        out=eo[i2][:], in_=o_t[:, hs], func=AF.Exp, scale=-1.0), [o_t], [eo[i2]])
                    kb.op("act", lambda h, i2=i2, z_t=z_t, hs=hs: h.activation(
                        out=ez[i2][:], in_=z_t[:, hs], func=AF.Exp, scale=-1.0), [z_t], [ez[i2]])
                    kb.op("pool", lambda h, i2=i2: h.tensor_scalar_add(
                        out=ez[i2][:], in0=ez[i2][:], scalar1=1.0), [ez[i2]], [ez[i2]])
                    kb.op("pool", lambda h, i2=i2: h.tensor_scalar_add(
                        out=eo[i2][:], in0=eo[i2][:], scalar1=1.0), [eo[i2]], [eo[i2]])
                    kb.op("pool", lambda h, i2=i2: h.tensor_tensor(
                        out=eo[i2][:], in0=eo[i2][:], in1=ez[i2][:], op=ALU.mult),
                        [eo[i2], ez[i2]], [eo[i2]])
                    kb.op("dve", lambda h, i2=i2: h.reciprocal(out=eo[i2][:], in_=eo[i2][:]),
                          [eo[i2]], [eo[i2]])
                    kb.op("pool", lambda h, i2=i2, z_t=z_t, hs=hs: h.tensor_tensor(
                        out=eo[i2][:], in0=eo[i2][:], in1=z_t[:, hs], op=ALU.mult),
                        [eo[i2], z_t], [eo[i2]])
                    kb.op("dve", lambda h, i2=i2, hs=hs: h.scalar_tensor_tensor(
                        out=hn[i2][:], in0=hraw[i2][:, 0:256], scalar=ssq[i2][:], in1=hnw[:, hs],
                        op0=ALU.mult, op1=ALU.mult), [hraw[i2], ssq[i2], hnw], [hn[i2]])
                    kb.op("dve", lambda h, i2=i2: h.tensor_tensor(
                        out=yb[i2][:], in0=hn[i2][:], in1=eo[i2][:], op=ALU.mult),
                        [hn[i2], eo[i2]], [yb[i2]])
                    if self.cstop <= 6:
                        kb.mute = True
                    yield
                    for j in range(2):
                        kb.op("pe", lambda h, i2=i2, j=j, pT=pT: h.transpose(
                            out=pT[:, 256 + j * 128:256 + (j + 1) * 128],
                            in_=yb[i2][:, j * 128:(j + 1) * 128], identity=ident_b[:]),
                            [yb[i2], ident_b], [pT])
                    self.evac(ys[:, h_ * 2:h_ * 2 + 2, (c % 4) * 128:(c % 4 + 1) * 128],
                              pT[:, 256:512].rearrange("p (j t) -> p j t", j=2), [pT], [ys])
                gens = [head_steps(c, 0, o_t, z_t, ys), head_steps(c, 1, o_t, z_t, ys)]
                while gens:
                    for g_ in list(gens):
                        try:
                            next(g_)
                        except StopIteration:
                            gens.remove(g_)
                if c % 4 == 3:
                    kb.dma("pool", YT[c // 4, 0:512, :].rearrange("(j p) t -> p j t", p=128),
                           ys[:], [ys], [YT.sub(("ml", c // 4))], ys)


    def phase4(self, YT, psf, psb, ident_b):
        kb = self.kb
        RG = [[0, 1, 2, 3], [4, 5, 6, 7]]
        wq_in = self.din("wq", [D, 1024])
        gq_in = self.din("gq", [D, 1024])
        pq_in = self.din("pq", [256, 1024])
        pT_in = self.din("pT", [256, S])
        xq_in = self.din("xq", [S, 1024])
        out = self.dout("out", [S, 1024])
        YG = kb.dram("YG", [8, 4 * 1024, 512], BF16)
        XT = kb.dram("XT", [8, 1024, 512], BF16)
        XG = kb.dram("XG", [8, 4 * 1024, 512], BF16)
        XN = kb.dram("XN", [S, 1024], F32)
        cc = TL(kb.new_sem("cc"), 1)
        with ExitStack() as es:
            if not self.have_nsa:
                zt_ = kb.sb("zeros_b", [128, 4, 512], BF16, es)
                kb.op("pool", lambda h: h.memset(zt_[:], 0.0), [], [zt_])
                for tg in range(8):
                    kb.dma("sp", YT[tg, 512:1024, :].rearrange("(j p) t -> p j t", p=128),
                           zt_[:], [zt_], [YT.sub(("ns", tg))], zt_)
            for tg in range(8):
                kb.custom("pool", lambda h, tg=tg: h.collective_compute(
                    "AllGather", ALU.bypass, replica_groups=RG, ins=[YT[tg]], outs=[YG[tg]]),
                    [YT.sub(("ml", tg)), YT.sub(("ns", tg))], [YG.sub(tg)], cc)
            wbf = kb.sb("w4bf", [128, 32, 1024], BF16, es)
            wst = [kb.sb("w4st%d" % i, [128, 1, 1024], F32, es) for i in range(3)]
            act_t = [kb.sb("a4t%d" % i, [128, 32, 512], BF16, es) for i in range(2)]
            xt_ = [kb.sb("x4t%d" % i, [128, 512], F32, es) for i in range(2)]
            xn = [kb.sb("xn4_%d" % i, [128, 512], F32, es) for i in range(2)]
            xnb = [kb.sb("xnb4_%d" % i, [128, 512], BF16, es) for i in range(2)]
            xTs = [kb.sb("xTs%d" % i, [128, 8, 512], BF16, es) for i in range(2)]
            sg = [kb.sb("sg4_%d" % i, [128, 512], F32, es) for i in range(2)]
            pqb = kb.sb("pqb", [128, 2, 1024], BF16, es)
            pTb = kb.sb("pTb", [128, 2, S], BF16, es)

            def load_w(src):
                v = src.t.rearrange("(k p) c -> p k c", p=128)
                for kq in range(32):
                    st = wst[kq % 3]
                    kb.dma("sp", st[:], v[:, kq:kq + 1, :], [src], [st], st)
                    self.cast(wbf[:, kq:kq + 1, :], st[:], [st], [wbf])

            load_w(wq_in)
            cnt = 0
            for tg in range(8):
                at = act_t[tg % 2]
                for q4 in range(4):
                    kb.dma("sp", at[:, q4 * 8:(q4 + 1) * 8, :],
                           YG[tg].rearrange("(k p) t -> p k t", p=128)[:, q4 * 8:(q4 + 1) * 8, :],
                           [YG.sub(tg)], [at], at, group=(q4 > 0))
                xs_ = xTs[tg % 2]
                for i4 in range(4):
                    t0 = tg * 512 + i4 * 128
                    for nt in range(2):
                        pt = psf[cnt % 4]
                        xb = xt_[cnt % 2]
                        xo = xn[cnt % 2]
                        xob = xnb[cnt % 2]
                        cnt += 1
                        kb.dma("sp", xb[:], xq_in[t0:t0 + 128, nt * 512:(nt + 1) * 512],
                               [xq_in], [xb], xb)
                        for k in range(32):
                            kb.op("pe", lambda h, pt=pt, at=at, k=k, i4=i4, nt=nt: h.matmul(
                                pt[:], lhsT=at[:, k, i4 * 128:(i4 + 1) * 128],
                                rhs=wbf[:, k, nt * 512:(nt + 1) * 512],
                                start=(k == 0), stop=(k == 31)), [at, wbf], [pt])
                        kb.op("dve", lambda h, pt=pt, xb=xb, xo=xo: h.tensor_tensor(
                            out=xo[:], in0=pt[:], in1=xb[:], op=ALU.add), [pt, xb], [xo])
                        kb.dma("pool", XN[t0:t0 + 128, nt * 512:(nt + 1) * 512], xo[:],
                               [xo], [XN], xo)
                        kb.op("act", lambda h, xo=xo, xob=xob: h.activation(
                            out=xob[:], in_=xo[:], func=AF.Copy), [xo], [xob])
                        pb = psb[cnt % 2]
                        for j in range(4):
                            kb.op("pe", lambda h, pb=pb, xob=xob, j=j: h.transpose(
                                out=pb[:, j * 128:(j + 1) * 128],
                                in_=xob[:, j * 128:(j + 1) * 128], identity=ident_b[:]),
                                [xob, ident_b], [pb])
                        self.evac(xs_[:, nt * 4:(nt + 1) * 4, i4 * 128:(i4 + 1) * 128],
                                  pb[:, 0:512].rearrange("p (j t) -> p j t", j=4), [pb], [xs_])
                kb.dma("pool", XT[tg].rearrange("(j p) t -> p j t", p=128),
                       xs_[:], [xs_], [XT.sub(tg)], xs_)
                kb.custom("pool", lambda h, tg=tg: h.collective_compute(
                    "AllGather", ALU.bypass, replica_groups=RG, ins=[XT[tg]], outs=[XG[tg]]),
                    [XT.sub(tg)], [XG.sub(tg)], cc)
            load_w(gq_in)
            for j in range(2):
                st = wst[j % 3]
                kb.dma("sp", st[:, 0, :], pq_in[j * 128:(j + 1) * 128, :], [pq_in], [st], st)
                self.cast(pqb[:, j, :], st[:, 0, :], [st], [pqb])
            for j in range(2):
                for q in range(4):
                    st = wst[(j * 4 + q) % 3]
                    kb.dma("sp", st[:, 0, :], pT_in[j * 128:(j + 1) * 128, q * 1024:(q + 1) * 1024],
                           [pT_in], [st], st)
                    self.cast(pTb[:, j, q * 1024:(q + 1) * 1024], st[:, 0, :], [st], [pTb])
            for tg in range(8):
                at = act_t[tg % 2]
                for q4 in range(4):
                    kb.dma("sp", at[:, q4 * 8:(q4 + 1) * 8, :],
                           XG[tg].rearrange("(k p) t -> p k t", p=128)[:, q4 * 8:(q4 + 1) * 8, :],
                           [XG.sub(tg)], [at], at, group=(q4 > 0))
                for i4 in range(4):
                    t0 = tg * 512 + i4 * 128
                    for nt in range(2):
                        pt = psf[cnt % 4]
                        pp = psf[4 + cnt % 2]
                        xb = xt_[cnt % 2]
                        xo = xn[cnt % 2]
                        sgt = sg[cnt % 2]
                        cnt += 1
                        kb.dma("sp", xb[:], XN[t0:t0 + 128, nt * 512:(nt + 1) * 512], [XN], [xb], xb)
                        for k in range(32):
                            kb.op("pe", lambda h, pt=pt, at=at, k=k, i4=i4, nt=nt: h.matmul(
                                pt[:], lhsT=at[:, k, i4 * 128:(i4 + 1) * 128],
                                rhs=wbf[:, k, nt * 512:(nt + 1) * 512],
                                start=(k == 0), stop=(k == 31)), [at, wbf], [pt])
                        for k in range(2):
                            kb.op("pe", lambda h, pp=pp, k=k, t0=t0, nt=nt: h.matmul(
                                pp[:], lhsT=pTb[:, k, t0:t0 + 128],
                                rhs=pqb[:, k, nt * 512:(nt + 1) * 512],
                                start=(k == 0), stop=(k == 1)), [pTb, pqb], [pp])
                        kb.op("act", lambda h, pt=pt, sgt=sgt: h.activation(
                            out=sgt[:], in_=pt[:], func=AF.Sigmoid), [pt], [sgt])
                        kb.op("dve", lambda h, pp=pp, sgt=sgt: h.tensor_tensor(
                            out=sgt[:], in0=sgt[:], in1=pp[:], op=ALU.mult), [sgt, pp], [sgt])
                        kb.op("pool", lambda h, sgt=sgt, xb=xb, xo=xo: h.tensor_tensor(
                            out=xo[:], in0=sgt[:], in1=xb[:], op=ALU.add), [sgt, xb], [xo])
                        kb.dma("pool", out[t0:t0 + 128, nt * 512:(nt + 1) * 512], xo[:],
                               [xo], [out], xo)

    def phase3(self, UT, MISC, ZNS, YT, psf, psb, ident_b):
        kb = self.kb
        ut_all = [UT.sub((ct, tt)) for ct in range(12) for tt in range(4)]
        misc_all = [MISC.sub(tt) for tt in range(4)]
        zns_all = [ZNS.sub(tt) for tt in range(4)]
        qnw_in = self.din("qnw", [128, 1])
        knw_in = self.din("knw", [128, 3])
        peK_in = self.din("peK", [128, 32])
        peV_in = self.din("peV", [128, 32])
        w1k_in = self.din("w1k", [D, 256])
        w2k_in = self.din("w2k", [256, 128])
        w1v_in = self.din("w1v", [D, 256])
        w2v_in = self.din("w2v", [256, 128])
        biasC_in = self.din("biasC", [8, 128, S])
        stripS_in = self.din("stripS", [4, 128, 1024])
        stripW_in = self.din("stripW", [4, 128, 1408])
        b31_in = self.din("b31", [128, 4])
        ov_in = self.din("ov", [256, 64])
        E_in = self.din("Esel", [64, S])
        keep_in = self.din("keep", [128, NT * 64])
        add_in = self.din("addc", [128, NT * 64])
        with ExitStack() as es:
            def small(name, src, shape):
                t = kb.sb(name, shape, F32, es)
                kb.dma("sp", t[:], src[:], [src], [t], t)
                return t
            qnw = small("qnw_s", qnw_in, [128, 1])
            knw = small("knw_s", knw_in, [128, 3])
            peK = small("peK_s", peK_in, [128, 32])
            peV = small("peV_s", peV_in, [128, 32])
            b31 = small("b31_s", b31_in, [128, 4])
            keep = small("keep_s", keep_in, [128, NT * 64])
            addc = small("add_s", add_in, [128, NT * 64])
            ones_b = kb.sb("ones3b", [128, 128], BF16, es)
            kb.op("pool", lambda h: h.memset(ones_b[:], 1.0), [], [ones_b])
            qn = [kb.sb("qn%d" % h_, [128, S], BF16, es) for h_ in range(4)]
            ksn = kb.sb("ksn", [128, S], BF16, es)
            kwn = kb.sb("kwn", [128, S], BF16, es)
            kcn = kb.sb("kcn", [128, 256], BF16, es)
            vcp = kb.sb("vcp", [128, 2, 193], BF16, es)
            vsp = kb.sb("vsp", [128, NT, 129], BF16, es)
            vwp = kb.sb("vwp", [128, NT, 129], BF16, es)
            stS = kb.sb("stS", [128, 4, 1024], BF16, es)
            stW = kb.sb("stW", [128, 4, 1408], BF16, es)
            Eb = kb.sb("Eb", [64, S], BF16, es)
            SG = kb.sb("SG", [128, NT, 12], F32, es)
            with ExitStack() as es2:
                X = kb.sb("X3", [128, S], F32, es2)
                sqb = [kb.sb("sqb%d" % i, [128, 512], BF16, es2) for i in range(2)]
                rr_ = [kb.sb("rr%d" % i, [128, 512], F32, es2) for i in range(2)]
                cnt = [0]

                def norm_block(src_ap, n, wcol, mul, add, srcbuf):
                    i2 = cnt[0] % 2
                    cnt[0] += 1
                    ps = psf[i2]
                    kb.op("act", lambda h: h.activation(out=sqb[i2][:, :n], in_=src_ap, func=AF.Square),
                          [srcbuf], [sqb[i2]])
                    kb.op("pe", lambda h: h.matmul(ps[:, :n], lhsT=ones_b[:], rhs=sqb[i2][:, :n],
                                                   start=True, stop=True), [ones_b, sqb[i2]], [ps])
                    kb.op("dve", lambda h: h.tensor_scalar(out=rr_[i2][:, :n], in0=ps[:, :n], scalar1=mul,
                                                           scalar2=add, op0=ALU.mult, op1=ALU.add),
                          [ps], [rr_[i2]])
                    kb.op("act", lambda h: h.activation(out=rr_[i2][:, :n], in_=rr_[i2][:, :n], func=AF.Ln),
                          [rr_[i2]], [rr_[i2]])
                    kb.op("act", lambda h: h.activation(out=rr_[i2][:, :n], in_=rr_[i2][:, :n], func=AF.Exp,
                                                        scale=-0.5), [rr_[i2]], [rr_[i2]])
                    return i2

                def fm_norm(ct, wcol, mul, add, dst, wbuf):
                    kb.dma("sp", X[:], UT[ct * 128:(ct + 1) * 128, :], ut_all, [X], X)
                    for cb in range(8):
                        cs = slice(cb * 512, (cb + 1) * 512)
                        i2 = norm_block(X[:, cs], 512, wcol, mul, add, X)
                        kb.op("dve", lambda h, i2=i2, cs=cs: h.scalar_tensor_tensor(
                            out=dst[:, cs], in0=X[:, cs], scalar=wcol, in1=rr_[i2][:],
                            op0=ALU.mult, op1=ALU.mult), [X, rr_[i2], wbuf], [dst])

                for h_ in range(4):
                    fm_norm(4 + h_, qnw[:, 0:1], 1.0, 128.0 * EPS, qn[h_], qnw)
                fm_norm(10, knw[:, 1:2], 1.0 / 128, EPS, ksn, knw)
                fm_norm(11, knw[:, 2:3], 1.0 / 128, EPS, kwn, knw)

                Rlo = kb.sb("Rlo", [128, 16, 256], BF16, es2)
                Rhi = kb.sb("Rhi", [128, 16, 256], BF16, es2)
                w1b = kb.sb("w1b", [128, 32, 256], BF16, es2)
                w1st = [kb.sb("w1st%d" % i, [128, 4, 256], F32, es2) for i in range(2)]
                w2st = kb.sb("w2st", [128, 2, 128], F32, es2)
                w2b = kb.sb("w2b", [128, 2, 128], BF16, es2)
                AT = [kb.sb("AT%d" % i, [128, 256], BF16, es2) for i in range(2)]
                tmpa = kb.sb("tmpa", [128, 256], F32, es2)
                kcf = kb.sb("kcf", [128, 256], F32, es2)
                ovst = kb.sb("ovst", [128, 2, 64], F32, es2)
                for which in range(2):
                    ct = 8 + which
                    pe_ = peK if which == 0 else peV
                    w1_in = w1k_in if which == 0 else w1v_in
                    w2_in = w2k_in if which == 0 else w2v_in
                    kb.dma("sp", X[:], UT[ct * 128:(ct + 1) * 128, :], ut_all, [X], X)
                    Xv = X.t[:].rearrange("d (c l) -> d l c", l=16)
                    for l in range(16):
                        kb.op("dve", lambda h, l=l, pe_=pe_, Xv=Xv: h.tensor_scalar(
                            out=Rlo[:, l, :], in0=Xv[:, l, :], scalar1=pe_[:, l:l + 1], scalar2=None,
                            op0=ALU.add), [X, pe_], [Rlo])
                        kb.op("pool", lambda h, l=l, pe_=pe_, Xv=Xv: h.tensor_scalar(
                            out=Rhi[:, l, :], in0=Xv[:, l, :], scalar1=pe_[:, 16 + l:17 + l], scalar2=None,
                            op0=ALU.add), [X, pe_], [Rhi])
                    w1v_ = w1_in.t.rearrange("(l d) j -> d l j", d=128)
                    for q in range(8):
                        st = w1st[q % 2]
                        kb.dma("sp", st[:], w1v_[:, q * 4:(q + 1) * 4, :], [w1_in], [st], st)
                        self.cast(w1b[:, q * 4:(q + 1) * 4, :], st[:], [st], [w1b])
                    kb.dma("sp", w2st[:], w2_in.t.rearrange("(j p) d -> p j d", p=128), [w2_in], [w2st], w2st)
                    kb.op("dve", lambda h: h.tensor_copy(out=w2b[:], in_=w2st[:]), [w2st], [w2b])
                    for jt in range(2):
                        ps = psf[2 + jt]
                        for l in range(32):
                            rhs = Rlo[:, l, 0:255] if l < 16 else Rhi[:, l - 16, 1:256]
                            kb.op("pe", lambda h, ps=ps, l=l, jt=jt, rhs=rhs: h.matmul(
                                ps[:, 0:255], lhsT=w1b[:, l, jt * 128:(jt + 1) * 128], rhs=rhs,
                                start=(l == 0), stop=(l == 31)), [w1b, Rlo, Rhi], [ps])
                        kb.op("act", lambda h, ps=ps: h.activation(out=tmpa[:, 0:255], in_=ps[:, 0:255],
                                                                   func=AF.Exp, scale=-1.0), [ps], [tmpa])
                        kb.op("dve", lambda h: h.tensor_scalar_add(out=tmpa[:, 0:255], in0=tmpa[:, 0:255],
                                                                   scalar1=1.0), [tmpa], [tmpa])
                        kb.op("dve", lambda h: h.reciprocal(out=tmpa[:, 0:255], in_=tmpa[:, 0:255]),
                              [tmpa], [tmpa])
                        kb.op("pool", lambda h, jt=jt: h.memset(AT[jt][:, 255:256], 0.0), [], [AT[jt].sub("z")])
                        kb.op("dve", lambda h, ps=ps, jt=jt: h.tensor_tensor(
                            out=AT[jt][:, 0:255], in0=tmpa[:, 0:255], in1=ps[:, 0:255], op=ALU.mult),
                            [tmpa, ps], [AT[jt]])
                    if which == 0:
                        ps = psf[4]
                        for jt in range(2):
                            kb.op("pe", lambda h, ps=ps, jt=jt: h.matmul(
                                ps[:, 0:256], lhsT=w2b[:, jt, :], rhs=AT[jt][:], start=(jt == 0),
                                stop=(jt == 1)), [w2b, AT[jt], AT[jt].sub("z")], [ps])
                        kb.op("act", lambda h, ps=ps: h.activation(out=kcf[:], in_=ps[:, 0:256], func=AF.Copy),
                              [ps], [kcf])
                        i2 = norm_block(kcf[:], 256, None, 1.0 / 128, EPS, kcf)
                        kb.op("dve", lambda h, i2=i2: h.scalar_tensor_tensor(
                            out=kcn[:], in0=kcf[:], scalar=knw[:, 0:1], in1=rr_[i2][:, 0:256],
                            op0=ALU.mult, op1=ALU.mult), [kcf, rr_[i2], knw], [kcn])
                    else:
                        for c2 in range(2):
                            ps = psf[4 + c2]
                            for jt in range(2):
                                kb.op("pe", lambda h, ps=ps, jt=jt, c2=c2: h.matmul(
                                    ps[:, 0:128], lhsT=AT[jt][:, c2 * 128:(c2 + 1) * 128], rhs=w2b[:, jt, :],
                                    start=(jt == 0), stop=(jt == 1)), [w2b, AT[jt], AT[jt].sub("z")], [ps])
                            kb.op("act", lambda h, ps=ps, c2=c2: h.activation(
                                out=vcp[:, c2, 0:128], in_=ps[:, 0:128], func=AF.Copy), [ps], [vcp])
                kb.op("pool", lambda h: h.memset(vcp[:, :, 128:129], 1.0), [], [vcp.sub("one")])
                kb.dma("sp", ovst[:], ov_in.t.rearrange("(c p) n -> p c n", p=128), [ov_in], [ovst], ovst)
                kb.op("dve", lambda h: h.tensor_copy(out=vcp[:, :, 129:193], in_=ovst[:]), [ovst], [vcp.sub("ov")])
                vcp_all = [vcp, vcp.sub("one"), vcp.sub("ov")]
                for h_ in range(4):
                    kb.dma("sp", X[:, 0:1024], stripS_in[h_], [stripS_in], [X], X)
                    self.cast(stS[:, h_, :], X[:, 0:1024], [X], [stS])
                    kb.dma("sp", X[:, 0:1408], stripW_in[h_], [stripW_in], [X], X)
                    self.cast(stW[:, h_, :], X[:, 0:1408], [X], [stW])
                kb.dma("sp", X[0:64, :], E_in[:], [E_in], [X], X)
                kb.op("dve", lambda h: h.tensor_copy(out=Eb[:], in_=X[0:64, :]), [X], [Eb])
                kb.op("pool", lambda h: h.memset(vsp[:, :, 128:129], 1.0), [], [vsp.sub("one")])
                kb.op("pool", lambda h: h.memset(vwp[:, :, 128:129], 1.0), [], [vwp.sub("one")])
                vst = [kb.sb("vst3_%d" % i, [128, 272], F32, es2) for i in range(2)]
                for n in range(NT):
                    st = vst[n % 2]
                    kb.dma("sp", st[:], MISC[n * 128:(n + 1) * 128, :], misc_all, [st], st)
                    self.cast(vsp[:, n, 0:128], st[:, 0:128], [st], [vsp])
                    self.cast(vwp[:, n, 0:128], st[:, 128:256], [st], [vwp])
                    kb.op("pool", lambda h, st=st, n=n: h.tensor_copy(out=SG[:, n, :], in_=st[:, 260:272]),
                          [st], [SG])
                kb.op("act", lambda h: h.activation(out=SG[:], in_=SG[:], func=AF.Exp, scale=-1.0), [SG], [SG])
                kb.op("dve", lambda h: h.tensor_scalar_add(out=SG[:], in0=SG[:], scalar1=1.0), [SG], [SG])
                kb.op("dve", lambda h: h.reciprocal(out=SG[:], in_=SG[:]), [SG], [SG])
            kb.barrier()
            vsp_all = [vsp, vsp.sub("one")]
            vwp_all = [vwp, vwp.sub("one")]
            bst = [kb.sb("bst%d" % i, [128, 512], F32, es) for i in range(2)]
            bcb = [kb.sb("bcb%d" % i, [128, 512], BF16, es) for i in range(2)]
            PT = [kb.sb("PT%d" % i, [128, 512], BF16, es) for i in range(4)]
            OC = kb.sb("OC", [128, 4, 4, 193], F32, es)
            OS = kb.sb("OS", [128, 4, 4, 129], F32, es)
            OW = kb.sb("OW", [128, 4, 4, 129], F32, es)
            rc4 = kb.sb("rc4", [128, 4], F32, es)
            imp = kb.sb("imp", [128, 64], F32, es)
            rep = kb.sb("rep", [128, 64], F32, es)
            m8a = kb.sb("m8a", [128, 8], F32, es)
            m8b = kb.sb("m8b", [128, 8], F32, es)
            selm = kb.sb("selm", [128, 64], F32, es)
            selb = kb.sb("selb", [128, 64], BF16, es)
            selT = kb.sb("selT", [64, 512], BF16, es)
            D12 = kb.sb("D12", [128, 4, 3], F32, es)
            CF = kb.sb("CF", [128, 12], F32, es)
            zt = [kb.sb("z3_%d" % i, [128, 512], F32, es) for i in range(2)]
            gz = kb.sb("gz", [128, 512], F32, es)
            yacc = kb.sb("yacc", [128, 128], F32, es)
            ybf = [kb.sb("ybf%d" % i, [128, 128], BF16, es) for i in range(2)]
            yst = [kb.sb("yst3_%d" % i, [128, 4, 512], BF16, es) for i in range(2)]
            sc = 0
            pc = 0
            for i in range(self.nsa_tiles):
                tcs = slice(i * 512, (i + 1) * 512)
                for h_ in range(4):
                    pts = []
                    for ct in range(2):
                        b1 = bst[sc % 2]
                        b2 = bcb[sc % 2]
                        pS = psf[sc % 2]
                        sc += 1
                        ptile = PT[pc % 4]
                        pc += 1
                        pts.append(ptile)
                        kb.dma("sp", b1[:], biasC_in[h_ * 2 + ct, :, tcs], [biasC_in], [b1], b1)
                        kb.op("pool", lambda h, b1=b1, b2=b2: h.tensor_copy(out=b2[:], in_=b1[:]), [b1], [b2])
                        kb.op("pe", lambda h, pS=pS, ct=ct, h_=h_, tcs=tcs: h.matmul(
                            pS[:], lhsT=kcn[:, ct * 128:(ct + 1) * 128], rhs=qn[h_][:, tcs],
                            start=True, stop=False), [kcn, qn[h_]], [pS])
                        kb.op("pe", lambda h, pS=pS, b2=b2: h.matmul(
                            pS[:], lhsT=ident_b[:], rhs=b2[:], start=False, stop=True), [ident_b, b2], [pS])
                        kb.op("act", lambda h, pS=pS, ptile=ptile: h.activation(
                            out=ptile[:], in_=pS[:], func=AF.Exp), [pS], [ptile])
                    for tt in range(4):
                        pO = psf[2 + tt]
                        for ct in range(2):
                            kb.op("pe", lambda h, pO=pO, ct=ct, tt=tt, p_=pts[ct]: h.matmul(
                                pO[:, 0:193], lhsT=p_[:, tt * 128:(tt + 1) * 128], rhs=vcp[:, ct, :],
                                start=(ct == 0), stop=(ct == 1)), [pts[ct]] + vcp_all, [pO])
                        self.evac(OC[:, h_, tt, :], pO[:, 0:193], [pO], [OC.sub((h_, tt))])
                pT = psb[0]
                for tt in range(4):
                    oc_r = [OC.sub((h_, tt)) for h_ in range(4)]
                    kt = slice((4 * i + tt) * 64, (4 * i + tt + 1) * 64)
                    kb.op("dve", lambda h, tt=tt: h.tensor_scalar_max(out=rc4[:], in0=OC[:, :, tt, 128],
                                                                      scalar1=1.0e-30), oc_r, [rc4])
                    kb.op("dve", lambda h: h.reciprocal(out=rc4[:], in_=rc4[:]), [rc4], [rc4])
                    for h_ in range(4):
                        if h_ == 0:
                            kb.op("dve", lambda h, tt=tt: h.tensor_scalar(
                                out=imp[:], in0=OC[:, 0, tt, 129:193], scalar1=rc4[:, 0:1], scalar2=None,
                                op0=ALU.mult), oc_r + [rc4], [imp])
                        else:
                            kb.op("dve", lambda h, tt=tt, h_=h_: h.scalar_tensor_tensor(
                                out=imp[:], in0=OC[:, h_, tt, 129:193], scalar=rc4[:, h_:h_ + 1], in1=imp[:],
                                op0=ALU.mult, op1=ALU.add), oc_r + [rc4, imp], [imp])
                    kb.op("dve", lambda h, kt=kt: h.tensor_tensor(out=imp[:], in0=imp[:], in1=keep[:, kt],
                                                                  op=ALU.mult), [imp, keep], [imp])
                    kb.op("dve", lambda h, kt=kt: h.tensor_tensor(out=imp[:], in0=imp[:], in1=addc[:, kt],
                                                                  op=ALU.add), [imp, addc], [imp])
                    kb.op("dve", lambda h: h.max(out=m8a[:], in_=imp[:]), [imp], [m8a])
                    kb.op("dve", lambda h: h.match_replace(out=rep[:], in_to_replace=m8a[:], in_values=imp[:],
                                                           imm_value=-3.0e38), [imp, m8a], [rep])
                    kb.op("dve", lambda h: h.max(out=m8b[:], in_=rep[:]), [rep], [m8b])
                    kb.op("dve", lambda h: h.tensor_scalar(out=selm[:], in0=imp[:], scalar1=m8b[:, 7:8],
                                                           scalar2=None, op0=ALU.is_ge), [imp, m8b], [selm])
                    kb.op("dve", lambda h: h.tensor_scalar(out=selb[:], in0=selm[:], scalar1=-NEG, scalar2=NEG,
                                                           op0=ALU.mult, op1=ALU.add), [selm], [selb])
                    kb.op("pe", lambda h, tt=tt, pT=pT: h.transpose(
                        out=pT[0:64, tt * 128:(tt + 1) * 128], in_=selb[:], identity=ident_b[:]),
                        [selb, ident_b], [pT])
                self.evac(selT[:], pT[0:64, 0:512], [pT], [selT])
                items = []
                for br in range(2):
                    j0 = 0 if br == 0 else max(0, 4 * i - 4)
                    for h_ in range(4):
                        for j in range(j0, 4 * i + 4):
                            items.append(dict(br=br, h=h_, j=j, j0=j0))

                def emit_qk(it, i=i, tcs=tcs):
                    nonlocal sc, pc
                    br, h_, j = it["br"], it["h"], it["j"]
                    kn = ksn if br == 0 else kwn
                    st_ = stS if br == 0 else stW
                    pS = psf[sc % 2]
                    sc += 1
                    ptile = PT[pc % 4]
                    pc += 1
                    it["pS"], it["pt"] = pS, ptile
                    delta = 512 * i - 128 * j
                    near = (br == 1) or (delta <= 128)
                    it["near"] = near
                    kb.op("pe", lambda h: h.matmul(
                        pS[:], lhsT=kn[:, j * 128:(j + 1) * 128], rhs=qn[h_][:, tcs],
                        start=True, stop=False), [kn, qn[h_]], [pS])
                    if br == 0:
                        kb.op("pe", lambda h: h.matmul(
                            pS[:], lhsT=Eb[0:64, j * 128:(j + 1) * 128], rhs=selT[0:64, :],
                            start=False, stop=(not near)), [Eb, selT], [pS])
                    if near:
                        off = delta + 384
                        kb.op("pe", lambda h: h.matmul(
                            pS[:], lhsT=ident_b[:], rhs=st_[:, h_, off:off + 512],
                            start=False, stop=True), [ident_b, st_], [pS])

                def emit_exp(it):
                    pS, ptile, h_ = it["pS"], it["pt"], it["h"]
                    if it["near"]:
                        kb.op("act", lambda h: h.activation(out=ptile[:], in_=pS[:], func=AF.Exp),
                              [pS], [ptile])
                    else:
                        kb.op("act", lambda h: h.activation(out=ptile[:], in_=pS[:], func=AF.Exp,
                                                            bias=b31[:, h_:h_ + 1]), [pS, b31], [ptile])

                def emit_pv(it, i=i):
                    br, h_, j, j0 = it["br"], it["h"], it["j"], it["j0"]
                    vp_ = vsp if br == 0 else vwp
                    vp_all = vsp_all if br == 0 else vwp_all
                    OB = OS if br == 0 else OW
                    ptile = it["pt"]
                    for tt in range(4):
                        if j > 4 * i + tt:
                            continue
                        pO = psf[2 + tt]
                        kb.op("pe", lambda h, pO=pO, tt=tt: h.matmul(
                            pO[:, 0:129], lhsT=ptile[:, tt * 128:(tt + 1) * 128], rhs=vp_[:, j, :],
                            start=(j == j0), stop=(j == 4 * i + tt)), [ptile] + vp_all, [pO])
                    if j == 4 * i + 3:
                        for tt in range(4):
                            self.evac(OB[:, h_, tt, :], psf[2 + tt][:, 0:129], [psf[2 + tt]],
                                      [OB.sub((h_, tt))])

                emit_qk(items[0])
                for k_, it in enumerate(items):
                    if k_ + 1 < len(items):
                        emit_qk(items[k_ + 1])
                    emit_exp(it)
                    emit_pv(it)
                ys = yst[i % 2]
                pY = psb[1]
                for tt in range(4):
                    n = 4 * i + tt
                    z_t = zt[n % 2]
                    kb.dma("sp", z_t[:], ZNS[n * 128:(n + 1) * 128, :], zns_all, [z_t], z_t)
                    srcs = [OC.sub((h_, tt)) for h_ in range(4)] + [OS.sub((h_, tt)) for h_ in range(4)] + \
                           [OW.sub((h_, tt)) for h_ in range(4)]
                    for bi, OB in enumerate((OC, OS, OW)):
                        kb.op("dve", lambda h, bi=bi, OB=OB, tt=tt: h.tensor_copy(
                            out=D12[:, :, bi], in_=OB[:, :, tt, 128]), srcs, [D12])
                    kb.op("dve", lambda h: h.tensor_scalar_max(out=D12[:], in0=D12[:], scalar1=1.0e-30),
                          [D12], [D12])
                    kb.op("dve", lambda h: h.reciprocal(out=D12[:], in_=D12[:]), [D12], [D12])
                    kb.op("dve", lambda h, n=n: h.tensor_tensor(
                        out=CF[:], in0=D12[:].rearrange("p a b -> p (a b)"), in1=SG[:, n, :], op=ALU.mult),
                        [D12, SG], [CF])
                    kb.op("act", lambda h, z_t=z_t: h.activation(out=gz[:], in_=z_t[:], func=AF.Exp, scale=-1.0),
                          [z_t], [gz])
                    kb.op("pool", lambda h: h.tensor_scalar_add(out=gz[:], in0=gz[:], scalar1=1.0), [gz], [gz])
                    kb.op("dve", lambda h: h.reciprocal(out=gz[:], in_=gz[:]), [gz], [gz])
                    kb.op("pool", lambda h, z_t=z_t: h.tensor_tensor(out=gz[:], in0=gz[:], in1=z_t[:], op=ALU.mult),
                          [gz, z_t], [gz])
                    for h_ in range(4):
                        yb_ = ybf[h_ % 2]
                        kb.op("dve", lambda h, h_=h_, tt=tt: h.tensor_scalar(
                            out=yacc[:], in0=OC[:, h_, tt, 0:128], scalar1=CF[:, 3 * h_:3 * h_ + 1], scalar2=None,
                            op0=ALU.mult), srcs + [CF], [yacc])
                        kb.op("dve", lambda h, h_=h_, tt=tt: h.scalar_tensor_tensor(
                            out=yacc[:], in0=OS[:, h_, tt, 0:128], scalar=CF[:, 3 * h_ + 1:3 * h_ + 2], in1=yacc[:],
                            op0=ALU.mult, op1=ALU.add), srcs + [CF, yacc], [yacc])
                        kb.op("dve", lambda h, h_=h_, tt=tt: h.scalar_tensor_tensor(
                            out=yacc[:], in0=OW[:, h_, tt, 0:128], scalar=CF[:, 3 * h_ + 2:3 * h_ + 3], in1=yacc[:],
                            op0=ALU.mult, op1=ALU.add), srcs + [CF, yacc], [yacc])
                        kb.op("dve", lambda h, h_=h_, yb_=yb_: h.tensor_tensor(
                            out=yb_[:], in0=yacc[:], in1=gz[:, h_ * 128:(h_ + 1) * 128], op=ALU.mult),
                            [yacc, gz], [yb_])
                        kb.op("pe", lambda h, h_=h_, yb_=yb_, pY=pY: h.transpose(
                            out=pY[:, h_ * 128:(h_ + 1) * 128], in_=yb_[:], identity=ident_b[:]),
                            [yb_, ident_b], [pY])
                    self.evac(ys[:, :, tt * 128:(tt + 1) * 128],
                              pY[:, 0:512].rearrange("p (j t) -> p j t", j=4), [pY], [ys])
                kb.dma("pool", YT[i, 512:1024, :].rearrange("(j p) t -> p j t", p=128), ys[:],
                       [ys], [YT.sub(("ns", i))], ys)


def make_core_inputs(inputs, c):
    g = c % 4
    b = c // 4
    w_in = inputs["w_in"][0]
    cols = []
    cols += list(range(256 * g, 256 * g + 256))
    cols += list(range(1024 + 256 * g, 1024 + 256 * g + 256))
    cols += list(range(8208 + 512 * g, 8208 + 512 * g + 512))
    cols += list(range(10256 + 128 * g, 10256 + 128 * g + 128))
    cols += list(range(10768 + 128 * g, 10768 + 128 * g + 128))
    cols += list(range(11280 + 128 * g, 11280 + 128 * g + 128))
    cols += list(range(12304 + 128 * g, 12304 + 128 * g + 128))
    cols += list(range(2048 + 512 * g, 2048 + 512 * g + 512))
    cols += list(range(4096 + 512 * g, 4096 + 512 * g + 512))
    cols += list(range(6144 + 512 * g, 6144 + 512 * g + 512))
    cols += list(range(11792 + 128 * g, 11792 + 128 * g + 128))
    cols += list(range(12816 + 128 * g, 12816 + 128 * g + 128))
    cols += [8192 + 2 * g, 8192 + 2 * g + 1, 8200 + 2 * g, 8200 + 2 * g + 1]
    cols += list(range(13328 + 12 * g, 13328 + 12 * g + 12))
    cols += list(range(13376 + 512 * g, 13376 + 512 * g + 512))
    assert len(cols) == WCOLS
    m = {
        "x": np.ascontiguousarray(inputs["x"][b]),
        "wcore": np.ascontiguousarray(w_in[:, cols]),
        "normw": np.ascontiguousarray(np.broadcast_to(inputs["norm_w"][0][None, :], (128, D))),
        "ident": np.eye(128, dtype=np.float32),
    }
    cw = inputs["ml_conv_w"][0]
    convw = np.zeros((128, 16), np.float32)
    for ci in range(4):
        base = (0 if ci < 2 else 1024) + 256 * g + 128 * (ci % 2)
        convw[:, ci * 4:(ci + 1) * 4] = cw[:, base:base + 128].T
    m["convw"] = convw
    gbv = np.array([inputs["ml_i_bias"][0][2 * g], inputs["ml_i_bias"][0][2 * g + 1],
                    inputs["ml_f_bias"][0][2 * g], inputs["ml_f_bias"][0][2 * g + 1]], np.float32)
    m["gbias"] = np.ascontiguousarray(np.broadcast_to(gbv[None, :], (128, 4)))
    hn = inputs["ml_head_norm_w"][0][2 * g:2 * g + 2].reshape(1, 512)
    m["hnw"] = np.ascontiguousarray(np.broadcast_to(hn, (128, 512)))
    m["tri"] = np.triu(np.ones((128, 128), np.float32))
    r = g
    qs = slice(1024 * r, 1024 * (r + 1))
    rows = []
    for gg in range(4):
        rows += list(range(512 * gg, 512 * gg + 512))
        rows += list(range(2048 + 512 * gg, 2048 + 512 * gg + 512))
    m["wq"] = np.ascontiguousarray(inputs["w_out"][0][rows][:, qs])
    m["gq"] = np.ascontiguousarray(inputs["ple_gate"][0][:, qs])
    m["pq"] = np.ascontiguousarray(inputs["ple_proj"][0][:, qs])
    m["pT"] = np.ascontiguousarray(inputs["p"][0, b].T)
    m["xq"] = np.ascontiguousarray(inputs["x"][b][:, qs])
    rb = inputs["rel_bias"][:, 4 * g:4 * g + 4]
    m["qnw"] = np.ascontiguousarray(inputs["nsa_q_norm_w"][0].reshape(128, 1))
    m["knw"] = np.ascontiguousarray(inputs["nsa_k_norm_w"][0].T)
    m["peK"] = np.ascontiguousarray(inputs["cmp_pe_k"][0].T)
    m["peV"] = np.ascontiguousarray(inputs["cmp_pe_v"][0].T)
    m["w1k"] = inputs["cmp_k_w1"][0]
    m["w2k"] = inputs["cmp_k_w2"][0]
    m["w1v"] = inputs["cmp_v_w1"][0]
    m["w2v"] = inputs["cmp_v_w2"][0]
    tab = _nsa_tables()
    negf = np.float32(NEG)
    bc = np.where(tab["c_valid"][None], rb.T[:, tab["c_bucket"]], negf).astype(np.float32)
    m["biasC"] = np.ascontiguousarray(bc.reshape(8, 128, S))
    m["stripS"] = np.ascontiguousarray(
        np.where(tab["s_valid"][None], rb.T[:, tab["s_bucket"]], negf).astype(np.float32))
    m["stripW"] = np.ascontiguousarray(
        np.where(tab["w_valid"][None], rb.T[:, tab["w_bucket"]], negf).astype(np.float32))
    m["b31"] = np.ascontiguousarray(np.broadcast_to(rb[31][None, :], (128, 4)))
    m["ov"] = tab["ov"]
    m["Esel"] = tab["E"]
    m["keep"] = tab["keep"]
    m["addc"] = tab["add"]
    return m


_TAB = {}


def _bucket(dist):
    import math
    n = np.maximum(dist, 0)
    nf = np.maximum(n, 1).astype(np.float32)
    large = 16 + (np.log(nf / np.float32(16)) / np.float32(math.log(128 / 16))
                  * np.float32(16)).astype(np.int32)
    large = np.minimum(large, 31)
    return np.where(n < 16, n, large)


def _nsa_tables():
    if _TAB:
        return _TAB
    t = np.arange(S)
    c = np.arange(256)
    dist = t[None, :] - (16 * c[:, None] + 31)
    _TAB["c_valid"] = (dist >= 0) & (c[:, None] <= 254)
    _TAB["c_bucket"] = _bucket(dist)
    sl = np.arange(128)[:, None]
    u = np.arange(1024)[None, :]
    d = u - 384 - sl
    _TAB["s_valid"] = d >= 0
    _TAB["s_bucket"] = _bucket(d)
    u = np.arange(1408)[None, :]
    d = u - 384 - sl
    _TAB["w_valid"] = (d >= 0) & (d < 512)
    _TAB["w_bucket"] = _bucket(d)
    n = np.arange(64)
    ov = ((16 * c[:, None] < 64 * n[None, :] + 64) & (16 * c[:, None] + 32 > 64 * n[None, :])
          & (c[:, None] <= 254)).astype(np.float32)
    _TAB["ov"] = np.ascontiguousarray(ov)
    _TAB["E"] = np.ascontiguousarray((t[None, :] // 64 == n[:, None]).astype(np.float32))
    tt = np.arange(NT)[None, :, None]
    p = np.arange(128)[:, None, None]
    cur = (128 * tt + p) // 64
    nn = n[None, None, :]
    forced = (nn == 0) | (nn == cur) | (nn == cur - 1)
    future = nn > cur
    keep = (~(forced | future)).astype(np.float32)
    add = np.where(forced, 1.0e4 + nn, np.where(future, -1.0e30, 0.0)).astype(np.float32)
    _TAB["keep"] = np.ascontiguousarray(keep.reshape(128, NT * 64))
    _TAB["add"] = np.ascontiguousarray(add.reshape(128, NT * 64))
    return _TAB


_CACHE = {}


def kernel(**inputs):
    inputs = {k: np.asarray(v) for k, v in inputs.items()}
    if "nc" not in _CACHE:
        _CACHE["nc"] = Prog(debug=False, have_nsa=True).build()
    nc = _CACHE["nc"]
    in_maps = [make_core_inputs(inputs, c) for c in range(8)]
    res = run_bass_kernel_spmd(nc, in_maps, core_ids=list(range(8)))
    out = np.zeros((2, S, D), np.float32)
    for c in range(8):
        out[c // 4, :, 1024 * (c % 4):1024 * (c % 4 + 1)] = res.results[c]["out"]
    return out
```

```python
import numpy as np
from contextlib import ExitStack
import concourse.bass as bass
import concourse.mybir as mybir
from concourse.bass_utils import run_bass_kernel_spmd

F32 = mybir.dt.float32
BF16 = mybir.dt.bfloat16
AF = mybir.ActivationFunctionType
ALU = mybir.AluOpType
AX = mybir.AxisListType

D = 4096
S = 4096
NT = S // 128
WCOLS = 3856
EPS = 1e-6
NEG = -30000.0


class TL:
    def __init__(self, sem, inc):
        self.sem = sem
        self.inc = inc
        self.count = 0


class Res:
    def __init__(self, name):
        self.name = name
        self.w = None
        self.r = {}


class Buf(Res):
    def __init__(self, kb, name, t):
        super().__init__(name)
        self.kb = kb
        self.t = t
        self._chan = None
        self.subs = {}

    def __getitem__(self, k):
        return self.t[k]

    def chan(self):
        if self._chan is None:
            self._chan = TL(self.kb.new_sem("c_" + self.name), 16)
            self.kb.chans.append(self._chan)
        return self._chan

    def sub(self, key):
        if key not in self.subs:
            b = Buf(self.kb, "%s_%s" % (self.name, key), self.t)
            self.subs[key] = b
        return self.subs[key]


class Eng:
    def __init__(self, name, tl):
        self.name = name
        self.tl = tl
        self.seen = {}
        self.ops = []


class KB:
    def __init__(self, nc):
        self.nc = nc
        self.es = ExitStack()
        self.nsem = 0
        self.engs = {}
        for n in ("pe", "dve", "act", "pool", "sp"):
            self.engs[n] = Eng(n, TL(self.new_sem("e_" + n), 1))
        self.nbuf = 0
        self.mute = False
        self.chans = []

    def new_sem(self, name):
        self.nsem += 1
        return self.es.enter_context(self.nc.semaphore(name))

    def sb(self, name, shape, dt, es=None):
        t = (es or self.es).enter_context(self.nc.sbuf_tensor(name, list(shape), dt))
        return Buf(self, name, t)

    def ps(self, name, shape, dt=F32):
        t = self.es.enter_context(self.nc.psum_tensor(name, list(shape), dt))
        return Buf(self, name, t)

    def dram(self, name, shape, dt, kind="Internal"):
        t = self.nc.dram_tensor(name, list(shape), dt, kind=kind)
        return Buf(self, name, t.ap())

    def _deps(self, E, reads, writes, skip_self):
        deps = {}

        def need(tlv):
            if tlv is None:
                return
            tl, v = tlv
            if skip_self and tl is E.tl:
                return
            if E.seen.get(tl, 0) >= v:
                return
            if deps.get(tl, 0) < v:
                deps[tl] = v

        for r in reads:
            need(r.w)
        for w in writes:
            need(w.w)
            for tl, v in w.r.items():
                need((tl, v))
        return deps, need

    def op(self, eng, fn, reads=(), writes=()):
        if self.mute:
            return
        E = self.engs[eng]
        deps, _ = self._deps(E, reads, writes, skip_self=(eng == "pe"))
        for tl, v in deps.items():
            E.seen[tl] = v
        E.tl.count += 1
        val = E.tl.count
        E.ops.append(([(tl.sem, v) for tl, v in deps.items()], fn, (E.tl.sem, 1)))
        for r in reads:
            r.r[E.tl] = val
        for w in writes:
            w.w = (E.tl, val)
            w.r = {}

    def dma(self, q, out_ap, in_ap, reads, writes, owner, group=False):
        if self.mute:
            return
        E = self.engs[q]
        ch = owner.chan()
        deps, need = self._deps(E, reads, writes, skip_self=False)
        if (not group) and ch.count > 0:
            need((ch, ch.count))
        for tl, v in deps.items():
            E.seen[tl] = v
        ch.count += 16
        val = ch.count
        E.ops.append(([(tl.sem, v) for tl, v in deps.items()],
                      lambda h: h.dma_start(out=out_ap, in_=in_ap), (ch.sem, 16)))
        for r in reads:
            r.r[ch] = val
        for w in writes:
            w.w = (ch, val)
            w.r = {}

    def custom(self, eng, fn, reads, writes, tl):
        E = self.engs[eng]
        deps, need = self._deps(E, reads, writes, skip_self=False)
        for t2, v in deps.items():
            E.seen[t2] = v
        tl.count += tl.inc
        val = tl.count
        E.ops.append(([(t2.sem, v) for t2, v in deps.items()], fn, (tl.sem, tl.inc)))
        for r in reads:
            r.r[tl] = val
        for w in writes:
            w.w = (tl, val)
            w.r = {}

    def barrier(self):
        tls = [e.tl for e in self.engs.values() if e.tl.count] + \
              [t for t in self.chans if t.count]
        for E in self.engs.values():
            waits = []
            for tl in tls:
                if tl is E.tl:
                    continue
                if E.seen.get(tl, 0) < tl.count:
                    E.seen[tl] = tl.count
                    waits.append((tl.sem, tl.count))
            if waits:
                E.ops.append((waits, None, None))

    def finish(self, final_res):
        E = self.engs["sp"]
        waits = {}
        for r in final_res:
            if r.w is not None:
                tl, v = r.w
                waits[tl] = max(waits.get(tl, 0), v)
        for e in self.engs.values():
            if e.tl.count:
                waits[e.tl] = e.tl.count
        E.ops.append(([(tl.sem, v) for tl, v in waits.items()], None, None))
        with self.nc.Block() as block:
            def runner(name):
                def f(h):
                    for waits_, fn, inc in self.engs[name].ops:
                        for sem, v in waits_:
                            h.wait_ge(sem, v)
                        if fn is not None:
                            ins = fn(h)
                            ins.then_inc(inc[0], inc[1])
                return f
            block.tensor(runner("pe"))
            block.vector(runner("dve"))
            block.scalar(runner("act"))
            block.gpsimd(runner("pool"))
            block.sync(runner("sp"))
        self.es.close()


class Prog:
    def __init__(self, debug=False, stop_after=99, skip1=False, p2stop=99, nchunks=NT, cstop=99, have_nsa=False, nsa_tiles=8):
        self.have_nsa = have_nsa
        self.nsa_tiles = nsa_tiles
        self.nchunks = nchunks
        self.cstop = cstop
        self.skip1 = skip1
        self.p2stop = p2stop
        self.debug = debug
        self.stop_after = stop_after
        nc = bass.Bass("TRN2", target_bir_lowering=False)
        self.nc = nc
        kb = KB(nc)
        self.kb = kb
        self.inp = {}
        self.outs = []
        self.rr = 0

    def din(self, name, shape, dt=F32):
        b = self.kb.dram(name, shape, dt, kind="ExternalInput")
        self.inp[name] = b
        return b

    def dout(self, name, shape, dt=F32):
        b = self.kb.dram(name, shape, dt, kind="ExternalOutput")
        self.outs.append(b)
        return b

    def evac(self, out_ap, in_ap, reads, writes, scale=None):
        kb = self.kb
        self.rr += 1
        if self.rr % 2 == 0:
            kb.op("act", lambda h: h.activation(out=out_ap, in_=in_ap, func=AF.Copy),
                  reads, writes)
        else:
            kb.op("dve", lambda h: h.tensor_copy(out=out_ap, in_=in_ap), reads, writes)

    def cast(self, out_ap, in_ap, reads, writes):
        kb = self.kb
        self.rr += 1
        m = self.rr % 2
        if m == 0:
            kb.op("act", lambda h: h.activation(out=out_ap, in_=in_ap, func=AF.Copy),
                  reads, writes)
        elif m == 1:
            kb.op("dve", lambda h: h.tensor_copy(out=out_ap, in_=in_ap), reads, writes)
        else:
            kb.op("pool", lambda h: h.tensor_copy(out=out_ap, in_=in_ap), reads, writes)

    def build(self):
        kb = self.kb
        nc = self.nc
        dbg = self.debug
        x = self.din("x", [S, D])
        wcore = self.din("wcore", [D, WCOLS])
        normw = self.din("normw", [128, D])
        ident_in = self.din("ident", [128, 128])
        UT = kb.dram("UT", [12 * 128, S], F32)
        VML = kb.dram("VML", [S, 512], F32)
        OML = kb.dram("OML", [S, 512], F32)
        ZML = kb.dram("ZML", [S, 512], F32)
        MISC = kb.dram("MISC", [S, 272], F32)
        ZNS = kb.dram("ZNS", [S, 512], F32)
        WBF = kb.dram("WBF", [128, 32, WCOLS], BF16)
        tm_dst = [VML, OML, ZML, MISC, ZNS]
        tm_cols = [(1536, 512), (2048, 512), (2560, 512), (3072, 272), (3344, 512)]

        ident_f = kb.sb("ident_f", [128, 128], F32)
        ident_b = kb.sb("ident_b", [128, 128], BF16)
        kb.dma("sp", ident_f[:], ident_in[:], [ident_in], [ident_f], ident_f)
        kb.op("dve", lambda h: h.tensor_copy(out=ident_b[:], in_=ident_f[:]), [ident_f], [ident_b])

        psf = [kb.ps("psf%d" % i, [128, 512], F32) for i in range(6)]
        psb = [kb.ps("psb%d" % i, [128, 1024], BF16) for i in range(2)]

        with ExitStack() as es1:
            nw = kb.sb("nw", [128, D], F32, es1)
            kb.dma("sp", nw[:], normw[:], [normw], [nw], nw)
            hT = kb.sb("hT", [128, 32, 1024], BF16, es1)
            xt = [kb.sb("xt%d" % i, [128, D], F32, es1) for i in range(2)]
            xs = [kb.sb("xs%d" % i, [128, D], BF16, es1) for i in range(1)]
            ss = [kb.sb("ss%d" % i, [128, 1], F32, es1) for i in range(2)]
            rstd = [kb.sb("rstd%d" % i, [128, 1], F32, es1) for i in range(2)]
            wst = [kb.sb("wst%d" % i, [128, 2, 512], F32, es1) for i in range(3)]
            wbf = [kb.sb("wbf%d" % i, [128, 32, 512], BF16, es1) for i in range(2)]
            ost = [kb.sb("ost%d" % i, [128, 512], F32, es1) for i in range(3)]
            wcv = wcore.t.rearrange("(k p) c -> p k c", p=128)
            xtile = 0
            wld = 0
            ostc = 0
            psc = 0
            for tt in range(0 if self.skip1 else 4):
                for i8 in range(8):
                    ti = tt * 8 + i8
                    xb = xt[xtile % 2]
                    xsb = xs[0]
                    ssb = ss[xtile % 2]
                    rsb = rstd[xtile % 2]
                    xtile += 1
                    kb.dma("sp", xb[:], x[ti * 128:(ti + 1) * 128, :], [x], [xb], xb)
                    kb.op("act", lambda h, xb=xb, ssb=ssb, xsb=xsb: h.activation(
                        out=xsb[:], in_=xb[:], func=AF.Square, accum_out=ssb[:]),
                        [xb], [xsb, ssb])
                    kb.op("dve", lambda h, ssb=ssb, rsb=rsb: h.tensor_scalar(
                        out=rsb[:], in0=ssb[:], scalar1=1.0 / D, scalar2=EPS,
                        op0=ALU.mult, op1=ALU.add), [ssb], [rsb])
                    kb.op("act", lambda h, rsb=rsb: h.activation(
                        out=rsb[:], in_=rsb[:], func=AF.Sqrt), [rsb], [rsb])
                    kb.op("dve", lambda h, rsb=rsb: h.reciprocal(out=rsb[:], in_=rsb[:]),
                          [rsb], [rsb])
                    kb.op("dve", lambda h, xb=xb, rsb=rsb, xsb=xsb: h.scalar_tensor_tensor(
                        out=xsb[:], in0=xb[:], scalar=rsb[:], in1=nw[:],
                        op0=ALU.mult, op1=ALU.mult), [xb, rsb, nw], [xsb])
                    for j4 in range(4):
                        pb = psb[j4 % 2]
                        for k8 in range(8):
                            kk = j4 * 8 + k8
                            kb.op("pe", lambda h, pb=pb, k8=k8, kk=kk, xsb=xsb: h.transpose(
                                out=pb[:, k8 * 128:(k8 + 1) * 128],
                                in_=xsb[:, kk * 128:(kk + 1) * 128], identity=ident_b[:]),
                                [xsb, ident_b], [pb])
                        self.evac(hT[:, j4 * 8:(j4 + 1) * 8, i8 * 128:(i8 + 1) * 128],
                                  pb[:].rearrange("p (k t) -> p k t", k=8),
                                  [pb], [hT.sub(i8)])
                hT_all = [hT.sub(i) for i in range(8)]

                def load_w(c0, ncol, tt=tt):
                    nonlocal wld
                    wb = wbf[wld % 2]
                    wld += 1
                    if tt > 0:
                        for q in range(4):
                            kb.dma("sp", wb[:, q * 8:(q + 1) * 8, :ncol],
                                   WBF[:, q * 8:(q + 1) * 8, c0:c0 + ncol],
                                   [WBF.sub(c0)], [wb], wb, group=(q > 0))
                        return wb
                    for kq in range(16):
                        st = wst[kq % 3]
                        kb.dma("sp", st[:, :, :ncol], wcv[:, kq * 2:(kq + 1) * 2, c0:c0 + ncol],
                               [wcore], [st], st)
                        self.cast(wb[:, kq * 2:(kq + 1) * 2, :ncol], st[:, :, :ncol], [st], [wb])
                    for q in range(4):
                        kb.dma("sp", WBF[:, q * 8:(q + 1) * 8, c0:c0 + ncol], wb[:, q * 8:(q + 1) * 8, :ncol],
                               [wb], [WBF.sub(c0)], wb, group=(q > 0))
                    return wb

                for cg in range(3):
                    wb = load_w(cg * 512, 512)
                    for c4 in range(4):
                        ct = cg * 4 + c4
                        for th in range(2):
                            pt = psf[psc % 6]
                            psc += 1
                            for k in range(32):
                                kb.op("pe", lambda h, pt=pt, wb=wb, c4=c4, k=k, th=th: h.matmul(
                                    pt[:], lhsT=wb[:, k, c4 * 128:(c4 + 1) * 128],
                                    rhs=hT[:, k, th * 512:(th + 1) * 512],
                                    start=(k == 0), stop=(k == 31)),
                                    [wb] + hT_all, [pt])
                            ob = ost[ostc % 3]
                            ostc += 1
                            self.evac(ob[:], pt[:], [pt], [ob])
                            t0 = tt * 1024 + th * 512
                            kb.dma("pool", UT[ct * 128:(ct + 1) * 128, t0:t0 + 512], ob[:],
                                   [ob], [UT.sub((ct, tt))], ob)
                for gi, (c0, ncol) in enumerate(tm_cols):
                    wb = load_w(c0, ncol)
                    for i8 in range(8):
                        pt = psf[psc % 6]
                        psc += 1
                        for k in range(32):
                            kb.op("pe", lambda h, pt=pt, wb=wb, k=k, i8=i8, ncol=ncol: h.matmul(
                                pt[:, :ncol], lhsT=hT[:, k, i8 * 128:(i8 + 1) * 128],
                                rhs=wb[:, k, :ncol], start=(k == 0), stop=(k == 31)),
                                [wb] + hT_all, [pt])
                        ob = ost[ostc % 3]
                        ostc += 1
                        self.evac(ob[:, :ncol], pt[:, :ncol], [pt], [ob])
                        t0 = tt * 1024 + i8 * 128
                        kb.dma("pool", tm_dst[gi][t0:t0 + 128, :], ob[:, :ncol],
                               [ob], [tm_dst[gi].sub(tt)], ob)

        kb.barrier()
        YT = kb.dram("YT", [8, 1024, 512], BF16)
        self.YT = YT
        if self.stop_after >= 2:
            self.phase2(UT, VML, OML, ZML, MISC, YT, psf, psb, ident_f, ident_b)

        if dbg and self.stop_after < 2:
            with ExitStack() as esd:
                db = kb.sb("dbgbuf", [128, 4096], F32, esd)
                o_ut = self.dout("o_ut", [12 * 128, S])
                ut_all = [UT.sub((ct, tt)) for ct in range(12) for tt in range(4)]
                for ct in range(12):
                    kb.dma("sp", db[:], UT[ct * 128:(ct + 1) * 128, :], ut_all, [db], db)
                    kb.dma("sp", o_ut[ct * 128:(ct + 1) * 128, :], db[:], [db], [o_ut], db)
                for nm, src, ncol in (("o_vml", VML, 512), ("o_misc", MISC, 272)):
                    o = self.dout(nm, [S, ncol])
                    srcs = [src.sub(tt) for tt in range(4)]
                    for i in range(NT):
                        kb.dma("sp", db[:, :ncol], src[i * 128:(i + 1) * 128, :], srcs, [db], db)
                        kb.dma("sp", o[i * 128:(i + 1) * 128, :], db[:, :ncol], [db], [o], db)

        kb.mute = False
        kb.barrier()
        if self.have_nsa and self.stop_after >= 3:
            self.phase3(UT, MISC, ZNS, YT, psf, psb, ident_b)
            kb.barrier()
        if self.stop_after >= 4:
            self.phase4(YT, psf, psb, ident_b)
            kb.barrier()
        if dbg and self.stop_after >= 2:
            with ExitStack() as esd:
                db2 = kb.sb("dbg2", [128, 8, 2048], BF16, esd)
                o_yt = self.dout("o_yt", [1024, S], BF16)
                yt_all = [YT.sub((k_, t_)) for k_ in ("ml", "ns") for t_ in range(8)]
                for tg in range(8):
                    kb.dma("sp", db2[:, :, 0:512], YT[tg].rearrange("(j p) t -> p j t", p=128),
                           yt_all, [db2], db2)
                    kb.dma("sp", o_yt[:, tg * 512:(tg + 1) * 512].rearrange("(j p) t -> p j t", p=128),
                           db2[:, :, 0:512], [db2], [o_yt], db2)
        kb.finish(self.outs)
        return nc


    def phase2(self, UT, VML, OML, ZML, MISC, YT, psf, psb, ident_f, ident_b):
        kb = self.kb
        LNC = float(np.log(128.0 ** -0.5))
        convw_in = self.din("convw", [128, 16])
        gb_in = self.din("gbias", [128, 4])
        hnw_in = self.din("hnw", [128, 512])
        tri_in = self.din("tri", [128, 128])
        ut_all = [UT.sub((ct, tt)) for ct in range(12) for tt in range(4)]
        misc_all = [MISC.sub(tt) for tt in range(4)]
        vml_all = [VML.sub(tt) for tt in range(4)]
        oml_all = [OML.sub(tt) for tt in range(4)]
        zml_all = [ZML.sub(tt) for tt in range(4)]
        with ExitStack() as es:
            convw = kb.sb("convw_s", [128, 16], F32, es)
            gb = kb.sb("gb_s", [128, 4], F32, es)
            hnw = kb.sb("hnw_s", [128, 512], F32, es)
            tri = kb.sb("tri_s", [128, 128], F32, es)
            ones = kb.sb("ones_s", [128, 128], F32, es)
            kb.dma("sp", convw[:], convw_in[:], [convw_in], [convw], convw)
            kb.dma("sp", gb[:], gb_in[:], [gb_in], [gb], gb)
            kb.dma("sp", hnw[:], hnw_in[:], [hnw_in], [hnw], hnw)
            kb.dma("sp", tri[:], tri_in[:], [tri_in], [tri], tri)
            kb.op("pool", lambda h: h.memset(ones[:], 1.0), [], [ones])
            G = kb.sb("G_s", [128, NT, 4], F32, es)
            for n in range(NT):
                kb.dma("sp", G[:, n, :], MISC[n * 128:(n + 1) * 128, 256:260],
                       misc_all, [G], G, group=True)
            LI = kb.sb("LI_s", [128, NT, 2], F32, es)
            LF = kb.sb("LF_s", [128, NT, 2], F32, es)
            for j in range(2):
                kb.op("dve", lambda h, j=j: h.tensor_scalar(
                    out=LI[:, :, j], in0=G[:, :, j], scalar1=gb[:, j:j + 1], scalar2=LNC,
                    op0=ALU.add, op1=ALU.add), [G, gb], [LI])
                kb.op("dve", lambda h, j=j: h.tensor_scalar(
                    out=LF[:, :, j], in0=G[:, :, 2 + j], scalar1=gb[:, 2 + j:3 + j], scalar2=None,
                    op0=ALU.add), [G, gb], [LF])
            kb.op("act", lambda h: h.activation(out=LF[:], in_=LF[:], func=AF.Exp, scale=-1.0),
                  [LF], [LF])
            kb.op("dve", lambda h: h.tensor_scalar_add(out=LF[:], in0=LF[:], scalar1=1.0), [LF], [LF])
            kb.op("act", lambda h: h.activation(out=LF[:], in_=LF[:], func=AF.Ln), [LF], [LF])
            kb.op("dve", lambda h: h.tensor_scalar_mul(out=LF[:], in0=LF[:], scalar1=-1.0), [LF], [LF])

            if self.p2stop < 1:
                return
            LFHL = kb.sb("LFHL_s", [128, NT, 2, 2], BF16, es)
            LFH32 = kb.sb("LFH32_s", [128, NT, 2], F32, es)
            LFL32 = kb.sb("LFL32_s", [128, NT, 2], F32, es)
            tri_b = kb.sb("tri_b", [128, 128], BF16, es)
            ones_b = kb.sb("ones_b", [128, 128], BF16, es)
            kb.op("dve", lambda h: h.tensor_copy(out=tri_b[:], in_=tri[:]), [tri], [tri_b])
            kb.op("dve", lambda h: h.tensor_copy(out=ones_b[:], in_=ones[:]), [ones], [ones_b])
            kb.op("dve", lambda h: h.tensor_copy(out=LFHL[:, :, :, 0], in_=LF[:]), [LF], [LFHL])
            kb.op("dve", lambda h: h.tensor_copy(out=LFH32[:], in_=LFHL[:, :, :, 0]), [LFHL], [LFH32])
            kb.op("dve", lambda h: h.tensor_tensor(out=LFL32[:], in0=LF[:], in1=LFH32[:],
                                                   op=ALU.subtract), [LF, LFH32], [LFL32])
            kb.op("dve", lambda h: h.tensor_copy(out=LFHL[:, :, :, 1], in_=LFL32[:]), [LFL32], [LFHL])
            P = kb.sb("P_s", [128, S + 4], F32, es)
            acc = kb.sb("acc_s", [128, S], F32, es)
            qc = [kb.sb("qc%d" % h_, [128, S], F32, es) for h_ in range(2)]
            kT = [kb.sb("kT%d" % h_, [128, S], BF16, es) for h_ in range(2)]
            kb.op("pool", lambda h: h.memset(P[:, 0:4], 0.0), [], [P.sub("pad")])
            for ci in range(4):
                kb.dma("sp", P[:, 4:4 + S], UT[ci * 128:(ci + 1) * 128, :], ut_all, [P], P)
                for j in range(4):
                    if j == 0:
                        kb.op("dve", lambda h, ci=ci: h.tensor_scalar(
                            out=acc[:], in0=P[:, 1:1 + S], scalar1=convw[:, ci * 4:ci * 4 + 1],
                            scalar2=None, op0=ALU.mult), [P, P.sub("pad"), convw], [acc])
                    else:
                        kb.op("dve", lambda h, ci=ci, j=j: h.scalar_tensor_tensor(
                            out=acc[:], in0=P[:, 1 + j:1 + j + S],
                            scalar=convw[:, ci * 4 + j:ci * 4 + j + 1], in1=acc[:],
                            op0=ALU.mult, op1=ALU.add), [P, P.sub("pad"), convw, acc], [acc])
                dst = qc[ci] if ci < 2 else kT[ci - 2]
                kb.op("act", lambda h: h.activation(out=P[:, 4:4 + S], in_=acc[:], func=AF.Exp,
                                                    scale=-1.0), [acc], [P])
                kb.op("dve", lambda h: h.tensor_scalar_add(out=P[:, 4:4 + S], in0=P[:, 4:4 + S],
                                                           scalar1=1.0), [P], [P])
                kb.op("dve", lambda h: h.reciprocal(out=P[:, 4:4 + S], in_=P[:, 4:4 + S]), [P], [P])
                kb.op("dve", lambda h, dst=dst: h.tensor_tensor(
                    out=dst[:], in0=acc[:], in1=P[:, 4:4 + S], op=ALU.mult), [acc, P], [dst])

            if self.p2stop < 2:
                return
            vst = kb.sb("vst_s", [128, 512], F32, es)
            vp = [kb.sb("vp%d" % h_, [128, NT, 257], BF16, es) for h_ in range(2)]
            for h_ in range(2):
                kb.op("pool", lambda h, h_=h_: h.memset(vp[h_][:, :, 256:257], 1.0), [], [vp[h_].sub("one")])
            for n in range(NT):
                kb.dma("sp", vst[:], VML[n * 128:(n + 1) * 128, :], vml_all, [vst], vst)
                for h_ in range(2):
                    self.cast(vp[h_][:, n, 0:256], vst[:, h_ * 256:(h_ + 1) * 256], [vst], [vp[h_]])

            if self.p2stop < 3:
                return
            C = [kb.sb("C%d" % h_, [128, 257], F32, es) for h_ in range(2)]
            Cb = [kb.sb("Cb%d" % h_, [128, 257], BF16, es) for h_ in range(2)]
            for h_ in range(2):
                kb.op("pool", lambda h, h_=h_: h.memset(C[h_][:], 0.0), [], [C[h_]])
                kb.op("pool", lambda h, h_=h_: h.memset(Cb[h_][:], 0.0), [], [Cb[h_]])
            LFB = [kb.sb("LFB%d" % i, [128, 128], BF16, es) for i in range(2)]
            LFBl = [kb.sb("LFBl%d" % i, [128, 128], BF16, es) for i in range(2)]
            eBr = [kb.sb("eBr%d" % i, [128, 128], F32, es) for i in range(2)]
            eBL = [kb.sb("eBL%d" % i, [128, 1], F32, es) for i in range(2)]
            E2 = [kb.sb("E2_%d" % i, [128, 128], F32, es) for i in range(2)]
            hraw = [kb.sb("hraw%d" % i, [128, 257], F32, es) for i in range(2)]
            wk = [kb.sb("wk%d" % i, [128, 1], F32, es) for i in range(2)]
            qs = [kb.sb("qs%d" % i, [128, 128], BF16, es) for i in range(2)]
            scT = [kb.sb("scT%d" % i, [128, 128], BF16, es) for i in range(2)]
            kh = [kb.sb("kh%d" % i, [128, 128], BF16, es) for i in range(2)]
            dm = [kb.sb("dm%d" % i, [128, 1], F32, es) for i in range(2)]
            hn = [kb.sb("hn%d" % i, [128, 256], F32, es) for i in range(2)]
            sq = [kb.sb("sq%d" % i, [128, 256], F32, es) for i in range(2)]
            ssq = [kb.sb("ssq%d" % i, [128, 1], F32, es) for i in range(2)]
            ot = [kb.sb("ot%d" % i, [128, 512], F32, es) for i in range(2)]
            zt = [kb.sb("zt%d" % i, [128, 512], F32, es) for i in range(2)]
            eo = [kb.sb("eo%d" % i, [128, 256], F32, es) for i in range(2)]
            ez = [kb.sb("ez%d" % i, [128, 256], F32, es) for i in range(2)]
            yb = [kb.sb("yb%d" % i, [128, 256], BF16, es) for i in range(2)]
            yst = [kb.sb("yst%d" % i, [128, 4, 512], BF16, es) for i in range(2)]
            it = 0
            GT = kb.sb("GT_s", [128, NT, 512], BF16, es)
            gtmp = [kb.sb("gtmp%d" % i, [128, 512], F32, es) for i in range(2)]
            nck = self.nchunks if self.p2stop >= 4 else 0
            for c in range(nck):
                o_t = ot[c % 2]
                kb.dma("sp", o_t[:], OML[c * 128:(c + 1) * 128, :], oml_all, [o_t], o_t)
                kb.op("act", lambda h, o_t=o_t, c=c: h.activation(out=GT[:, c, :], in_=o_t[:], func=AF.Sigmoid),
                      [o_t], [GT.sub(c)])
            for c in range(nck):
                z_t = zt[c % 2]
                gt_ = gtmp[c % 2]
                kb.dma("sp", z_t[:], ZML[c * 128:(c + 1) * 128, :], zml_all, [z_t], z_t)
                kb.op("act", lambda h, z_t=z_t, gt_=gt_: h.activation(out=gt_[:], in_=z_t[:], func=AF.Silu),
                      [z_t], [gt_])
                kb.op("dve", lambda h, gt_=gt_, c=c: h.tensor_tensor(out=GT[:, c, :], in0=GT[:, c, :], in1=gt_[:],
                                                                    op=ALU.mult), [gt_, GT.sub(c)], [GT.sub(c)])
            for c in range(nck):
                o_t = ot[c % 2]
                z_t = zt[c % 2]
                ys = yst[(c // 4) % 2]

                def head_steps(c, h_, o_t, z_t, ys):
                    i2 = h_
                    cs = slice(c * 128, (c + 1) * 128)
                    pA = psf[0 + i2]
                    pO = psf[2 + i2]
                    pC = psf[4 + i2]
                    pT = psb[i2]
                    lfc = LF[:, c, h_:h_ + 1]
                    lic = LI[:, c, h_:h_ + 1]
                    kb.op("dve", lambda h, i2=i2, c=c, h_=h_: h.tensor_scalar(
                        out=LFB[i2][:], in0=ones_b[:], scalar1=LFH32[:, c, h_:h_ + 1], scalar2=None,
                        op0=ALU.mult), [ones_b, LFH32], [LFB[i2]])
                    kb.op("dve", lambda h, i2=i2, c=c, h_=h_: h.tensor_scalar(
                        out=LFBl[i2][:], in0=ones_b[:], scalar1=LFL32[:, c, h_:h_ + 1], scalar2=None,
                        op0=ALU.mult), [ones_b, LFL32], [LFBl[i2]])
                    kb.op("pe", lambda h, i2=i2, pA=pA: h.matmul(
                        pA[:, 0:128], lhsT=LFB[i2][:], rhs=tri_b[:], start=True, stop=False),
                        [LFB[i2], tri_b], [pA])
                    kb.op("pe", lambda h, i2=i2, pA=pA: h.matmul(
                        pA[:, 0:128], lhsT=LFBl[i2][:], rhs=tri_b[:], start=False, stop=True),
                        [LFBl[i2], tri_b], [pA])
                    kb.op("pe", lambda h, i2=i2, pA=pA: h.matmul(
                        pA[:, 128:256], lhsT=tri_b[:], rhs=LFB[i2][:], start=True, stop=False),
                        [LFB[i2], tri_b], [pA])
                    kb.op("pe", lambda h, i2=i2, pA=pA: h.matmul(
                        pA[:, 128:256], lhsT=tri_b[:], rhs=LFBl[i2][:], start=False, stop=True),
                        [LFBl[i2], tri_b], [pA])
                    kb.op("act", lambda h, i2=i2, pA=pA: h.activation(
                        out=eBr[i2][:], in_=pA[:, 0:128], func=AF.Exp), [pA], [eBr[i2]])
                    kb.op("act", lambda h, i2=i2, pA=pA, lic=lic: h.activation(
                        out=E2[i2][:], in_=pA[:, 128:256], func=AF.Exp, scale=-1.0, bias=lic),
                        [pA, LI], [E2[i2]])
                    kb.op("dve", lambda h, i2=i2: h.tensor_tensor(
                        out=wk[i2][:], in0=E2[i2][:, 0:1], in1=eBr[i2][:, 127:128], op=ALU.mult),
                        [E2[i2], eBr[i2]], [wk[i2]])
                    if self.cstop <= 1:
                        kb.mute = True
                    yield
                    kb.op("dve", lambda h, i2=i2, h_=h_, cs=cs: h.tensor_tensor(
                        out=qs[i2][:], in0=qc[h_][:, cs], in1=eBr[i2][:], op=ALU.mult),
                        [qc[h_], eBr[i2]], [qs[i2]])
                    kb.op("pe", lambda h, i2=i2, h_=h_, cs=cs, pA=pA: h.matmul(
                        pA[:, 256:384], lhsT=kT[h_][:, cs], rhs=qs[i2][:], start=True, stop=True),
                        [kT[h_], qs[i2]], [pA])
                    kb.op("dve", lambda h, i2=i2, pA=pA: h.scalar_tensor_tensor(
                        out=scT[i2][:], in0=pA[:, 256:384], scalar=E2[i2][:, 0:1], in1=tri[:],
                        op0=ALU.mult, op1=ALU.mult), [pA, E2[i2], tri], [scT[i2]])
                    if self.cstop <= 2:
                        kb.mute = True
                    yield
                    kb.op("pe", lambda h, h_=h_, cs=cs, pT=pT: h.transpose(
                        out=pT[:, 0:128], in_=kT[h_][:, cs], identity=ident_b[:]),
                        [kT[h_], ident_b], [pT])
                    kb.op("act", lambda h, i2=i2, pT=pT: h.activation(
                        out=kh[i2][:], in_=pT[:, 0:128], func=AF.Copy, scale=wk[i2][:]),
                        [pT, wk[i2]], [kh[i2]])
                    if self.cstop <= 3:
                        kb.mute = True
                    yield
                    kb.op("pe", lambda h, i2=i2, h_=h_, pO=pO: h.matmul(
                        pO[:, 0:257], lhsT=qs[i2][:], rhs=Cb[h_][:], start=True, stop=False),
                        [qs[i2], Cb[h_]], [pO])
                    kb.op("pe", lambda h, i2=i2, h_=h_, c=c, pO=pO: h.matmul(
                        pO[:, 0:257], lhsT=scT[i2][:], rhs=vp[h_][:, c, :], start=False, stop=True),
                        [scT[i2], vp[h_], vp[h_].sub("one")], [pO])
                    yield
                    kb.op("pe", lambda h, i2=i2, h_=h_, c=c, pC=pC: h.matmul(
                        pC[:, 0:257], lhsT=kh[i2][:], rhs=vp[h_][:, c, :], start=True, stop=True),
                        [kh[i2], vp[h_], vp[h_].sub("one")], [pC])
                    kb.op("dve", lambda h, i2=i2, h_=h_, pC=pC: h.scalar_tensor_tensor(
                        out=C[h_][:], in0=C[h_][:], scalar=eBr[i2][:, 127:128], in1=pC[:, 0:257],
                        op0=ALU.mult, op1=ALU.add), [C[h_], eBr[i2], pC], [C[h_]])
                    kb.op("act", lambda h, h_=h_: h.activation(out=Cb[h_][:], in_=C[h_][:], func=AF.Copy),
                          [C[h_]], [Cb[h_]])
                    if self.cstop <= 4:
                        kb.mute = True
                    yield
                    kb.op("act", lambda h, i2=i2, pO=pO: h.activation(
                        out=hraw[i2][:], in_=pO[:, 0:257], func=AF.Copy), [pO], [hraw[i2]])
                    kb.op("dve", lambda h, i2=i2: h.tensor_scalar(
                        out=wk[i2][:], in0=hraw[i2][:, 256:257], scalar1=-1.0, scalar2=1.0,
                        op0=ALU.mult, op1=ALU.max), [hraw[i2]], [wk[i2]])
                    kb.op("dve", lambda h, i2=i2: h.scalar_tensor_tensor(
                        out=dm[i2][:], in0=hraw[i2][:, 256:257], scalar=1.0, in1=wk[i2][:],
                        op0=ALU.max, op1=ALU.max), [hraw[i2], wk[i2]], [dm[i2]])
                    kb.op("dve", lambda h, i2=i2: h.reciprocal(out=dm[i2][:], in_=dm[i2][:]),
                          [dm[i2]], [dm[i2]])
                    kb.op("act", lambda h, i2=i2: h.activation(
                        out=sq[i2][:], in_=hraw[i2][:, 0:256], func=AF.Square, scale=dm[i2][:],
                        accum_out=ssq[i2][:]), [hraw[i2], dm[i2]], [sq[i2], ssq[i2]])
                    kb.op("dve", lambda h, i2=i2: h.tensor_scalar(
                        out=ssq[i2][:], in0=ssq[i2][:], scalar1=1.0 / 256, scalar2=EPS,
                        op0=ALU.mult, op1=ALU.add), [ssq[i2]], [ssq[i2]])
                    kb.op("act", lambda h, i2=i2: h.activation(
                        out=ssq[i2][:], in_=ssq[i2][:], func=AF.Ln), [ssq[i2]], [ssq[i2]])
                    kb.op("act", lambda h, i2=i2: h.activation(
                        out=ssq[i2][:], in_=ssq[i2][:], func=AF.Exp, scale=-0.5), [ssq[i2]], [ssq[i2]])
                    kb.op("dve", lambda h, i2=i2: h.tensor_tensor(
                        out=ssq[i2][:], in0=ssq[i2][:], in1=dm[i2][:], op=ALU.mult),
                        [ssq[i2], dm[i2]], [ssq[i2]])
                    if self.cstop <= 5:
                        kb.mute = True
                    yield
                    hs = slice(h_ * 256, (h_ + 1) * 256)
                    kb.op("dve", lambda h, i2=i2, hs=hs: h.scalar_tensor_tensor(
                        out=hn[i2][:], in0=hraw[i2][:, 0:256], scalar=ssq[i2][:], in1=hnw[:, hs],
                        op0=ALU.mult, op1=ALU.mult), [hraw[i2], ssq[i2], hnw], [hn[i2]])
                    kb.op("dve", lambda h, i2=i2, c=c, hs=hs: h.tensor_tensor(
                        out=yb[i2][:], in0=hn[i2][:], in1=GT[:, c, hs], op=ALU.mult),
                        [hn[i2], GT.sub(c)], [yb[i2]])
                    if self.cstop <= 6:
                        kb.mute = True
                    yield
                    for j in range(2):
                        kb.op("pe", lambda h, i2=i2, j=j, pT=pT: h.transpose(
                            out=pT[:, 256 + j * 128:256 + (j + 1) * 128],
                            in_=yb[i2][:, j * 128:(j + 1) * 128], identity=ident_b[:]),
                            [yb[i2], ident_b], [pT])
                    self.evac(ys[:, h_ * 2:h_ * 2 + 2, (c % 4) * 128:(c % 4 + 1) * 128],
                              pT[:, 256:512].rearrange("p (j t) -> p j t", j=2), [pT], [ys])
                gens = [head_steps(c, 0, o_t, z_t, ys), head_steps(c, 1, o_t, z_t, ys)]
                while gens:
                    for g_ in list(gens):
                        try:
                            next(g_)
                        except StopIteration:
                            gens.remove(g_)
                if c % 4 == 3:
                    kb.dma("pool", YT[c // 4, 0:512, :].rearrange("(j p) t -> p j t", p=128),
                           ys[:], [ys], [YT.sub(("ml", c // 4))], ys)


    def phase4(self, YT, psf, psb, ident_b):
        kb = self.kb
        RG = [[0, 1, 2, 3], [4, 5, 6, 7]]
        wq_in = self.din("wq", [D, 1024])
        gq_in = self.din("gq", [D, 1024])
        pq_in = self.din("pq", [256, 1024])
        pT_in = self.din("pT", [256, S])
        xq_in = self.din("xq", [S, 1024])
        out = self.dout("out", [S, 1024])
        YG = kb.dram("YG", [8, 4 * 1024, 512], BF16)
        XT = kb.dram("XT", [8, 1024, 512], BF16)
        XG = kb.dram("XG", [8, 4 * 1024, 512], BF16)
        XN = kb.dram("XN", [S, 1024], F32)
        cc = TL(kb.new_sem("cc"), 1)
        with ExitStack() as es:
            if not self.have_nsa:
                zt_ = kb.sb("zeros_b", [128, 4, 512], BF16, es)
                kb.op("pool", lambda h: h.memset(zt_[:], 0.0), [], [zt_])
                for tg in range(8):
                    kb.dma("sp", YT[tg, 512:1024, :].rearrange("(j p) t -> p j t", p=128),
                           zt_[:], [zt_], [YT.sub(("ns", tg))], zt_)
            for tg in range(8):
                kb.custom("pool", lambda h, tg=tg: h.collective_compute(
                    "AllGather", ALU.bypass, replica_groups=RG, ins=[YT[tg]], outs=[YG[tg]]),
                    [YT.sub(("ml", tg)), YT.sub(("ns", tg))], [YG.sub(tg)], cc)
            wbf = kb.sb("w4bf", [128, 32, 1024], BF16, es)
            wst = [kb.sb("w4st%d" % i, [128, 1, 1024], F32, es) for i in range(3)]
            act_t = [kb.sb("a4t%d" % i, [128, 32, 512], BF16, es) for i in range(2)]
            xt_ = [kb.sb("x4t%d" % i, [128, 512], F32, es) for i in range(2)]
            xn = [kb.sb("xn4_%d" % i, [128, 512], F32, es) for i in range(2)]
            xnb = [kb.sb("xnb4_%d" % i, [128, 512], BF16, es) for i in range(2)]
            xTs = [kb.sb("xTs%d" % i, [128, 8, 512], BF16, es) for i in range(2)]
            sg = [kb.sb("sg4_%d" % i, [128, 512], F32, es) for i in range(2)]
            pqb = kb.sb("pqb", [128, 2, 1024], BF16, es)
            pTb = kb.sb("pTb", [128, 2, S], BF16, es)

            def load_w(src):
                v = src.t.rearrange("(k p) c -> p k c", p=128)
                for kq in range(32):
                    st = wst[kq % 3]
                    kb.dma("sp", st[:], v[:, kq:kq + 1, :], [src], [st], st)
                    self.cast(wbf[:, kq:kq + 1, :], st[:], [st], [wbf])

            load_w(wq_in)
            cnt = 0
            for tg in range(8):
                at = act_t[tg % 2]
                for q4 in range(4):
                    kb.dma("sp", at[:, q4 * 8:(q4 + 1) * 8, :],
                           YG[tg].rearrange("(k p) t -> p k t", p=128)[:, q4 * 8:(q4 + 1) * 8, :],
                           [YG.sub(tg)], [at], at, group=(q4 > 0))
                xs_ = xTs[tg % 2]
                for i4 in range(4):
                    t0 = tg * 512 + i4 * 128
                    for nt in range(2):
                        pt = psf[cnt % 4]
                        xb = xt_[cnt % 2]
                        xo = xn[cnt % 2]
                        xob = xnb[cnt % 2]
                        cnt += 1
                        kb.dma("sp", xb[:], xq_in[t0:t0 + 128, nt * 512:(nt + 1) * 512],
                               [xq_in], [xb], xb)
                        for k in range(32):
                            kb.op("pe", lambda h, pt=pt, at=at, k=k, i4=i4, nt=nt: h.matmul(
                                pt[:], lhsT=at[:, k, i4 * 128:(i4 + 1) * 128],
                                rhs=wbf[:, k, nt * 512:(nt + 1) * 512],
                                start=(k == 0), stop=(k == 31)), [at, wbf], [pt])
                        kb.op("dve", lambda h, pt=pt, xb=xb, xo=xo: h.tensor_tensor(
                            out=xo[:], in0=pt[:], in1=xb[:], op=ALU.add), [pt, xb], [xo])
                        kb.dma("pool", XN[t0:t0 + 128, nt * 512:(nt + 1) * 512], xo[:],
                               [xo], [XN], xo)
                        kb.op("act", lambda h, xo=xo, xob=xob: h.activation(
                            out=xob[:], in_=xo[:], func=AF.Copy), [xo], [xob])
                        pb = psb[cnt % 2]
                        for j in range(4):
                            kb.op("pe", lambda h, pb=pb, xob=xob, j=j: h.transpose(
                                out=pb[:, j * 128:(j + 1) * 128],
                                in_=xob[:, j * 128:(j + 1) * 128], identity=ident_b[:]),
                                [xob, ident_b], [pb])
                        self.evac(xs_[:, nt * 4:(nt + 1) * 4, i4 * 128:(i4 + 1) * 128],
                                  pb[:, 0:512].rearrange("p (j t) -> p j t", j=4), [pb], [xs_])
                kb.dma("pool", XT[tg].rearrange("(j p) t -> p j t", p=128),
                       xs_[:], [xs_], [XT.sub(tg)], xs_)
                kb.custom("pool", lambda h, tg=tg: h.collective_compute(
                    "AllGather", ALU.bypass, replica_groups=RG, ins=[XT[tg]], outs=[XG[tg]]),
                    [XT.sub(tg)], [XG.sub(tg)], cc)
            load_w(gq_in)
            for j in range(2):
                st = wst[j % 3]
                kb.dma("sp", st[:, 0, :], pq_in[j * 128:(j + 1) * 128, :], [pq_in], [st], st)
                self.cast(pqb[:, j, :], st[:, 0, :], [st], [pqb])
            for j in range(2):
                for q in range(4):
                    st = wst[(j * 4 + q) % 3]
                    kb.dma("sp", st[:, 0, :], pT_in[j * 128:(j + 1) * 128, q * 1024:(q + 1) * 1024],
                           [pT_in], [st], st)
                    self.cast(pTb[:, j, q * 1024:(q + 1) * 1024], st[:, 0, :], [st], [pTb])
            for tg in range(8):
                at = act_t[tg % 2]
                for q4 in range(4):
                    kb.dma("sp", at[:, q4 * 8:(q4 + 1) * 8, :],
                           XG[tg].rearrange("(k p) t -> p k t", p=128)[:, q4 * 8:(q4 + 1) * 8, :],
                           [XG.sub(tg)], [at], at, group=(q4 > 0))
                for i4 in range(4):
                    t0 = tg * 512 + i4 * 128
                    for nt in range(2):
                        pt = psf[cnt % 4]
                        pp = psf[4 + cnt % 2]
                        xb = xt_[cnt % 2]
                        xo = xn[cnt % 2]
                        sgt = sg[cnt % 2]
                        cnt += 1
                        kb.dma("sp", xb[:], XN[t0:t0 + 128, nt * 512:(nt + 1) * 512], [XN], [xb], xb)
                        for k in range(32):
                            kb.op("pe", lambda h, pt=pt, at=at, k=k, i4=i4, nt=nt: h.matmul(
                                pt[:], lhsT=at[:, k, i4 * 128:(i4 + 1) * 128],
                                rhs=wbf[:, k, nt * 512:(nt + 1) * 512],
                                start=(k == 0), stop=(k == 31)), [at, wbf], [pt])
                        for k in range(2):
                            kb.op("pe", lambda h, pp=pp, k=k, t0=t0, nt=nt: h.matmul(
                                pp[:], lhsT=pTb[:, k, t0:t0 + 128],
                                rhs=pqb[:, k, nt * 512:(nt + 1) * 512],
                                start=(k == 0), stop=(k == 1)), [pTb, pqb], [pp])
                        kb.op("act", lambda h, pt=pt, sgt=sgt: h.activation(
                            out=sgt[:], in_=pt[:], func=AF.Sigmoid), [pt], [sgt])
                        kb.op("dve", lambda h, pp=pp, sgt=sgt: h.tensor_tensor(
                            out=sgt[:], in0=sgt[:], in1=pp[:], op=ALU.mult), [sgt, pp], [sgt])
                        kb.op("pool", lambda h, sgt=sgt, xb=xb, xo=xo: h.tensor_tensor(
                            out=xo[:], in0=sgt[:], in1=xb[:], op=ALU.add), [sgt, xb], [xo])
                        kb.dma("pool", out[t0:t0 + 128, nt * 512:(nt + 1) * 512], xo[:],
                               [xo], [out], xo)

    def phase3(self, UT, MISC, ZNS, YT, psf, psb, ident_b):
        kb = self.kb
        ut_all = [UT.sub((ct, tt)) for ct in range(12) for tt in range(4)]
        misc_all = [MISC.sub(tt) for tt in range(4)]
        zns_all = [ZNS.sub(tt) for tt in range(4)]
        qnw_in = self.din("qnw", [128, 1])
        knw_in = self.din("knw", [128, 3])
        peK_in = self.din("peK", [128, 32])
        peV_in = self.din("peV", [128, 32])
        w1k_in = self.din("w1k", [D, 256])
        w2k_in = self.din("w2k", [256, 128])
        w1v_in = self.din("w1v", [D, 256])
        w2v_in = self.din("w2v", [256, 128])
        biasC_in = self.din("biasC", [8, 128, S])
        stripS_in = self.din("stripS", [4, 128, 1024])
        stripW_in = self.din("stripW", [4, 128, 1408])
        b31_in = self.din("b31", [128, 4])
        ov_in = self.din("ov", [256, 64])
        E_in = self.din("Esel", [64, S])
        keep_in = self.din("keep", [128, NT * 64])
        add_in = self.din("addc", [128, NT * 64])
        with ExitStack() as es:
            def small(name, src, shape):
                t = kb.sb(name, shape, F32, es)
                kb.dma("sp", t[:], src[:], [src], [t], t)
                return t
            qnw = small("qnw_s", qnw_in, [128, 1])
            knw = small("knw_s", knw_in, [128, 3])
            peK = small("peK_s", peK_in, [128, 32])
            peV = small("peV_s", peV_in, [128, 32])
            b31 = small("b31_s", b31_in, [128, 4])
            keep = small("keep_s", keep_in, [128, NT * 64])
            addc = small("add_s", add_in, [128, NT * 64])
            ones_b = kb.sb("ones3b", [128, 128], BF16, es)
            kb.op("pool", lambda h: h.memset(ones_b[:], 1.0), [], [ones_b])
            qn = [kb.sb("qn%d" % h_, [128, S], BF16, es) for h_ in range(4)]
            ksn = kb.sb("ksn", [128, S], BF16, es)
            kwn = kb.sb("kwn", [128, S], BF16, es)
            kcn = kb.sb("kcn", [128, 256], BF16, es)
            vcp = kb.sb("vcp", [128, 2, 193], BF16, es)
            vsp = kb.sb("vsp", [128, NT, 129], BF16, es)
            vwp = kb.sb("vwp", [128, NT, 129], BF16, es)
            stS = kb.sb("stS", [128, 4, 1024], BF16, es)
            stW = kb.sb("stW", [128, 4, 1408], BF16, es)
            Eb = kb.sb("Eb", [64, S], BF16, es)
            SG = kb.sb("SG", [128, NT, 12], F32, es)
            with ExitStack() as es2:
                X = kb.sb("X3", [128, S], F32, es2)
                sqb = [kb.sb("sqb%d" % i, [128, 512], BF16, es2) for i in range(2)]
                rr_ = [kb.sb("rr%d" % i, [128, 512], F32, es2) for i in range(2)]
                cnt = [0]

                def norm_block(src_ap, n, wcol, mul, add, srcbuf):
                    i2 = cnt[0] % 2
                    cnt[0] += 1
                    ps = psf[i2]
                    kb.op("act", lambda h: h.activation(out=sqb[i2][:, :n], in_=src_ap, func=AF.Square),
                          [srcbuf], [sqb[i2]])
                    kb.op("pe", lambda h: h.matmul(ps[:, :n], lhsT=ones_b[:], rhs=sqb[i2][:, :n],
                                                   start=True, stop=True), [ones_b, sqb[i2]], [ps])
                    kb.op("dve", lambda h: h.tensor_scalar(out=rr_[i2][:, :n], in0=ps[:, :n], scalar1=mul,
                                                           scalar2=add, op0=ALU.mult, op1=ALU.add),
                          [ps], [rr_[i2]])
                    kb.op("act", lambda h: h.activation(out=rr_[i2][:, :n], in_=rr_[i2][:, :n], func=AF.Ln),
                          [rr_[i2]], [rr_[i2]])
                    kb.op("act", lambda h: h.activation(out=rr_[i2][:, :n], in_=rr_[i2][:, :n], func=AF.Exp,
                                                        scale=-0.5), [rr_[i2]], [rr_[i2]])
                    return i2

                def fm_norm(ct, wcol, mul, add, dst, wbuf):
                    kb.dma("sp", X[:], UT[ct * 128:(ct + 1) * 128, :], ut_all, [X], X)
                    for cb in range(8):
                        cs = slice(cb * 512, (cb + 1) * 512)
                        i2 = norm_block(X[:, cs], 512, wcol, mul, add, X)
                        kb.op("dve", lambda h, i2=i2, cs=cs: h.scalar_tensor_tensor(
                            out=dst[:, cs], in0=X[:, cs], scalar=wcol, in1=rr_[i2][:],
                            op0=ALU.mult, op1=ALU.mult), [X, rr_[i2], wbuf], [dst])

                for h_ in range(4):
                    fm_norm(4 + h_, qnw[:, 0:1], 1.0, 128.0 * EPS, qn[h_], qnw)
                fm_norm(10, knw[:, 1:2], 1.0 / 128, EPS, ksn, knw)
                fm_norm(11, knw[:, 2:3], 1.0 / 128, EPS, kwn, knw)

                Rlo = kb.sb("Rlo", [128, 16, 256], BF16, es2)
                Rhi = kb.sb("Rhi", [128, 16, 256], BF16, es2)
                w1b = kb.sb("w1b", [128, 32, 256], BF16, es2)
                w1st = [kb.sb("w1st%d" % i, [128, 4, 256], F32, es2) for i in range(2)]
                w2st = kb.sb("w2st", [128, 2, 128], F32, es2)
                w2b = kb.sb("w2b", [128, 2, 128], BF16, es2)
                AT = [kb.sb("AT%d" % i, [128, 256], BF16, es2) for i in range(2)]
                tmpa = kb.sb("tmpa", [128, 256], F32, es2)
                kcf = kb.sb("kcf", [128, 256], F32, es2)
                ovst = kb.sb("ovst", [128, 2, 64], F32, es2)
                for which in range(2):
                    ct = 8 + which
                    pe_ = peK if which == 0 else peV
                    w1_in = w1k_in if which == 0 else w1v_in
                    w2_in = w2k_in if which == 0 else w2v_in
                    kb.dma("sp", X[:], UT[ct * 128:(ct + 1) * 128, :], ut_all, [X], X)
                    Xv = X.t[:].rearrange("d (c l) -> d l c", l=16)
                    for l in range(16):
                        kb.op("dve", lambda h, l=l, pe_=pe_, Xv=Xv: h.tensor_scalar(
                            out=Rlo[:, l, :], in0=Xv[:, l, :], scalar1=pe_[:, l:l + 1], scalar2=None,
                            op0=ALU.add), [X, pe_], [Rlo])
                        kb.op("pool", lambda h, l=l, pe_=pe_, Xv=Xv: h.tensor_scalar(
                            out=Rhi[:, l, :], in0=Xv[:, l, :], scalar1=pe_[:, 16 + l:17 + l], scalar2=None,
                            op0=ALU.add), [X, pe_], [Rhi])
                    w1v_ = w1_in.t.rearrange("(l d) j -> d l j", d=128)
                    for q in range(8):
                        st = w1st[q % 2]
                        kb.dma("sp", st[:], w1v_[:, q * 4:(q + 1) * 4, :], [w1_in], [st], st)
                        self.cast(w1b[:, q * 4:(q + 1) * 4, :], st[:], [st], [w1b])
                    kb.dma("sp", w2st[:], w2_in.t.rearrange("(j p) d -> p j d", p=128), [w2_in], [w2st], w2st)
                    kb.op("dve", lambda h: h.tensor_copy(out=w2b[:], in_=w2st[:]), [w2st], [w2b])
                    for jt in range(2):
                        ps = psf[2 + jt]
                        for l in range(32):
                            rhs = Rlo[:, l, 0:255] if l < 16 else Rhi[:, l - 16, 1:256]
                            kb.op("pe", lambda h, ps=ps, l=l, jt=jt, rhs=rhs: h.matmul(
                                ps[:, 0:255], lhsT=w1b[:, l, jt * 128:(jt + 1) * 128], rhs=rhs,
                                start=(l == 0), stop=(l == 31)), [w1b, Rlo, Rhi], [ps])
                        kb.op("act", lambda h, ps=ps: h.activation(out=tmpa[:, 0:255], in_=ps[:, 0:255],
                                                                   func=AF.Exp, scale=-1.0), [ps], [tmpa])
                        kb.op("dve", lambda h: h.tensor_scalar_add(out=tmpa[:, 0:255], in0=tmpa[:, 0:255],
                                                                   scalar1=1.0), [tmpa], [tmpa])
                        kb.op("dve", lambda h: h.reciprocal(out=tmpa[:, 0:255], in_=tmpa[:, 0:255]),
                              [tmpa], [tmpa])
                        kb.op("pool", lambda h, jt=jt: h.memset(AT[jt][:, 255:256], 0.0), [], [AT[jt].sub("z")])
                        kb.op("dve", lambda h, ps=ps, jt=jt: h.tensor_tensor(
                            out=AT[jt][:, 0:255], in0=tmpa[:, 0:255], in1=ps[:, 0:255], op=ALU.mult),
                            [tmpa, ps], [AT[jt]])
                    if which == 0:
                        ps = psf[4]
                        for jt in range(2):
                            kb.op("pe", lambda h, ps=ps, jt=jt: h.matmul(
                                ps[:, 0:256], lhsT=w2b[:, jt, :], rhs=AT[jt][:], start=(jt == 0),
                                stop=(jt == 1)), [w2b, AT[jt], AT[jt].sub("z")], [ps])
                        kb.op("act", lambda h, ps=ps: h.activation(out=kcf[:], in_=ps[:, 0:256], func=AF.Copy),
                              [ps], [kcf])
                        i2 = norm_block(kcf[:], 256, None, 1.0 / 128, EPS, kcf)
                        kb.op("dve", lambda h, i2=i2: h.scalar_tensor_tensor(
                            out=kcn[:], in0=kcf[:], scalar=knw[:, 0:1], in1=rr_[i2][:, 0:256],
                            op0=ALU.mult, op1=ALU.mult), [kcf, rr_[i2], knw], [kcn])
                    else:
                        for c2 in range(2):
                            ps = psf[4 + c2]
                            for jt in range(2):
                                kb.op("pe", lambda h, ps=ps, jt=jt, c2=c2: h.matmul(
                                    ps[:, 0:128], lhsT=AT[jt][:, c2 * 128:(c2 + 1) * 128], rhs=w2b[:, jt, :],
                                    start=(jt == 0), stop=(jt == 1)), [w2b, AT[jt], AT[jt].sub("z")], [ps])
                            kb.op("act", lambda h, ps=ps, c2=c2: h.activation(
                                out=vcp[:, c2, 0:128], in_=ps[:, 0:128], func=AF.Copy), [ps], [vcp])
                kb.op("pool", lambda h: h.memset(vcp[:, :, 128:129], 1.0), [], [vcp.sub("one")])
                kb.dma("sp", ovst[:], ov_in.t.rearrange("(c p) n -> p c n", p=128), [ov_in], [ovst], ovst)
                kb.op("dve", lambda h: h.tensor_copy(out=vcp[:, :, 129:193], in_=ovst[:]), [ovst], [vcp.sub("ov")])
                vcp_all = [vcp, vcp.sub("one"), vcp.sub("ov")]
                for h_ in range(4):
                    kb.dma("sp", X[:, 0:1024], stripS_in[h_], [stripS_in], [X], X)
                    self.cast(stS[:, h_, :], X[:, 0:1024], [X], [stS])
                    kb.dma("sp", X[:, 0:1408], stripW_in[h_], [stripW_in], [X], X)
                    self.cast(stW[:, h_, :], X[:, 0:1408], [X], [stW])
                kb.dma("sp", X[0:64, :], E_in[:], [E_in], [X], X)
                kb.op("dve", lambda h: h.tensor_copy(out=Eb[:], in_=X[0:64, :]), [X], [Eb])
                kb.op("pool", lambda h: h.memset(vsp[:, :, 128:129], 1.0), [], [vsp.sub("one")])
                kb.op("pool", lambda h: h.memset(vwp[:, :, 128:129], 1.0), [], [vwp.sub("one")])
                vst = [kb.sb("vst3_%d" % i, [128, 272], F32, es2) for i in range(2)]
                for n in range(NT):
                    st = vst[n % 2]
                    kb.dma("sp", st[:], MISC[n * 128:(n + 1) * 128, :], misc_all, [st], st)
                    self.cast(vsp[:, n, 0:128], st[:, 0:128], [st], [vsp])
                    self.cast(vwp[:, n, 0:128], st[:, 128:256], [st], [vwp])
                    kb.op("pool", lambda h, st=st, n=n: h.tensor_copy(out=SG[:, n, :], in_=st[:, 260:272]),
                          [st], [SG])
                kb.op("act", lambda h: h.activation(out=SG[:], in_=SG[:], func=AF.Exp, scale=-1.0), [SG], [SG])
                kb.op("dve", lambda h: h.tensor_scalar_add(out=SG[:], in0=SG[:], scalar1=1.0), [SG], [SG])
                kb.op("dve", lambda h: h.reciprocal(out=SG[:], in_=SG[:]), [SG], [SG])
            kb.barrier()
            vsp_all = [vsp, vsp.sub("one")]
            vwp_all = [vwp, vwp.sub("one")]
            bst = [kb.sb("bst%d" % i, [128, 512], F32, es) for i in range(2)]
            bcb = [kb.sb("bcb%d" % i, [128, 512], BF16, es) for i in range(2)]
            PT = [kb.sb("PT%d" % i, [128, 512], BF16, es) for i in range(4)]
            OC = kb.sb("OC", [128, 4, 4, 193], F32, es)
            OS = kb.sb("OS", [128, 4, 4, 129], F32, es)
            OW = kb.sb("OW", [128, 4, 4, 129], F32, es)
            rc4 = kb.sb("rc4", [128, 4], F32, es)
            imp = kb.sb("imp", [128, 64], F32, es)
            rep = kb.sb("rep", [128, 64], F32, es)
            m8a = kb.sb("m8a", [128, 8], F32, es)
            m8b = kb.sb("m8b", [128, 8], F32, es)
            selm = kb.sb("selm", [128, 64], F32, es)
            selb = kb.sb("selb", [128, 64], BF16, es)
            selT = kb.sb("selT", [64, 512], BF16, es)
            D12 = kb.sb("D12", [128, 4, 3], F32, es)
            CF = kb.sb("CF", [128, 12], F32, es)
            zt = [kb.sb("z3_%d" % i, [128, 512], F32, es) for i in range(2)]
            gz = kb.sb("gz", [128, 512], F32, es)
            yacc = kb.sb("yacc", [128, 128], F32, es)
            ybf = [kb.sb("ybf%d" % i, [128, 128], BF16, es) for i in range(2)]
            yst = [kb.sb("yst3_%d" % i, [128, 4, 512], BF16, es) for i in range(2)]
            sc = 0
            pc = 0
            for i in range(self.nsa_tiles):
                tcs = slice(i * 512, (i + 1) * 512)
                for h_ in range(4):
                    pts = []
                    for ct in range(2):
                        b1 = bst[sc % 2]
                        b2 = bcb[sc % 2]
                        pS = psf[sc % 2]
                        sc += 1
                        ptile = PT[pc % 4]
                        pc += 1
                        pts.append(ptile)
                        kb.dma("sp", b1[:], biasC_in[h_ * 2 + ct, :, tcs], [biasC_in], [b1], b1)
                        kb.op("pool", lambda h, b1=b1, b2=b2: h.tensor_copy(out=b2[:], in_=b1[:]), [b1], [b2])
                        kb.op("pe", lambda h, pS=pS, ct=ct, h_=h_, tcs=tcs: h.matmul(
                            pS[:], lhsT=kcn[:, ct * 128:(ct + 1) * 128], rhs=qn[h_][:, tcs],
                            start=True, stop=False), [kcn, qn[h_]], [pS])
                        kb.op("pe", lambda h, pS=pS, b2=b2: h.matmul(
                            pS[:], lhsT=ident_b[:], rhs=b2[:], start=False, stop=True), [ident_b, b2], [pS])
                        kb.op("act", lambda h, pS=pS, ptile=ptile: h.activation(
                            out=ptile[:], in_=pS[:], func=AF.Exp), [pS], [ptile])
                    for tt in range(4):
                        pO = psf[2 + tt]
                        for ct in range(2):
                            kb.op("pe", lambda h, pO=pO, ct=ct, tt=tt, p_=pts[ct]: h.matmul(
                                pO[:, 0:193], lhsT=p_[:, tt * 128:(tt + 1) * 128], rhs=vcp[:, ct, :],
                                start=(ct == 0), stop=(ct == 1)), [pts[ct]] + vcp_all, [pO])
                        self.evac(OC[:, h_, tt, :], pO[:, 0:193], [pO], [OC.sub((h_, tt))])
                pT = psb[0]
                for tt in range(4):
                    oc_r = [OC.sub((h_, tt)) for h_ in range(4)]
                    kt = slice((4 * i + tt) * 64, (4 * i + tt + 1) * 64)
                    kb.op("dve", lambda h, tt=tt: h.tensor_scalar_max(out=rc4[:], in0=OC[:, :, tt, 128],
                                                                      scalar1=1.0e-30), oc_r, [rc4])
                    kb.op("dve", lambda h: h.reciprocal(out=rc4[:], in_=rc4[:]), [rc4], [rc4])
                    for h_ in range(4):
                        if h_ == 0:
                            kb.op("dve", lambda h, tt=tt: h.tensor_scalar(
                                out=imp[:], in0=OC[:, 0, tt, 129:193], scalar1=rc4[:, 0:1], scalar2=None,
                                op0=ALU.mult), oc_r + [rc4], [imp])
                        else:
                            kb.op("dve", lambda h, tt=tt, h_=h_: h.scalar_tensor_tensor(
                                out=imp[:], in0=OC[:, h_, tt, 129:193], scalar=rc4[:, h_:h_ + 1], in1=imp[:],
                                op0=ALU.mult, op1=ALU.add), oc_r + [rc4, imp], [imp])
                    kb.op("dve", lambda h, kt=kt: h.tensor_tensor(out=imp[:], in0=imp[:], in1=keep[:, kt],
                                                                  op=ALU.mult), [imp, keep], [imp])
                    kb.op("dve", lambda h, kt=kt: h.tensor_tensor(out=imp[:], in0=imp[:], in1=addc[:, kt],
                                                                  op=ALU.add), [imp, addc], [imp])
                    kb.op("dve", lambda h: h.max(out=m8a[:], in_=imp[:]), [imp], [m8a])
                    kb.op("dve", lambda h: h.match_replace(out=rep[:], in_to_replace=m8a[:], in_values=imp[:],
                                                           imm_value=-3.0e38), [imp, m8a], [rep])
                    kb.op("dve", lambda h: h.max(out=m8b[:], in_=rep[:]), [rep], [m8b])
                    kb.op("dve", lambda h: h.tensor_scalar(out=selm[:], in0=imp[:], scalar1=m8b[:, 7:8],
                                                           scalar2=None, op0=ALU.is_ge), [imp, m8b], [selm])
                    kb.op("dve", lambda h: h.tensor_scalar(out=selb[:], in0=selm[:], scalar1=-NEG, scalar2=NEG,
                                                           op0=ALU.mult, op1=ALU.add), [selm], [selb])
                    kb.op("pe", lambda h, tt=tt, pT=pT: h.transpose(
                        out=pT[0:64, tt * 128:(tt + 1) * 128], in_=selb[:], identity=ident_b[:]),
                        [selb, ident_b], [pT])
                self.evac(selT[:], pT[0:64, 0:512], [pT], [selT])
                items = []
                for br in range(2):
                    j0 = 0 if br == 0 else max(0, 4 * i - 4)
                    for h_ in range(4):
                        for j in range(j0, 4 * i + 4):
                            items.append(dict(br=br, h=h_, j=j, j0=j0))

                def emit_qk(it, i=i, tcs=tcs):
                    nonlocal sc, pc
                    br, h_, j = it["br"], it["h"], it["j"]
                    kn = ksn if br == 0 else kwn
                    st_ = stS if br == 0 else stW
                    pS = psf[sc % 2]
                    sc += 1
                    ptile = PT[pc % 4]
                    pc += 1
                    it["pS"], it["pt"] = pS, ptile
                    delta = 512 * i - 128 * j
                    near = (br == 1) or (delta <= 128)
                    it["near"] = near
                    kb.op("pe", lambda h: h.matmul(
                        pS[:], lhsT=kn[:, j * 128:(j + 1) * 128], rhs=qn[h_][:, tcs],
                        start=True, stop=False), [kn, qn[h_]], [pS])
                    if br == 0:
                        kb.op("pe", lambda h: h.matmul(
                            pS[:], lhsT=Eb[0:64, j * 128:(j + 1) * 128], rhs=selT[0:64, :],
                            start=False, stop=(not near)), [Eb, selT], [pS])
                    if near:
                        off = delta + 384
                        kb.op("pe", lambda h: h.matmul(
                            pS[:], lhsT=ident_b[:], rhs=st_[:, h_, off:off + 512],
                            start=False, stop=True), [ident_b, st_], [pS])

                def emit_exp(it):
                    pS, ptile, h_ = it["pS"], it["pt"], it["h"]
                    if it["near"]:
                        kb.op("act", lambda h: h.activation(out=ptile[:], in_=pS[:], func=AF.Exp),
                              [pS], [ptile])
                    else:
                        kb.op("act", lambda h: h.activation(out=ptile[:], in_=pS[:], func=AF.Exp,
                                                            bias=b31[:, h_:h_ + 1]), [pS, b31], [ptile])

                def emit_pv(it, i=i):
                    br, h_, j, j0 = it["br"], it["h"], it["j"], it["j0"]
                    vp_ = vsp if br == 0 else vwp
                    vp_all = vsp_all if br == 0 else vwp_all
                    OB = OS if br == 0 else OW
                    ptile = it["pt"]
                    for tt in range(4):
                        if j > 4 * i + tt:
                            continue
                        pO = psf[2 + tt]
                        kb.op("pe", lambda h, pO=pO, tt=tt: h.matmul(
                            pO[:, 0:129], lhsT=ptile[:, tt * 128:(tt + 1) * 128], rhs=vp_[:, j, :],
                            start=(j == j0), stop=(j == 4 * i + tt)), [ptile] + vp_all, [pO])
                    if j == 4 * i + 3:
                        for tt in range(4):
                            self.evac(OB[:, h_, tt, :], psf[2 + tt][:, 0:129], [psf[2 + tt]],
                                      [OB.sub((h_, tt))])

                emit_qk(items[0])
                for k_, it in enumerate(items):
                    if k_ + 1 < len(items):
                        emit_qk(items[k_ + 1])
                    emit_exp(it)
                    emit_pv(it)
                ys = yst[i % 2]
                pY = psb[1]
                for tt in range(4):
                    n = 4 * i + tt
                    z_t = zt[n % 2]
                    kb.dma("sp", z_t[:], ZNS[n * 128:(n + 1) * 128, :], zns_all, [z_t], z_t)
                    srcs = [OC.sub((h_, tt)) for h_ in range(4)] + [OS.sub((h_, tt)) for h_ in range(4)] + \
                           [OW.sub((h_, tt)) for h_ in range(4)]
                    for bi, OB in enumerate((OC, OS, OW)):
                        kb.op("dve", lambda h, bi=bi, OB=OB, tt=tt: h.tensor_copy(
                            out=D12[:, :, bi], in_=OB[:, :, tt, 128]), srcs, [D12])
                    kb.op("dve", lambda h: h.tensor_scalar_max(out=D12[:], in0=D12[:], scalar1=1.0e-30),
                          [D12], [D12])
                    kb.op("dve", lambda h: h.reciprocal(out=D12[:], in_=D12[:]), [D12], [D12])
                    kb.op("dve", lambda h, n=n: h.tensor_tensor(
                        out=CF[:], in0=D12[:].rearrange("p a b -> p (a b)"), in1=SG[:, n, :], op=ALU.mult),
                        [D12, SG], [CF])
                    kb.op("act", lambda h, z_t=z_t: h.activation(out=gz[:], in_=z_t[:], func=AF.Silu),
                          [z_t], [gz])
                    for h_ in range(4):
                        yb_ = ybf[h_ % 2]
                        kb.op("dve", lambda h, h_=h_, tt=tt: h.tensor_scalar(
                            out=yacc[:], in0=OC[:, h_, tt, 0:128], scalar1=CF[:, 3 * h_:3 * h_ + 1], scalar2=None,
                            op0=ALU.mult), srcs + [CF], [yacc])
                        kb.op("dve", lambda h, h_=h_, tt=tt: h.scalar_tensor_tensor(
                            out=yacc[:], in0=OS[:, h_, tt, 0:128], scalar=CF[:, 3 * h_ + 1:3 * h_ + 2], in1=yacc[:],
                            op0=ALU.mult, op1=ALU.add), srcs + [CF, yacc], [yacc])
                        kb.op("dve", lambda h, h_=h_, tt=tt: h.scalar_tensor_tensor(
                            out=yacc[:], in0=OW[:, h_, tt, 0:128], scalar=CF[:, 3 * h_ + 2:3 * h_ + 3], in1=yacc[:],
                            op0=ALU.mult, op1=ALU.add), srcs + [CF, yacc], [yacc])
                        kb.op("dve", lambda h, h_=h_, yb_=yb_: h.tensor_tensor(
                            out=yb_[:], in0=yacc[:], in1=gz[:, h_ * 128:(h_ + 1) * 128], op=ALU.mult),
                            [yacc, gz], [yb_])
                        kb.op("pe", lambda h, h_=h_, yb_=yb_, pY=pY: h.transpose(
                            out=pY[:, h_ * 128:(h_ + 1) * 128], in_=yb_[:], identity=ident_b[:]),
                            [yb_, ident_b], [pY])
                    self.evac(ys[:, :, tt * 128:(tt + 1) * 128],
                              pY[:, 0:512].rearrange("p (j t) -> p j t", j=4), [pY], [ys])
                kb.dma("pool", YT[i, 512:1024, :].rearrange("(j p) t -> p j t", p=128), ys[:],
                       [ys], [YT.sub(("ns", i))], ys)


def make_core_inputs(inputs, c):
    g = c % 4
    b = c // 4
    w_in = inputs["w_in"][0]
    cols = []
    cols += list(range(256 * g, 256 * g + 256))
    cols += list(range(1024 + 256 * g, 1024 + 256 * g + 256))
    cols += list(range(8208 + 512 * g, 8208 + 512 * g + 512))
    cols += list(range(10256 + 128 * g, 10256 + 128 * g + 128))
    cols += list(range(10768 + 128 * g, 10768 + 128 * g + 128))
    cols += list(range(11280 + 128 * g, 11280 + 128 * g + 128))
    cols += list(range(12304 + 128 * g, 12304 + 128 * g + 128))
    cols += list(range(2048 + 512 * g, 2048 + 512 * g + 512))
    cols += list(range(4096 + 512 * g, 4096 + 512 * g + 512))
    cols += list(range(6144 + 512 * g, 6144 + 512 * g + 512))
    cols += list(range(11792 + 128 * g, 11792 + 128 * g + 128))
    cols += list(range(12816 + 128 * g, 12816 + 128 * g + 128))
    cols += [8192 + 2 * g, 8192 + 2 * g + 1, 8200 + 2 * g, 8200 + 2 * g + 1]
    cols += list(range(13328 + 12 * g, 13328 + 12 * g + 12))
    cols += list(range(13376 + 512 * g, 13376 + 512 * g + 512))
    assert len(cols) == WCOLS
    m = {
        "x": np.ascontiguousarray(inputs["x"][b]),
        "wcore": np.ascontiguousarray(w_in[:, cols]),
        "normw": np.ascontiguousarray(np.broadcast_to(inputs["norm_w"][0][None, :], (128, D))),
        "ident": np.eye(128, dtype=np.float32),
    }
    cw = inputs["ml_conv_w"][0]
    convw = np.zeros((128, 16), np.float32)
    for ci in range(4):
        base = (0 if ci < 2 else 1024) + 256 * g + 128 * (ci % 2)
        convw[:, ci * 4:(ci + 1) * 4] = cw[:, base:base + 128].T
    m["convw"] = convw
    gbv = np.array([inputs["ml_i_bias"][0][2 * g], inputs["ml_i_bias"][0][2 * g + 1],
                    inputs["ml_f_bias"][0][2 * g], inputs["ml_f_bias"][0][2 * g + 1]], np.float32)
    m["gbias"] = np.ascontiguousarray(np.broadcast_to(gbv[None, :], (128, 4)))
    hn = inputs["ml_head_norm_w"][0][2 * g:2 * g + 2].reshape(1, 512)
    m["hnw"] = np.ascontiguousarray(np.broadcast_to(hn, (128, 512)))
    m["tri"] = np.triu(np.ones((128, 128), np.float32))
    r = g
    qs = slice(1024 * r, 1024 * (r + 1))
    rows = []
    for gg in range(4):
        rows += list(range(512 * gg, 512 * gg + 512))
        rows += list(range(2048 + 512 * gg, 2048 + 512 * gg + 512))
    m["wq"] = np.ascontiguousarray(inputs["w_out"][0][rows][:, qs])
    m["gq"] = np.ascontiguousarray(inputs["ple_gate"][0][:, qs])
    m["pq"] = np.ascontiguousarray(inputs["ple_proj"][0][:, qs])
    m["pT"] = np.ascontiguousarray(inputs["p"][0, b].T)
    m["xq"] = np.ascontiguousarray(inputs["x"][b][:, qs])
    rb = inputs["rel_bias"][:, 4 * g:4 * g + 4]
    m["qnw"] = np.ascontiguousarray(inputs["nsa_q_norm_w"][0].reshape(128, 1))
    m["knw"] = np.ascontiguousarray(inputs["nsa_k_norm_w"][0].T)
    m["peK"] = np.ascontiguousarray(inputs["cmp_pe_k"][0].T)
    m["peV"] = np.ascontiguousarray(inputs["cmp_pe_v"][0].T)
    m["w1k"] = inputs["cmp_k_w1"][0]
    m["w2k"] = inputs["cmp_k_w2"][0]
    m["w1v"] = inputs["cmp_v_w1"][0]
    m["w2v"] = inputs["cmp_v_w2"][0]
    tab = _nsa_tables()
    negf = np.float32(NEG)
    bc = np.where(tab["c_valid"][None], rb.T[:, tab["c_bucket"]], negf).astype(np.float32)
    m["biasC"] = np.ascontiguousarray(bc.reshape(8, 128, S))
    m["stripS"] = np.ascontiguousarray(
        np.where(tab["s_valid"][None], rb.T[:, tab["s_bucket"]], negf).astype(np.float32))
    m["stripW"] = np.ascontiguousarray(
        np.where(tab["w_valid"][None], rb.T[:, tab["w_bucket"]], negf).astype(np.float32))
    m["b31"] = np.ascontiguousarray(np.broadcast_to(rb[31][None, :], (128, 4)))
    m["ov"] = tab["ov"]
    m["Esel"] = tab["E"]
    m["keep"] = tab["keep"]
    m["addc"] = tab["add"]
    return m


_TAB = {}


def _bucket(dist):
    import math
    n = np.maximum(dist, 0)
    nf = np.maximum(n, 1).astype(np.float32)
    large = 16 + (np.log(nf / np.float32(16)) / np.float32(math.log(128 / 16))
                  * np.float32(16)).astype(np.int32)
    large = np.minimum(large, 31)
    return np.where(n < 16, n, large)


def _nsa_tables():
    if _TAB:
        return _TAB
    t = np.arange(S)
    c = np.arange(256)
    dist = t[None, :] - (16 * c[:, None] + 31)
    _TAB["c_valid"] = (dist >= 0) & (c[:, None] <= 254)
    _TAB["c_bucket"] = _bucket(dist)
    sl = np.arange(128)[:, None]
    u = np.arange(1024)[None, :]
    d = u - 384 - sl
    _TAB["s_valid"] = d >= 0
    _TAB["s_bucket"] = _bucket(d)
    u = np.arange(1408)[None, :]
    d = u - 384 - sl
    _TAB["w_valid"] = (d >= 0) & (d < 512)
    _TAB["w_bucket"] = _bucket(d)
    n = np.arange(64)
    ov = ((16 * c[:, None] < 64 * n[None, :] + 64) & (16 * c[:, None] + 32 > 64 * n[None, :])
          & (c[:, None] <= 254)).astype(np.float32)
    _TAB["ov"] = np.ascontiguousarray(ov)
    _TAB["E"] = np.ascontiguousarray((t[None, :] // 64 == n[:, None]).astype(np.float32))
    tt = np.arange(NT)[None, :, None]
    p = np.arange(128)[:, None, None]
    cur = (128 * tt + p) // 64
    nn = n[None, None, :]
    forced = (nn == 0) | (nn == cur) | (nn == cur - 1)
    future = nn > cur
    keep = (~(forced | future)).astype(np.float32)
    add = np.where(forced, 1.0e4 + nn, np.where(future, -1.0e30, 0.0)).astype(np.float32)
    _TAB["keep"] = np.ascontiguousarray(keep.reshape(128, NT * 64))
    _TAB["add"] = np.ascontiguousarray(add.reshape(128, NT * 64))
    return _TAB


_CACHE = {}


def kernel(**inputs):
    inputs = {k: np.asarray(v) for k, v in inputs.items()}
    if "nc" not in _CACHE:
        _CACHE["nc"] = Prog(debug=False, have_nsa=True).build()
    nc = _CACHE["nc"]
    in_maps = [make_core_inputs(inputs, c) for c in range(8)]
    res = run_bass_kernel_spmd(nc, in_maps, core_ids=list(range(8)))
    out = np.zeros((2, S, D), np.float32)
    for c in range(8):
        out[c // 4, :, 1024 * (c % 4):1024 * (c % 4 + 1)] = res.results[c]["out"]
    return out
```
